# Optimizing a Trainium2 kernel written in Bass

```python
import jax, jax.numpy as jnp
from jax import lax
import numpy as np

D_MODEL = 1024
BATCH = 8
SEQ = 2048
DEPTH = 1
DEC_BATCH = 32
DEC_SEQ = 4
PAST_LEN = 16384
PAGE_SIZE = 128

N_HEADS = 8
HEAD_DIM = 64
N_KV = 2
HPG = N_HEADS // N_KV
NSA_W = N_HEADS * HEAD_DIM
KV_W = N_KV * HEAD_DIM
CMP_LEN = 32
CMP_STRIDE = 16
CMP_R = CMP_LEN // CMP_STRIDE
CMP_HID = HEAD_DIM
SLC_LEN = 64
TOP_N = 16
WINDOW = 512
CONV_C = D_MODEL // 2
CONV_W = 31
Q_BLOCK = 64
SPLITS = (NSA_W, 6 * KV_W, 3 * N_HEADS, NSA_W, 2 * CONV_C, CONV_C, 2 * D_MODEL)
D_IN = NSA_W + 6 * KV_W + 3 * N_HEADS + NSA_W + 2 * CONV_C + CONV_C + 2 * D_MODEL
EPS = 1e-6
FORCED_SCORE = 1e4
NEG_INF = -1e30

kernel_name = 'nsa_conformer_gated_hybrid_step'


def split_cols(z, widths):
    outs, start = [], 0
    for w in widths:
        outs.append(z[..., start:start + w])
        start += w
    return outs


def rmsnorm(x, g):
    xf = x.astype(jnp.float32)
    y = xf * lax.rsqrt(jnp.mean(xf * xf, axis=-1, keepdims=True) + EPS)
    return (y * g.astype(jnp.float32)).astype(x.dtype)


def layernorm(x, g, b):
    xf = x.astype(jnp.float32)
    mu = jnp.mean(xf, axis=-1, keepdims=True)
    var = jnp.mean(jnp.square(xf - mu), axis=-1, keepdims=True)
    y = (xf - mu) * lax.rsqrt(var + EPS) * g.astype(jnp.float32) + b.astype(jnp.float32)
    return y.astype(x.dtype)


def masked_softmax(s, mask):
    s = jnp.where(mask, s.astype(jnp.float32), NEG_INF)
    e = jnp.where(mask, jnp.exp(s - jnp.max(s, axis=-1, keepdims=True)), 0.0)
    return e / jnp.maximum(jnp.sum(e, axis=-1, keepdims=True), 1e-30)


def compress(k, pe, w1, w2):
    B, T = k.shape[:2]
    n_chunk = T // CMP_STRIDE
    n_cmp = n_chunk - CMP_R + 1
    kc = k[:, :n_chunk * CMP_STRIDE].reshape(B, n_chunk, CMP_STRIDE, N_KV, HEAD_DIM)
    w1r = w1.reshape(CMP_R, CMP_STRIDE, HEAD_DIM, CMP_HID)
    per = jnp.einsum('bcjgd,rjdh->rbcgh', kc, w1r)
    h = jnp.einsum('rjd,rjdh->h', pe.reshape(CMP_R, CMP_STRIDE, HEAD_DIM), w1r)
    for r in range(CMP_R):
        h = h + per[r, :, r:r + n_cmp]
    return jax.nn.silu(h) @ w2


def to_blocks(k):
    B, T = k.shape[:2]
    n_slc = -(-T // SLC_LEN)
    k = jnp.pad(k, ((0, 0), (0, n_slc * SLC_LEN - T), (0, 0), (0, 0)))
    return k.reshape(B, n_slc, SLC_LEN, N_KV, HEAD_DIM).transpose(0, 3, 1, 2, 4)


def cmp_to_slc(n_cmp, n_slc):
    i = jnp.arange(n_cmp)[:, None] * CMP_STRIDE
    j = jnp.arange(n_slc)[None, :] * SLC_LEN
    return ((i < j + SLC_LEN) & (i + CMP_LEN > j)).astype(jnp.float32)


def nsa_context(k_c, v_c, k_s, v_s, pe_k, w1_k, w2_k, pe_v, w1_v, w2_v):
    kc = compress(k_c, pe_k, w1_k, w2_k)
    vc = compress(v_c, pe_v, w1_v, w2_v)
    n_cmp = kc.shape[1]
    c_end = jnp.arange(n_cmp) * CMP_STRIDE + (CMP_LEN - 1)
    ks_bg, vs_bg = to_blocks(k_s), to_blocks(v_s)
    agg = cmp_to_slc(n_cmp, ks_bg.shape[2])
    return kc, vc, c_end, agg, ks_bg, vs_bg


def nsa_block(q, gates, q_pos, ctx, kw, vw, w_pos):
    kc, vc, c_end, agg, ks_bg, vs_bg = ctx
    B, Q = q.shape[:2]
    qf = q * (HEAD_DIM ** -0.5)
    s_c = jnp.einsum('bqghd,bcgd->bqghc', qf, kc)
    m_c = (c_end[None, :] <= q_pos[:, None])[None, :, None, None, :]
    p_c = masked_softmax(s_c, m_c)
    o_c = jnp.einsum('bqghc,bcgd->bqghd', p_c.astype(vc.dtype), vc)
    n_slc = ks_bg.shape[2]
    imp = jnp.einsum('bqgc,cj->bqgj', jnp.sum(p_c, axis=3), agg)
    j = jnp.arange(n_slc)[None, :]
    cur = (q_pos // SLC_LEN)[:, None]
    valid = j * SLC_LEN <= q_pos[:, None]
    forced = (j == 0) | (j == cur) | (j == cur - 1)
    score = jnp.where(forced[None, :, None, :], FORCED_SCORE, imp)
    score = jnp.where(valid[None, :, None, :], score, -1.0)
    n_top = min(TOP_N, n_slc)
    _, idx = lax.top_k(score, n_top)
    idx = idx.transpose(0, 2, 1, 3)
    b_i = jnp.arange(B)[:, None, None, None]
    g_i = jnp.arange(N_KV)[None, :, None, None]
    n_sel = n_top * SLC_LEN
    k_sel = ks_bg[b_i, g_i, idx].reshape(B, N_KV, Q, n_sel, HEAD_DIM)
    v_sel = vs_bg[b_i, g_i, idx].reshape(B, N_KV, Q, n_sel, HEAD_DIM)
    key_pos = (idx[..., None] * SLC_LEN + jnp.arange(SLC_LEN)).reshape(B, N_KV, Q, n_sel)
    m_s = (key_pos <= q_pos[None, None, :, None]).transpose(0, 2, 1, 3)[:, :, :, None, :]
    s_s = jnp.einsum('bqghd,bgqnd->bqghn', qf, k_sel)
    p_s = masked_softmax(s_s, m_s)
    o_s = jnp.einsum('bqghn,bgqnd->bqghd', p_s.astype(v_sel.dtype), v_sel)
    wp, qp = w_pos[None, :], q_pos[:, None]
    m_w = ((wp <= qp) & (wp >= qp - WINDOW) & (wp >= 0))[None, :, None, None, :]
    s_w = jnp.einsum('bqghd,bwgd->bqghw', qf, kw)
    p_w = masked_softmax(s_w, m_w)
    o_w = jnp.einsum('bqghw,bwgd->bqghd', p_w.astype(vw.dtype), vw)
    g = jax.nn.sigmoid(gates.astype(jnp.float32)).astype(q.dtype)
    return g[..., 0:1] * o_c + g[..., 1:2] * o_s + g[..., 2:3] * o_w


def attend_prompt(q, gates, k_c, v_c, k_s, v_s, k_w, v_w, cmp_params):
    B, T = q.shape[:2]
    ctx = nsa_context(k_c, v_c, k_s, v_s, *cmp_params)
    pad = ((0, 0), (WINDOW, 0), (0, 0), (0, 0))
    kw_pad, vw_pad = jnp.pad(k_w, pad), jnp.pad(v_w, pad)
    n_qb = T // Q_BLOCK

    def blocks(t):
        return t.reshape((B, n_qb, Q_BLOCK) + t.shape[2:]).swapaxes(0, 1)

    def step(args):
        qb, gb, s = args
        q_pos = s + jnp.arange(Q_BLOCK)
        kw = lax.dynamic_slice_in_dim(kw_pad, s, WINDOW + Q_BLOCK, axis=1)
        vw = lax.dynamic_slice_in_dim(vw_pad, s, WINDOW + Q_BLOCK, axis=1)
        w_pos = s - WINDOW + jnp.arange(WINDOW + Q_BLOCK)
        return nsa_block(qb, gb, q_pos, ctx, kw, vw, w_pos)

    out = lax.map(step, (blocks(q), blocks(gates), jnp.arange(n_qb) * Q_BLOCK))
    return out.swapaxes(0, 1).reshape(B, T, NSA_W)


def gather_pages(pool, page_table):
    g = pool[page_table]
    return g.reshape(page_table.shape[0], -1, N_KV, HEAD_DIM)


def make_attend_sample(pool_k_c, pool_v_c, pool_k_s, pool_v_s, buf_k, buf_v, page_table):
    def attend(q, gates, k_c, v_c, k_s, v_s, k_w, v_w, cmp_params):
        def full(pool, new):
            return jnp.concatenate([gather_pages(pool, page_table), new], axis=1)
        ctx = nsa_context(full(pool_k_c, k_c), full(pool_v_c, v_c),
                          full(pool_k_s, k_s), full(pool_v_s, v_s), *cmp_params)
        n_new = q.shape[1]
        n_buf = buf_k.shape[1]
        q_pos = PAST_LEN + jnp.arange(n_new)
        kw = jnp.concatenate([buf_k, k_w], axis=1)
        vw = jnp.concatenate([buf_v, v_w], axis=1)
        w_pos = PAST_LEN - n_buf + jnp.arange(n_buf + n_new)
        out = nsa_block(q, gates, q_pos, ctx, kw, vw, w_pos)
        return out.reshape(q.shape[0], n_new, NSA_W)
    return attend


def mixer_layer(x, conv_buf, attend, ln_g, w_in, pe_k, w1_k, w2_k, pe_v, w1_v, w2_v,
                dw_k, dw_b, cln_g, cln_b, pw_w, pw_b, w_pa, w_pb, w_o):
    B, T, _ = x.shape
    xn = rmsnorm(x, ln_g)
    q, kv, g_nsa, g_a, glu, g_b, g_mrg = split_cols(xn @ w_in, SPLITS)
    k_c, v_c, k_s, v_s, k_w, v_w = [t.reshape(B, T, N_KV, HEAD_DIM) for t in split_cols(kv, (KV_W,) * 6)]
    q = q.reshape(B, T, N_KV, HPG, HEAD_DIM)
    g_nsa = g_nsa.reshape(B, T, N_KV, HPG, 3)
    o_a = attend(q, g_nsa, k_c, v_c, k_s, v_s, k_w, v_w, (pe_k, w1_k, w2_k, pe_v, w1_v, w2_v))
    u_val, u_gate = split_cols(glu, (CONV_C, CONV_C))
    u = u_val * jax.nn.sigmoid(u_gate)
    up = jnp.concatenate([conv_buf, u], axis=1)
    c = lax.conv_general_dilated(up, dw_k[:, None, :], (1,), 'VALID',
                                 dimension_numbers=('NWC', 'WIO', 'NWC'),
                                 feature_group_count=CONV_C) + dw_b
    c = jax.nn.silu(layernorm(c, cln_g, cln_b)) @ pw_w + pw_b
    br_a = (o_a * jax.nn.silu(g_a)) @ w_pa
    br_b = (c * jax.nn.silu(g_b)) @ w_pb
    m_a, m_b = split_cols(g_mrg, (D_MODEL, D_MODEL))
    h = jax.nn.sigmoid(m_a) * br_a + jax.nn.sigmoid(m_b) * br_b
    return x + h @ w_o, (k_c, v_c, k_s, v_s, k_w, v_w, up[:, -(CONV_W - 1):])


def setup_inputs(seed: int = 0) -> dict:
    key = jax.random.key(seed)
    ks = jax.random.split(key, 40)
    n_pages = PAST_LEN // PAGE_SIZE
    n_pool = (DEC_BATCH * n_pages * 5) // 4
    win_buf = min(WINDOW, PAST_LEN)

    def nrm(k, shape, scale):
        return jax.random.normal(k, shape, jnp.float32) * scale

    pool_shape = (DEPTH, n_pool, PAGE_SIZE, N_KV, HEAD_DIM)
    win_shape = (DEPTH, DEC_BATCH, win_buf, N_KV, HEAD_DIM)
    page_table = jax.random.permutation(ks[10], n_pool)[: DEC_BATCH * n_pages]
    page_table = page_table.reshape(DEC_BATCH, n_pages).astype(jnp.int32)
    return {
        'x_prompt': nrm(ks[0], (BATCH, SEQ, D_MODEL), 1.0),
        'x_sample': nrm(ks[1], (DEC_BATCH, DEC_SEQ, D_MODEL), 1.0),
        'cache_k_cmp': nrm(ks[2], pool_shape, 1.0),
        'cache_v_cmp': nrm(ks[3], pool_shape, 1.0),
        'cache_k_slc': nrm(ks[4], pool_shape, 1.0),
        'cache_v_slc': nrm(ks[5], pool_shape, 1.0),
        'cache_k_win': nrm(ks[6], win_shape, 1.0),
        'cache_v_win': nrm(ks[7], win_shape, 1.0),
        'state_conv': nrm(ks[8], (DEPTH, DEC_BATCH, CONV_W - 1, CONV_C), 0.5),
        'page_table': page_table,
        'ln_g': 1.0 + nrm(ks[11], (DEPTH, D_MODEL), 0.02),
        'w_in': nrm(ks[12], (DEPTH, D_MODEL, D_IN), D_MODEL ** -0.5),
        'pe_k': nrm(ks[13], (DEPTH, CMP_LEN, HEAD_DIM), 0.1),
        'w1_k': nrm(ks[14], (DEPTH, CMP_LEN * HEAD_DIM, CMP_HID), (CMP_LEN * HEAD_DIM) ** -0.5),
        'w2_k': nrm(ks[15], (DEPTH, CMP_HID, HEAD_DIM), CMP_HID ** -0.5),
        'pe_v': nrm(ks[16], (DEPTH, CMP_LEN, HEAD_DIM), 0.1),
        'w1_v': nrm(ks[17], (DEPTH, CMP_LEN * HEAD_DIM, CMP_HID), (CMP_LEN * HEAD_DIM) ** -0.5),
        'w2_v': nrm(ks[18], (DEPTH, CMP_HID, HEAD_DIM), CMP_HID ** -0.5),
        'dw_k': nrm(ks[19], (DEPTH, CONV_W, CONV_C), CONV_W ** -0.5),
        'dw_b': nrm(ks[20], (DEPTH, CONV_C), 0.02),
        'cln_g': 1.0 + nrm(ks[21], (DEPTH, CONV_C), 0.02),
        'cln_b': nrm(ks[22], (DEPTH, CONV_C), 0.02),
        'pw_w': nrm(ks[23], (DEPTH, CONV_C, CONV_C), CONV_C ** -0.5),
        'pw_b': nrm(ks[24], (DEPTH, CONV_C), 0.02),
        'w_pa': nrm(ks[25], (DEPTH, NSA_W, D_MODEL), NSA_W ** -0.5),
        'w_pb': nrm(ks[26], (DEPTH, CONV_C, D_MODEL), CONV_C ** -0.5),
        'w_o': nrm(ks[27], (DEPTH, D_MODEL, D_MODEL), D_MODEL ** -0.5),
        'final_g': 1.0 + nrm(ks[28], (D_MODEL,), 0.02),
    }


def reference(x_prompt, x_sample, cache_k_cmp, cache_v_cmp, cache_k_slc, cache_v_slc,
              cache_k_win, cache_v_win, state_conv, page_table,
              ln_g, w_in, pe_k, w1_k, w2_k, pe_v, w1_v, w2_v,
              dw_k, dw_b, cln_g, cln_b, pw_w, pw_b, w_pa, w_pb, w_o, final_g):
    xp, xs = x_prompt, x_sample
    win_p = min(WINDOW, xp.shape[1])
    win_s = cache_k_win.shape[2]
    pst = [[] for _ in range(7)]
    sst = [[] for _ in range(7)]
    for l in range(DEPTH):
        params = (ln_g[l], w_in[l], pe_k[l], w1_k[l], w2_k[l], pe_v[l], w1_v[l], w2_v[l],
                  dw_k[l], dw_b[l], cln_g[l], cln_b[l], pw_w[l], pw_b[l], w_pa[l], w_pb[l], w_o[l])
        zero_buf = jnp.zeros((xp.shape[0], CONV_W - 1, CONV_C), xp.dtype)
        xp, sp = mixer_layer(xp, zero_buf, attend_prompt, *params)
        attend_s = make_attend_sample(cache_k_cmp[l], cache_v_cmp[l], cache_k_slc[l], cache_v_slc[l],
                                      cache_k_win[l], cache_v_win[l], page_table)
        xs, ss = mixer_layer(xs, state_conv[l], attend_s, *params)
        p_rows = (sp[0], sp[1], sp[2], sp[3], sp[4][:, -win_p:], sp[5][:, -win_p:], sp[6])
        s_rows = (ss[0], ss[1], ss[2], ss[3],
                  jnp.concatenate([cache_k_win[l], ss[4]], axis=1)[:, -win_s:],
                  jnp.concatenate([cache_v_win[l], ss[5]], axis=1)[:, -win_s:], ss[6])
        for i in range(7):
            pst[i].append(p_rows[i])
            sst[i].append(s_rows[i])
    y_prompt = rmsnorm(xp, final_g)
    y_sample = rmsnorm(xs, final_g)
    p_k_cmp, p_v_cmp, p_k_slc, p_v_slc, p_k_win, p_v_win, p_conv = [jnp.stack(t, 0) for t in pst]
    s_k_cmp, s_v_cmp, s_k_slc, s_v_slc, s_k_win, s_v_win, s_conv = [jnp.stack(t, 0) for t in sst]
    return (y_prompt, y_sample, p_k_cmp, p_v_cmp, p_k_slc, p_v_slc, p_k_win, p_v_win, p_conv,
            s_k_cmp, s_v_cmp, s_k_slc, s_v_slc, s_k_win, s_v_win, s_conv)
```

```python
import contextlib
import numpy as np
import concourse.bass as bass
import concourse.mybir as mybir
from concourse.bass_utils import run_bass_kernel_spmd

F32 = mybir.dt.float32
BF16 = mybir.dt.bfloat16
I32 = mybir.dt.int32
AF = mybir.ActivationFunctionType
ALU = mybir.AluOpType
AX = mybir.AxisListType

ENGS = ("pe", "act", "dve", "pool", "sp")
SEM_LIMIT = 30000
N_DMA_SEMS = 12


class Prog:
    def __init__(self, nc):
        self.nc = nc
        self.stack = contextlib.ExitStack()
        self.ops = {e: [] for e in ENGS}
        self.sems = {}
        self.res = {}
        self.seen = {e: {} for e in ENGS}
        self.cur = {}
        self.cnt = {}
        self.epoch = {e: 0 for e in ENGS}
        for e in ENGS:
            self._new_eng_sem(e)
        self.dma_pool = {}
        self.dma_rr = {}
        self.all_dma = []
        self.out_waits = []

    def _sem(self, name):
        h = self.stack.enter_context(self.nc.semaphore(name))
        self.sems[name] = h
        return name

    def _new_eng_sem(self, e):
        key = self._sem("s_%s_%d" % (e, self.epoch[e]))
        self.epoch[e] += 1
        self.cur[e] = key
        self.cnt[key] = 0

    def sbuf(self, name, shape, dt):
        return self.stack.enter_context(self.nc.sbuf_tensor(name, list(shape), dt))

    def psum(self, name, shape, dt):
        return self.stack.enter_context(self.nc.psum_tensor(name, list(shape), dt))

    def arena_init(self, nbytes):
        self.AR = self.sbuf("AR", [128, nbytes // 2], BF16)
        self.ar_off = 0
        self.ar_size = nbytes
        self.ar_peak = 0

    def ar(self, name, shape, dt):
        esz = 2 if dt == BF16 else 4
        n = esz
        for d in shape[1:]:
            n *= d
        n_al = (n + 63) // 64 * 64
        assert self.ar_off + n_al <= self.ar_size, ("arena overflow", name, self.ar_off, n_al, self.ar_size)
        v = self.AR[0:shape[0], self.ar_off // 2:(self.ar_off + n) // 2]
        self.ar_off += n_al
        self.ar_peak = max(self.ar_peak, self.ar_off)
        if esz == 4:
            v = v.bitcast(dt)
        if len(shape) == 3:
            v = v.rearrange("p (a b) -> p a b", a=shape[1])
        elif len(shape) == 4:
            v = v.rearrange("p (a b c) -> p a b c", a=shape[1], b=shape[2])
        return v

    def barrier(self):
        self.bar_snap = {k: v for k, v in self.cnt.items() if v > 0}
        self.bar_pending = set(ENGS)

    def arena_reset(self, keep=0):
        self.barrier()
        self.ar_off = keep

    def _need(self, eng, dep, waits):
        if dep is None:
            return
        key, val = dep
        if self.seen[eng].get(key, 0) >= val:
            return
        self.seen[eng][key] = val
        waits.append((key, val))

    def _deps(self, eng, reads, writes, is_dma):
        waits = []
        own = "s_%s_" % eng
        if getattr(self, "bar_pending", None) and eng in self.bar_pending:
            self.bar_pending.discard(eng)
            for k, v in self.bar_snap.items():
                if k.startswith(own):
                    continue
                self._need(eng, (k, v), waits)
        for r in reads:
            st = self.res.get(r)
            if st:
                self._need(eng, st[0], waits)
                if r.startswith("ps"):
                    for rd in st[1]:
                        if not rd[0].startswith(own):
                            self._need(eng, rd, waits)
        for w in writes:
            st = self.res.get(w)
            if st:
                if not (st[0] is not None and st[0][0].startswith(own) and not is_dma):
                    self._need(eng, st[0], waits)
                for rd in st[1]:
                    if rd[0].startswith(own) and not is_dma:
                        continue
                    self._need(eng, rd, waits)
        return waits

    def _mark(self, reads, writes, done):
        for r in reads:
            st = self.res.setdefault(r, [None, []])
            st[1].append(done)
            if len(st[1]) > 64:
                best = {}
                for k, v in st[1]:
                    best[k] = max(best.get(k, 0), v)
                st[1] = list(best.items())
        for w in writes:
            self.res[w] = [done, []]

    @staticmethod
    def _snap(fn):
        cl = fn.__closure__ or ()
        out = []
        for c in cl:
            try:
                out.append(id(c.cell_contents))
            except ValueError:
                out.append(None)
        return out

    def op(self, eng, fn, reads=(), writes=()):
        fn._snap = self._snap(fn)
        waits = self._deps(eng, reads, writes, False)
        key = self.cur[eng]
        self.cnt[key] += 1
        done = (key, self.cnt[key])
        self.ops[eng].append((waits, fn, key, 1))
        self._mark(reads, writes, done)
        if self.cnt[key] >= SEM_LIMIT:
            self._new_eng_sem(eng)
        return done

    def dma(self, eng, fn, reads=(), writes=(), is_output=False):
        fn._snap = self._snap(fn)
        waits = self._deps(eng, reads, writes, True)
        pool = self.dma_pool.setdefault(eng, [])
        if len(pool) < N_DMA_SEMS:
            key = self._sem("d_%s_%d" % (eng, len(pool)))
            pool.append(key)
            self.all_dma.append(key)
            self.cnt[key] = 0
            self.dma_rr[eng] = len(pool) % N_DMA_SEMS
        else:
            i = self.dma_rr[eng]
            key = pool[i]
            self.dma_rr[eng] = (i + 1) % N_DMA_SEMS
            if self.cnt[key] >= SEM_LIMIT:
                key = self._sem("d_%s_%d_%d" % (eng, i, len(self.sems)))
                pool[i] = key
                self.all_dma.append(key)
                self.cnt[key] = 0
        if self.cnt[key] > 0:
            self._need(eng, (key, self.cnt[key]), waits)
        self.cnt[key] += 16
        done = (key, self.cnt[key])
        self.ops[eng].append((waits, fn, key, 16))
        self._mark(reads, writes, done)
        if is_output:
            self.out_waits.append(done)
        return done

    def finish(self):
        fin = []
        best = {}
        for k, v in self.out_waits:
            best[k] = max(best.get(k, 0), v)
        for e in ENGS:
            for k in [kk for kk in self.cnt if kk.startswith("s_%s_" % e)]:
                if self.cnt[k] > 0:
                    best[k] = max(best.get(k, 0), self.cnt[k])
        for k in self.all_dma:
            if self.cnt[k] > 0:
                best[k] = max(best.get(k, 0), self.cnt[k])
        fin = list(best.items())
        nc = self.nc
        sems = self.sems
        ops = self.ops

        def emit(engobj, lst, final):
            for waits, fn, key, inc in lst:
                for (k, v) in waits:
                    engobj.wait_ge(sems[k], v)
                if fn._snap != self._snap(fn):
                    raise RuntimeError("late-bound closure variable changed: %s %s" % (fn.__code__.co_freevars, fn.__code__.co_firstlineno))
                inst = fn(engobj)
                inst.then_inc(sems[key], inc)
            if final:
                for (k, v) in fin:
                    engobj.wait_ge(sems[k], v)

        with nc.Block() as block:
            @block.sync
            def _(e):
                emit(e, ops["sp"], True)

            @block.tensor
            def _(e):
                emit(e, ops["pe"], False)

            @block.scalar
            def _(e):
                emit(e, ops["act"], False)

            @block.vector
            def _(e):
                emit(e, ops["dve"], False)

            @block.gpsimd
            def _(e):
                emit(e, ops["pool"], False)
        self.stack.close()


D = 1024
T = 2048
NT = T // 128
NB = T // 512
NS = 16
TA = T + NS
NBA = NB + 1
DIN = 5400
C_Q, C_KV, C_GN, C_GA, C_GLU, C_GB, C_MA, C_MB = 0, 512, 1280, 1304, 1816, 2840, 3352, 4376
BIG = 32768.0
EPS = 1e-6


class Banks:
    def __init__(self, ids):
        self.ids = list(ids)
        self.i = 0

    def next(self):
        b = self.ids[self.i % len(self.ids)]
        self.i += 1
        return b


def build_program(do_sample=True, stop_after=99):
    nc = bass.Bass("TRN2", target_bir_lowering=False)
    p = Prog(nc)

    in_names = []

    def din(name, shape, dt=F32):
        in_names.append(name)
        return nc.dram_tensor(name, list(shape), dt, kind="ExternalInput").ap()

    def dout(name, shape, dt=F32):
        return nc.dram_tensor(name, list(shape), dt, kind="ExternalOutput").ap()

    x_p = din("x_p", [T, D])
    w_in = din("w_in", [D, DIN])
    ln_g = din("ln_g", [1, D])
    final_g = din("final_g", [1, D])
    w1_k = din("w1_k", [2048, 64])
    w1_v = din("w1_v", [2048, 64])
    w2_k = din("w2_k", [64, 64])
    w2_v = din("w2_v", [64, 64])
    pe_k = din("pe_k", [32, 64])
    pe_v = din("pe_v", [32, 64])
    vecs = din("vecs", [35, 512])
    pw_w = din("pw_w", [512, 512])
    w_pa = din("w_pa", [512, D])
    w_pb = din("w_pb", [512, D])
    w_o = din("w_o", [D, D])

    x_s = din("x_s", [NS, D])
    ckw = din("ckw", [4, 512, 128])
    cvw = din("cvw", [4, 512, 128])
    sconv = din("sconv", [4, 30, 512])
    y_s = dout("y_s", [NS, D])
    s_kv = [dout("s_kv%d" % i, [NS, 128]) for i in range(4)]
    s_kw = dout("s_kw", [4, 512, 128])
    s_vw = dout("s_vw", [4, 512, 128])
    s_conv = dout("s_conv", [4, 30, 512])
    y_p = dout("y_p", [T, D])
    o_kv = [dout("o_kv%d" % i, [T, 128]) for i in range(4)]
    o_kw = dout("o_kw", [512, 128])
    o_vw = dout("o_vw", [512, 128])
    o_conv = dout("o_conv", [30, 512])

    p.arena_init(118 * 1024)

    PS = [p.psum("psb%d" % i, [128, 512], F32) for i in range(8)]

    def psr(i):
        return "ps%d" % i

    gen = Banks(range(8))

    idf = p.sbuf("idf", [128, 128], F32)
    idb = p.sbuf("idb", [128, 128], BF16)
    ones_b = p.sbuf("ones_b", [128, 128], BF16)
    p.op("pool", lambda e: e.memset(idf[:], 0.0), writes=["idf"])
    p.op("pool", lambda e: e.affine_select(out=idf[:], in_=idf[:], compare_op=ALU.not_equal, fill=1.0,
                                           base=0, pattern=[[-1, 128]], channel_multiplier=1),
         reads=["idf"], writes=["idf"])
    p.op("pool", lambda e: e.tensor_copy(out=idb[:], in_=idf[:]), reads=["idf"], writes=["idb"])
    p.op("pool", lambda e: e.memset(ones_b[:], 1.0), writes=["ones_b"])

    lng_b = p.sbuf("lng_b", [128, D], F32)
    p.dma("sp", lambda e: e.dma_start(out=lng_b[:], in_=ln_g.broadcast_to([128, D])), writes=["lng_b"])

    xnT = p.sbuf("xnT", [128, 8, TA], BF16)
    xt = [p.sbuf("xt%d" % i, [128, D], F32) for i in range(2)]
    xs = [p.sbuf("xs%d" % i, [128, D], BF16) for i in range(2)]
    sq_junk = p.sbuf("sq_junk", [128, D], BF16)
    st = p.sbuf("st", [128, NT, 4], F32)

    def rms_stats(src_ap, n, ss, tmp, rstd, reads, tag):
        p.op("act", lambda e: e.activation(out=sq_junk[0:n, :], in_=src_ap, func=AF.Square, accum_out=ss),
             reads=reads, writes=["sq_junk", tag + "a"])
        p.op("dve", lambda e: e.tensor_scalar(out=tmp, in0=ss, scalar1=1.0 / D, scalar2=EPS,
                                              op0=ALU.mult, op1=ALU.add), reads=[tag + "a"], writes=[tag + "b"])
        p.op("act", lambda e: e.activation(out=tmp, in_=tmp, func=AF.Sqrt), reads=[tag + "b"], writes=[tag + "b"])
        p.op("dve", lambda e: e.reciprocal(out=rstd, in_=tmp), reads=[tag + "b"], writes=[tag + "c"])

    for t in range(NT):
        xb_ = xt[t % 2]
        xs_ = xs[t % 2]
        rx, rs_ = "xt%d" % (t % 2), "xs%d" % (t % 2)
        p.dma("sp", lambda e, t=t, xb_=xb_: e.dma_start(out=xb_[:], in_=x_p[t * 128:(t + 1) * 128, :]), writes=[rx])
        tag = "st%d" % t
        rms_stats(xb_[:], 128, st[:, t, 0:1], st[:, t, 1:2], st[:, t, 2:3], [rx], tag)
        p.op("dve", lambda e, t=t, xb_=xb_, xs_=xs_: e.scalar_tensor_tensor(
            out=xs_[:], in0=xb_[:], scalar=st[:, t, 2:3], in1=lng_b[:], op0=ALU.mult, op1=ALU.mult),
            reads=[rx, tag + "c", "lng_b"], writes=[rs_])
        b = gen.next()
        pst = PS[b].bitcast(BF16)
        for kt in range(8):
            p.op("pe", lambda e, kt=kt, pst=pst, xs_=xs_: e.transpose(pst[:, kt * 128:(kt + 1) * 128],
                                                                     xs_[:, kt * 128:(kt + 1) * 128], idb[:]),
                 reads=[rs_, "idb"], writes=[psr(b)])
        eng = "act" if t % 2 == 0 else "dve"
        dst = xnT[:, :, t * 128:(t + 1) * 128]
        src = pst[:, :].rearrange("p (k t) -> p k t", k=8)
        if eng == "act":
            p.op("act", lambda e, dst=dst, src=src: e.copy(out=dst, in_=src), reads=[psr(b)], writes=["xnT.%d" % t])
        else:
            p.op("dve", lambda e, dst=dst, src=src: e.tensor_copy(out=dst, in_=src), reads=[psr(b)], writes=["xnT.%d" % t])

    st_s = p.sbuf("st_s", [128, 4], F32)
    p.dma("sp", lambda e: e.dma_start(out=xt[0][0:NS, :], in_=x_s), writes=["xt0"])
    rms_stats(xt[0][0:NS, :], NS, st_s[0:NS, 0:1], st_s[0:NS, 1:2], st_s[0:NS, 2:3], ["xt0"], "sts")
    p.op("dve", lambda e: e.scalar_tensor_tensor(out=xs[0][0:NS, :], in0=xt[0][0:NS, :], scalar=st_s[0:NS, 2:3], in1=lng_b[0:NS, :],
                                                 op0=ALU.mult, op1=ALU.mult), reads=["xt0", "stsc", "lng_b"], writes=["xs0"])
    b = gen.next()
    pst = PS[b].bitcast(BF16)
    for kt in range(8):
        p.op("pe", lambda e, kt=kt, pst=pst: e.transpose(pst[:, kt * 128:kt * 128 + NS], xs[0][0:NS, kt * 128:(kt + 1) * 128], idb[0:NS, 0:NS]),
             reads=["xs0", "idb"], writes=[psr(b)])
    p.op("dve", lambda e, pst=pst: e.tensor_copy(out=xnT[:, :, T:TA], in_=pst[:, :].rearrange("p (k t) -> p k t", k=8)[:, :, 0:NS]),
         reads=[psr(b)], writes=["xnT.s"])
    XN_ALL = ["xnT.%d" % t for t in range(NT)]
    if stop_after == 0:
        dbg = dout("dbg", [128, 8 * T], BF16)
        p.dma("sp", lambda e: e.dma_start(out=dbg, in_=xnT[:].rearrange("p k t -> p (k t)")), reads=XN_ALL, is_output=True)
        return nc, p, locals()

    def xn_blk(b):
        if b == NB:
            return ["xnT.s"]
        return ["xnT.%d" % t for t in range(4 * b, 4 * b + 4)]

    def bn(b):
        return NS if b == NB else 512

    def bsl(b):
        return slice(T, TA) if b == NB else slice(b * 512, (b + 1) * 512)

    NWB = 2
    wbuf = [p.sbuf("wbuf%d" % i, [128, 8, 512], BF16) for i in range(NWB)]
    wctr = [0]
    w_in_v = w_in.rearrange("(kt p) c -> p kt c", p=128)

    def load_w(c0, ncols=512, qperm=False):
        i = wctr[0] % NWB
        wctr[0] += 1
        wb = wbuf[i]
        name = "wbuf%d" % i
        if qperm:
            for h in range(4):
                for g in range(2):
                    srcv = w_in_v[:, :, g * 256 + h * 64: g * 256 + h * 64 + 64]
                    dstv = wb[:, :, h * 128 + g * 64: h * 128 + g * 64 + 64]
                    p.dma("pool", lambda e, srcv=srcv, dstv=dstv: e.dma_start(out=dstv, in_=srcv), writes=[name])
        else:
            for half in range(2):
                ks = slice(half * 4, half * 4 + 4)
                p.dma("pool", lambda e, ks=ks: e.dma_start(out=wb[:, ks, 0:ncols], in_=w_in_v[:, ks, c0:c0 + ncols]),
                      writes=[name])
        return wb, name

    def proj_fm(wb, wname, cc, blk, bank):
        for kt in range(8):
            p.op("pe", lambda e, kt=kt: e.matmul(PS[bank][:, 0:bn(blk)], wb[:, kt, cc * 128:(cc + 1) * 128],
                                                  xnT[:, kt, bsl(blk)],
                                                  start=(kt == 0), stop=(kt == 7)),
                 reads=[wname] + xn_blk(blk), writes=[psr(bank)])

    evac_rr = [0]

    def evac(out_ap, bank, writes, func=None, scale=1.0, eng=None, in_ap=None, extra_reads=(), n=512):
        src = PS[bank][:, 0:n] if in_ap is None else in_ap
        if func is not None:
            eng = "act"
        if eng is None:
            eng = "act" if evac_rr[0] % 2 == 0 else "dve"
            evac_rr[0] += 1
        rd = [psr(bank)] + list(extra_reads)
        if eng == "act":
            f = AF.Copy if func is None else func
            p.op("act", lambda e: e.activation(out=out_ap, in_=src, func=f, scale=scale), reads=rd, writes=writes)
        else:
            if scale == 1.0:
                p.op("dve", lambda e: e.tensor_copy(out=out_ap, in_=src), reads=rd, writes=writes)
            else:
                p.op("dve", lambda e: e.tensor_scalar_mul(out=out_ap, in0=src, scalar1=scale), reads=rd, writes=writes)

    W1bd = [p.ar("W1bd%d" % i, [128, 32, 128], BF16) for i in range(2)]
    kvss = p.ar("kvss", [NS, 768], F32)
    gates_s = p.ar("gates_s", [NS, 24], F32)
    VNs = p.ar("VNs", [NS, 128], BF16)
    VNw = p.ar("VNw", [NS, 128], BF16)
    QTs = p.ar("QTs", [128, 4, NS], BF16)
    KTs = p.ar("KTs", [128, 4, NS], BF16)
    keep_s = p.ar_off
    QT = p.ar("QT", [128, 4, TA], BF16)
    KT = p.ar("KT", [128, 4, TA], BF16)
    SGA = p.ar("SGA", [128, 4, TA], BF16)
    VS = p.ar("VS", [128, NT, 2, 65], BF16)
    VW = p.ar("VW", [128, NT, 2, 65], BF16)
    gates = p.ar("gates", [128, NT, 24], F32)
    kvst = [p.ar("kvst%d" % i, [128, 768], F32) for i in range(1)]
    p.op("pool", lambda e: e.memset(VS[:], 1.0), writes=["VS.ones"])
    p.op("pool", lambda e: e.memset(VW[:], 1.0), writes=["VW.ones"])

    wb, wn = load_w(0, qperm=True)
    for cc in range(4):
        for blk in range(NBA):
            bk = gen.next()
            proj_fm(wb, wn, cc, blk, bk)
            evac(QT[:, cc, bsl(blk)], bk, ["QT.%d.%d" % (cc, blk)], scale=0.125, n=bn(blk))
    if stop_after <= 0.5:
        return nc, p, locals()
    wkv1, wkv1n = load_w(C_KV)
    wkv2, wkv2n = load_w(C_KV + 512, 280)
    Cm = p.ar("Cm", [128, 512], BF16)
    Lm = p.ar("Lm", [128, 512], BF16)
    Em = p.ar("Em", [128, 16, 128], BF16)
    p.op("pool", lambda e: e.memset(Cm[:], 0.0), writes=["Cm"])
    p.op("pool", lambda e: e.affine_select(out=Cm[:], in_=Cm[:], compare_op=ALU.is_ge, fill=-BIG, base=0,
                                           pattern=[[1, 512]], channel_multiplier=-1), reads=["Cm"], writes=["Cm"])
    p.op("pool", lambda e: e.memset(Lm[:], 0.0), writes=["Lm"])
    p.op("pool", lambda e: e.affine_select(out=Lm[:], in_=Lm[:], compare_op=ALU.is_ge, fill=-BIG, base=384,
                                           pattern=[[-1, 512]], channel_multiplier=1), reads=["Lm"], writes=["Lm"])
    p.op("pool", lambda e: e.memset(Em[:], 0.0), writes=["Em"])
    p.op("pool", lambda e: e.affine_select(out=Em[0:32].rearrange("p t (a b) -> p t a b", a=2), in_=Em[0:32].rearrange("p t (a b) -> p t a b", a=2),
                                           compare_op=ALU.not_equal, fill=BIG, base=0,
                                           pattern=[[-2, 16], [-1, 2], [0, 64]], channel_multiplier=1), reads=["Em"], writes=["Em"])
    Asc = p.ar("Asc", [128, 8, 32], F32)
    Bsc = p.ar("Bsc", [128, 8, 32], F32)
    p.op("pool", lambda e: e.memset(Asc[:], 0.0), writes=["Asc"])
    p.op("pool", lambda e: e.memset(Bsc[:], 0.0), writes=["Bsc"])
    for t in range(8, 16):
        for half in range(2):
            cur = 2 * t + half
            ps_ = slice(half * 64, half * 64 + 64)
            p.op("pool", lambda e, t=t, ps_=ps_, cur=cur: e.memset(Asc[ps_, t - 8, 1:cur - 1], 1.0), reads=["Asc"], writes=["Asc"])
            p.op("pool", lambda e, t=t, ps_=ps_: e.memset(Bsc[ps_, t - 8, 0:1], 1e4), reads=["Bsc"], writes=["Bsc"])
            p.op("pool", lambda e, t=t, ps_=ps_, cur=cur: e.memset(Bsc[ps_, t - 8, cur - 1:cur + 1], 1e4), reads=["Bsc"], writes=["Bsc"])
            if cur + 1 < 32:
                p.op("pool", lambda e, t=t, ps_=ps_, cur=cur: e.memset(Bsc[ps_, t - 8, cur + 1:32], -1.0), reads=["Bsc"], writes=["Bsc"])

    for (wbx, wnx, cc, slot) in ((wkv1, wkv1n, 0, 0), (wkv1, wkv1n, 1, 1), (wkv1, wkv1n, 2, 2), (wkv2, wkv2n, 0, 3)):
        for blk in range(NBA):
            bk = gen.next()
            proj_fm(wbx, wnx, cc, blk, bk)
            evac(KT[:, slot, bsl(blk)], bk, ["KT.%d.%d" % (slot, blk)], n=bn(blk))
    if stop_after <= 0.6:
        return nc, p, locals()
    for t in range(NT):
        b1 = gen.next()
        b2 = gen.next()
        for kt in range(8):
            p.op("pe", lambda e, b1=b1, kt=kt, t=t: e.matmul(PS[b1][:, :], xnT[:, kt, t * 128:(t + 1) * 128], wkv1[:, kt, 0:512],
                                                      start=(kt == 0), stop=(kt == 7)),
                 reads=[wkv1n, "xnT.%d" % t], writes=[psr(b1)])
        for kt in range(8):
            p.op("pe", lambda e, b2=b2, kt=kt, t=t: e.matmul(PS[b2][:, 0:280], xnT[:, kt, t * 128:(t + 1) * 128], wkv2[:, kt, 0:280],
                                                      start=(kt == 0), stop=(kt == 7)),
                 reads=[wkv2n, "xnT.%d" % t], writes=[psr(b2)])
        ks_ = kvst[0]
        kn = "kvst0"
        p.op("dve", lambda e, b1=b1, ks_=ks_: e.tensor_copy(out=ks_[:, 0:512], in_=PS[b1][:, :]), reads=[psr(b1)], writes=[kn + "a"])
        p.op("act", lambda e, b2=b2, ks_=ks_: e.copy(out=ks_[:, 512:768], in_=PS[b2][:, 0:256]), reads=[psr(b2)], writes=[kn + "b"])
        p.op("act", lambda e, b2=b2, t=t: e.activation(out=gates[:, t, :], in_=PS[b2][:, 256:280], func=AF.Sigmoid),
             reads=[psr(b2)], writes=["gates.%d" % t])
        p.op("dve", lambda e, b1=b1, t=t: e.tensor_copy(out=VS[:, t, :, 0:64], in_=PS[b1][:, 384:512].rearrange("p (g d) -> p g d", g=2)),
             reads=[psr(b1), "VS.ones"], writes=["VS.%d" % t])
        p.op("dve", lambda e, b2=b2, t=t: e.tensor_copy(out=VW[:, t, :, 0:64], in_=PS[b2][:, 128:256].rearrange("p (g d) -> p g d", g=2)),
             reads=[psr(b2), "VW.ones"], writes=["VW.%d" % t])
        for i in range(4):
            p.dma("sp", lambda e, i=i, t=t, ks_=ks_: e.dma_start(out=o_kv[i][t * 128:(t + 1) * 128, :], in_=ks_[:, i * 128:(i + 1) * 128]),
                  reads=[kn + "a"], is_output=True)
        if t >= NT - 4:
            r0 = (t - (NT - 4)) * 128
            p.dma("sp", lambda e, r0=r0, ks_=ks_: e.dma_start(out=o_kw[r0:r0 + 128, :], in_=ks_[:, 512:640]), reads=[kn + "b"], is_output=True)
            p.dma("sp", lambda e, r0=r0, ks_=ks_: e.dma_start(out=o_vw[r0:r0 + 128, :], in_=ks_[:, 640:768]), reads=[kn + "b"], is_output=True)
    b1 = gen.next()
    b2 = gen.next()
    for kt in range(8):
        p.op("pe", lambda e, b1=b1, kt=kt: e.matmul(PS[b1][0:NS, :], xnT[:, kt, T:TA], wkv1[:, kt, 0:512], start=(kt == 0), stop=(kt == 7)),
             reads=[wkv1n, "xnT.s"], writes=[psr(b1)])
    for kt in range(8):
        p.op("pe", lambda e, b2=b2, kt=kt: e.matmul(PS[b2][0:NS, 0:280], xnT[:, kt, T:TA], wkv2[:, kt, 0:280], start=(kt == 0), stop=(kt == 7)),
             reads=[wkv2n, "xnT.s"], writes=[psr(b2)])
    p.op("dve", lambda e, b1=b1: e.tensor_copy(out=kvss[:, 0:512], in_=PS[b1][0:NS, :]), reads=[psr(b1)], writes=["kvss.a"])
    p.op("dve", lambda e, b1=b1: e.tensor_copy(out=VNs[:, :], in_=PS[b1][0:NS, 384:512]), reads=[psr(b1)], writes=["VNs"])
    p.op("act", lambda e, b2=b2: e.copy(out=kvss[:, 512:768], in_=PS[b2][0:NS, 0:256]), reads=[psr(b2)], writes=["kvss.b"])
    p.op("act", lambda e, b2=b2: e.copy(out=VNw[:, :], in_=PS[b2][0:NS, 128:256]), reads=[psr(b2)], writes=["VNw"])
    p.op("act", lambda e, b2=b2: e.activation(out=gates_s[:, :], in_=PS[b2][0:NS, 256:280], func=AF.Sigmoid), reads=[psr(b2)], writes=["gates_s"])
    for i in range(4):
        p.dma("sp", lambda e, i=i: e.dma_start(out=s_kv[i][:, :], in_=kvss[:, i * 128:(i + 1) * 128]), reads=["kvss.a"], is_output=True)
    for sq in range(4):
        p.dma("sp", lambda e, sq=sq: e.dma_start(out=s_kw[sq, 0:508, :], in_=ckw[sq, 4:512, :]), is_output=True)
        p.dma("sp", lambda e, sq=sq: e.dma_start(out=s_vw[sq, 0:508, :], in_=cvw[sq, 4:512, :]), is_output=True)
        p.dma("sp", lambda e, sq=sq: e.dma_start(out=s_kw[sq, 508:512, :], in_=kvss[4 * sq:4 * sq + 4, 512:640]), reads=["kvss.b"], is_output=True)
        p.dma("sp", lambda e, sq=sq: e.dma_start(out=s_vw[sq, 508:512, :], in_=kvss[4 * sq:4 * sq + 4, 640:768]), reads=["kvss.b"], is_output=True)
    if stop_after <= 0.7:
        return nc, p, locals()
    wb, wn = load_w(C_GA)
    for cc in range(4):
        for blk in range(NBA):
            bk = gen.next()
            proj_fm(wb, wn, cc, blk, bk)
            evac(SGA[:, cc, bsl(blk)], bk, ["SGA.%d.%d" % (cc, blk)], func=AF.Silu, n=bn(blk))


    if stop_after <= 1:
        return nc, p, locals()

    W2bd = [p.sbuf("W2bd%d" % i, [128, 128], BF16) for i in range(2)]
    PET = [p.sbuf("PET%d" % i, [128, 32], BF16) for i in range(2)]
    H0 = [p.sbuf("H0_%d" % i, [128, 1], F32) for i in range(2)]
    pen = p.sbuf("pen", [32, 128], F32)
    for i, (w1, w2, pe) in enumerate(((w1_k, w2_k, pe_k), (w1_v, w2_v, pe_v))):
        p.op("pool", lambda e, i=i: e.memset(W1bd[i][:], 0.0), writes=["W1bd%d" % i])
        p.op("pool", lambda e, i=i: e.memset(W2bd[i][:], 0.0), writes=["W2bd%d" % i])
        w1v = w1.rearrange("(j d) h -> d j h", d=64)
        for g in range(2):
            p.dma("pool", lambda e, i=i, g=g, w1v=w1v: e.dma_start(out=W1bd[i][g * 64:(g + 1) * 64, :, g * 64:(g + 1) * 64], in_=w1v),
                  writes=["W1bd%d" % i])
            p.dma("pool", lambda e, i=i, g=g, w2=w2: e.dma_start(out=W2bd[i][g * 64:(g + 1) * 64, g * 64:(g + 1) * 64], in_=w2),
                  writes=["W2bd%d" % i])
            p.dma("sp", lambda e, g=g, pe=pe: e.dma_start(out=pen[:, g * 64:(g + 1) * 64], in_=pe), writes=["pen"])
        bk = gen.next()
        p.op("pe", lambda e, bk=bk: e.transpose(PS[bk][:, 0:32], pen[:, :], idf[0:32, 0:32]), reads=["pen", "idf"], writes=[psr(bk)])
        p.op("dve", lambda e, bk=bk, i=i: e.tensor_copy(out=PET[i][:], in_=PS[bk][:, 0:32]), reads=[psr(bk)], writes=["PET%d" % i])
        bk = gen.next()
        for j in range(32):
            p.op("pe", lambda e, bk=bk, i=i, j=j: e.matmul(PS[bk][:, 0:1], W1bd[i][:, j, :], PET[i][:, j:j + 1], start=(j == 0), stop=(j == 31)),
                 reads=["W1bd%d" % i, "PET%d" % i], writes=[psr(bk)])
        p.op("dve", lambda e, bk=bk, i=i: e.tensor_copy(out=H0[i][:], in_=PS[bk][:, 0:1]), reads=[psr(bk)], writes=["H0_%d" % i])

    kcmpT = p.sbuf("kcmpT", [128, 128], BF16)
    vaug = p.sbuf("vaug", [128, 2, 97], BF16)
    hs = p.sbuf("hs", [128, 128], BF16)
    aggf = p.sbuf("aggf", [128, 32], F32)
    p.op("pool", lambda e: e.memset(aggf[:], 1.0), writes=["aggf"])
    p.op("pool", lambda e: e.affine_select(out=aggf[:], in_=aggf[:], compare_op=ALU.is_ge, fill=0.0, base=1,
                                           pattern=[[-4, 32]], channel_multiplier=1), reads=["aggf"], writes=["aggf"])
    p.op("pool", lambda e: e.affine_select(out=aggf[:], in_=aggf[:], compare_op=ALU.is_ge, fill=0.0, base=3,
                                           pattern=[[4, 32]], channel_multiplier=-1), reads=["aggf"], writes=["aggf"])
    p.op("pool", lambda e: e.memset(vaug[:], 1.0), writes=["vaug"])
    for g in range(2):
        p.op("pool", lambda e, g=g: e.tensor_copy(out=vaug[:, g, 65:97], in_=aggf[:]), reads=["aggf", "vaug"], writes=["vaug"])

    KT_ALL = lambda slot: ["KT.%d.%d" % (slot, b) for b in range(NB)]
    NCMP = 127
    for i in range(2):
        bk = gen.next()
        srcv = KT[:, i, :].rearrange("p (c j) -> p j c", j=16)
        for j in range(32):
            r, jj = j // 16, j % 16
            p.op("pe", lambda e, bk=bk, i=i, j=j, r=r, jj=jj, srcv=srcv: e.matmul(
                PS[bk][:, 0:NCMP], W1bd[i][:, j, :], srcv[:, jj, r:r + NCMP], start=(j == 0), stop=(j == 31)),
                reads=["W1bd%d" % i] + KT_ALL(i), writes=[psr(bk)])
        p.op("act", lambda e, bk=bk, i=i: e.activation(out=hs[:, 0:NCMP], in_=PS[bk][:, 0:NCMP], func=AF.Silu, bias=H0[i][:, 0:1]),
             reads=[psr(bk), "H0_%d" % i], writes=["hs"])
        bk2 = gen.next()
        if i == 0:
            p.op("pe", lambda e, bk2=bk2: e.matmul(PS[bk2][:, 0:NCMP], W2bd[0][:, :], hs[:, 0:NCMP], start=True, stop=True),
                 reads=["W2bd0", "hs"], writes=[psr(bk2)])
            p.op("dve", lambda e, bk2=bk2: e.tensor_copy(out=kcmpT[:, 0:NCMP], in_=PS[bk2][:, 0:NCMP]), reads=[psr(bk2)], writes=["kcmpT"])
        else:
            p.op("pe", lambda e, bk2=bk2: e.matmul(PS[bk2][0:NCMP, 0:128], hs[:, 0:NCMP], W2bd[1][:, :], start=True, stop=True),
                 reads=["W2bd1", "hs"], writes=[psr(bk2)])
            p.op("dve", lambda e, bk2=bk2: e.tensor_copy(out=vaug[0:NCMP, :, 0:64], in_=PS[bk2][0:NCMP, 0:128].rearrange("p (g d) -> p g d", g=2)),
                 reads=[psr(bk2), "vaug"], writes=["vaug"])

    if stop_after <= 2:
        dbg = dout("dbg", [128, 128], BF16)
        dbg2 = dout("dbg2", [128, 194], BF16)
        p.dma("sp", lambda e: e.dma_start(out=dbg, in_=kcmpT[:]), reads=["kcmpT"], is_output=True)
        p.dma("sp", lambda e: e.dma_start(out=dbg2, in_=vaug[:].rearrange("p g c -> p (g c)")), reads=["vaug"], is_output=True)
        return nc, p, locals()

    mcmp = p.ar("mcmp", [128, 512], BF16)
    PTb = [p.ar("PT%d" % i, [128, 512], BF16) for i in range(4)]
    Oacc = p.ar("Oacc", [128, 4, 512], F32)
    Obf = p.ar("Obf", [128, 4, 512], BF16)
    imp = p.ar("imp", [128, 4, 2, 32], F32)
    MT = p.ar("MT", [128, 2, 512], BF16)
    QZ = p.ar("QZ", [128, 8, 512], BF16)
    p.op("pool", lambda e: e.memset(MT[:], 0.0), writes=["MT.0", "MT.1"])
    p.op("pool", lambda e: e.memset(QZ[:], 0.0), writes=["QZ.%d" % h_ for h_ in range(8)])
    OAT = p.sbuf("OAT", [128, 4, TA], BF16)
    ep = [p.sbuf("ep%d" % i, [128, 16], F32) for i in range(2)]
    eptmp = [p.sbuf("eptmp%d" % i, [128, 4, 64], F32) for i in range(2)]
    sc = p.sbuf("sc", [128, 32], F32)
    scr = p.sbuf("scr", [128, 32], F32)
    m16 = p.sbuf("m16", [128, 16], F32)
    msel = p.sbuf("msel", [128, 32], BF16)
    sb_S = Banks([0, 1, 4])
    sb_O = Banks([2, 3])
    gen2 = Banks([5, 6, 7])
    ptc = [0]
    epc = [0]

    for b in range(NB):
        qs = slice(b * 512, (b + 1) * 512)
        gate_r = ["gates.%d" % t for t in range(4 * b, 4 * b + 4)]
        p.op("pool", lambda e: e.memset(mcmp[:], 0.0), writes=["mcmp"])
        p.op("pool", lambda e, b=b: e.affine_select(out=mcmp[:], in_=mcmp[:], compare_op=ALU.is_ge, fill=-BIG, base=512 * b - 31,
                                                    pattern=[[1, 512]], channel_multiplier=-16), reads=["mcmp"], writes=["mcmp"])
        for h_ in range(8):
            cp_, hh_ = h_ % 4, h_ // 4
            psl_ = slice(hh_ * 64, hh_ * 64 + 64)
            eng_ = "pool" if h_ % 2 == 0 else "dve"
            p.op(eng_, lambda e, h_=h_, cp_=cp_, psl_=psl_, b=b: e.tensor_copy(out=QZ[psl_, h_, :], in_=QT[psl_, cp_, b * 512:(b + 1) * 512]),
                 reads=["QT.%d.%d" % (cp_, b), "QZ.%d" % h_], writes=["QZ.%d" % h_])
        def run_tiles(tiles):
            sbk = [None] * len(tiles)
            def issue_S(n):
                sbk[n] = sb_S.next()
                tiles[n]["S"](sbk[n])
            for n0 in range(min(2, len(tiles))):
                issue_S(n0)
            for n, tl in enumerate(tiles):
                if n + 2 < len(tiles):
                    issue_S(n + 2)
                pi = ptc[0] % 4
                ptc[0] += 1
                lo, hi = tl["cols"]
                nr = tl["rows"]
                bs = sbk[n]
                p.op("act", lambda e, pi=pi, lo=lo, hi=hi, nr=nr, bs=bs: e.activation(out=PTb[pi][0:nr, lo:hi], in_=PS[bs][0:nr, lo:hi], func=AF.Exp),
                     reads=[psr(bs)], writes=["PT%d" % pi])
                tl["PV"](pi)
                if tl.get("epi"):
                    tl["epi"]()

        def head_info(cp, hh):
            h = cp + 4 * hh
            return h, hh, slice(hh * 64, hh * 64 + 64)

        tiles = []
        for cp in range(4):
            for hh in range(2):
                h, g, psl = head_info(cp, hh)
                bo = sb_O.next()

                def S(bs, cp=cp, psl=psl, b=b, h=h):
                    p.op("pe", lambda e: e.matmul(PS[bs][0:NCMP, :], kcmpT[:, 0:NCMP], QZ[:, h, :], start=True, stop=False),
                         reads=["kcmpT", "QZ.%d" % h], writes=[psr(bs)])
                    p.op("pe", lambda e: e.matmul(PS[bs][0:NCMP, :], idb[0:NCMP, 0:NCMP], mcmp[0:NCMP, :], start=False, stop=True),
                         reads=["idb", "mcmp"], writes=[psr(bs)])

                def PV(pi, g=g, bo=bo):
                    for qt in range(4):
                        p.op("pe", lambda e, qt=qt: e.matmul(PS[bo][:, qt * 97:(qt + 1) * 97], PTb[pi][0:NCMP, qt * 128:(qt + 1) * 128],
                                                              vaug[0:NCMP, g, :], start=(qt == 0), stop=True, skip_group_check=True),
                             reads=["PT%d" % pi, "vaug"], writes=[psr(bo)])

                def epi(h=h, g=g, bo=bo, b=b, first_in_group=(cp == 0)):
                    k = epc[0] % 2
                    epc[0] += 1
                    Ov = PS[bo][:, 0:388].rearrange("p (q c) -> p q c", q=4)
                    e_, et = ep[k], eptmp[k]
                    en, etn = "ep%d" % k, "eptmp%d" % k
                    p.op("dve", lambda e: e.tensor_scalar_max(out=e_[:, 0:4].unsqueeze(2), in0=Ov[:, :, 64:65], scalar1=1e-30), reads=[psr(bo)], writes=[en])
                    p.op("dve", lambda e: e.reciprocal(out=e_[:, 4:8], in_=e_[:, 0:4]), reads=[en], writes=[en])
                    p.op("dve", lambda e: e.tensor_tensor(out=e_[:, 8:12], in0=e_[:, 4:8], in1=gates[:, 4 * b:4 * b + 4, 3 * h + 0], op=ALU.mult),
                         reads=[en] + gate_r, writes=[en])
                    p.op("dve", lambda e: e.tensor_tensor(out=Oacc[:, :, h * 64:(h + 1) * 64], in0=Ov[:, :, 0:64],
                                                          in1=e_[:, 8:12].unsqueeze(2).to_broadcast([128, 4, 64]), op=ALU.mult),
                         reads=[psr(bo), en], writes=["Oacc.%d" % h])
                    if first_in_group:
                        p.op("dve", lambda e: e.tensor_tensor(out=imp[:, :, g, :], in0=Ov[:, :, 65:97],
                                                              in1=e_[:, 4:8].unsqueeze(2).to_broadcast([128, 4, 32]), op=ALU.mult),
                             reads=[psr(bo), en], writes=["imp.%d" % g])
                    else:
                        p.op("dve", lambda e: e.tensor_tensor(out=et[:, :, 0:32], in0=Ov[:, :, 65:97],
                                                              in1=e_[:, 4:8].unsqueeze(2).to_broadcast([128, 4, 32]), op=ALU.mult),
                             reads=[psr(bo), en], writes=[etn])
                        p.op("dve", lambda e: e.tensor_tensor(out=imp[:, :, g, :], in0=imp[:, :, g, :], in1=et[:, :, 0:32], op=ALU.add),
                             reads=[etn, "imp.%d" % g], writes=["imp.%d" % g])

                tiles.append(dict(S=S, rows=NCMP, cols=(0, 512), PV=PV, epi=epi))
        run_tiles(tiles)

        use_sel = b >= 2
        if use_sel:
            for qt in range(4):
                t = 4 * b + qt
                for g in range(2):
                    p.op("dve", lambda e, qt=qt, g=g, t=t: e.tensor_tensor(out=sc[:], in0=imp[:, qt, g, :], in1=Asc[:, t - 8, :], op=ALU.mult),
                         reads=["imp.%d" % g, "Asc"], writes=["sc"])
                    p.op("dve", lambda e, t=t: e.tensor_tensor(out=sc[:], in0=sc[:], in1=Bsc[:, t - 8, :], op=ALU.add), reads=["sc", "Bsc"], writes=["sc"])
                    p.op("dve", lambda e: e.max(out=m16[:, 0:8], in_=sc[:]), reads=["sc"], writes=["m16a"])
                    p.op("dve", lambda e: e.match_replace(out=scr[:], in_to_replace=m16[:, 0:8], in_values=sc[:], imm_value=-1e9),
                         reads=["sc", "m16a"], writes=["scr"])
                    p.op("dve", lambda e: e.max(out=m16[:, 8:16], in_=scr[:]), reads=["scr"], writes=["m16b"])
                    p.op("dve", lambda e: e.tensor_scalar(out=msel[:], in0=sc[:], scalar1=m16[:, 15:16], scalar2=1.0, op0=ALU.is_ge, op1=ALU.subtract),
                         reads=["sc", "m16b"], writes=["msel"])
                    bk = gen2.next()
                    pst = PS[bk].bitcast(BF16)
                    p.op("pe", lambda e, pst=pst, bk=bk: e.transpose(pst[0:32, 0:128], msel[:, :], idb[:]), reads=["msel", "idb"], writes=[psr(bk)])
                    p.op("dve", lambda e, pst=pst, g=g, qt=qt: e.tensor_copy(out=MT[0:32, g, qt * 128:(qt + 1) * 128], in_=pst[0:32, 0:128]),
                         reads=[psr(bk)], writes=["MT.%d" % g])

        tiles = []
        for cp in range(4):
            for hh in range(2):
                h, g, psl = head_info(cp, hh)
                for br, slot, Vt, vname in ((1, 2, VS, "VS"), (2, 3, VW, "VW")):
                    bo = sb_O.next()
                    if br == 1:
                        kts = list(range(0, 4 * b + 4))
                    else:
                        kts = [kt for kt in range(4 * b - 4, 4 * b + 4) if kt >= 0]
                    state = {"first": True}
                    for kt in kts:
                        i = kt - 4 * b
                        if i >= 0:
                            lo, hi = 128 * i, 512
                            qts = list(range(i, 4))
                            mask = ("C", 0, 512 - 128 * i)
                        elif br == 2:
                            lo, hi = 0, 128 * (5 + i)
                            qts = list(range(0, 5 + i))
                            c0 = -128 * i - 128
                            mask = ("L", c0, c0 + hi)
                        else:
                            lo, hi = 0, 512
                            qts = list(range(4))
                            mask = None
                        selm = (br == 1 and use_sel)

                        def S(bs, cp=cp, psl=psl, b=b, slot=slot, kt=kt, lo=lo, hi=hi, mask=mask, selm=selm, g=g, h=h):
                            last = (mask is None and not selm)
                            p.op("pe", lambda e: e.matmul(PS[bs][:, lo:hi], KT[:, slot, kt * 128:(kt + 1) * 128],
                                                          QZ[:, h, lo:hi], start=True, stop=last),
                                 reads=["KT.%d.%d" % (slot, kt // 4), "QZ.%d" % h], writes=[psr(bs)])
                            if selm:
                                p.op("pe", lambda e: e.matmul(PS[bs][:, lo:hi], Em[:, kt, :], MT[:, g, lo:hi], start=False, stop=(mask is None)),
                                     reads=["Em", "MT.%d" % g], writes=[psr(bs)])
                            if mask is not None:
                                mt = Cm if mask[0] == "C" else Lm
                                p.op("pe", lambda e: e.matmul(PS[bs][:, lo:hi], idb[:, :], mt[:, mask[1]:mask[2]], start=False, stop=True),
                                     reads=["idb", "Cm", "Lm"], writes=[psr(bs)])

                        def PV(pi, kt=kt, qts=qts, g=g, bo=bo, Vt=Vt, vname=vname, state=state, b=b):
                            for qt in qts:
                                first = state["first"]
                                state["first"] = False
                                p.op("pe", lambda e, qt=qt, first=first: e.matmul(PS[bo][:, qt * 65:(qt + 1) * 65], PTb[pi][:, qt * 128:(qt + 1) * 128],
                                                                                   Vt[:, kt, g, :], start=first, stop=(kt == 4 * b + qt), skip_group_check=True),
                                     reads=["PT%d" % pi, "%s.%d" % (vname, kt)], writes=[psr(bo)])

                        epi = None
                        if kt == kts[-1]:
                            def epi(h=h, bo=bo, b=b, br=br):
                                k = epc[0] % 2
                                epc[0] += 1
                                Ov = PS[bo][:, 0:260].rearrange("p (q c) -> p q c", q=4)
                                e_, et = ep[k], eptmp[k]
                                en, etn = "ep%d" % k, "eptmp%d" % k
                                p.op("dve", lambda e: e.reciprocal(out=e_[:, 4:8].unsqueeze(2), in_=Ov[:, :, 64:65]), reads=[psr(bo)], writes=[en])
                                p.op("dve", lambda e: e.tensor_tensor(out=e_[:, 8:12], in0=e_[:, 4:8], in1=gates[:, 4 * b:4 * b + 4, 3 * h + br], op=ALU.mult),
                                     reads=[en] + gate_r, writes=[en])
                                p.op("dve", lambda e: e.tensor_tensor(out=et[:, :, :], in0=Ov[:, :, 0:64],
                                                                      in1=e_[:, 8:12].unsqueeze(2).to_broadcast([128, 4, 64]), op=ALU.mult),
                                     reads=[psr(bo), en], writes=[etn])
                                p.op("pool", lambda e: e.tensor_tensor(out=Oacc[:, :, h * 64:(h + 1) * 64], in0=Oacc[:, :, h * 64:(h + 1) * 64],
                                                                       in1=et[:, :, :], op=ALU.add),
                                     reads=[etn, "Oacc.%d" % h], writes=["Oacc.%d" % h])
                        tiles.append(dict(S=S, rows=128, cols=(lo, hi), PV=PV, epi=epi))
        run_tiles(tiles)

        p.op("act", lambda e: e.copy(out=Obf[:], in_=Oacc[:]), reads=["Oacc.%d" % h for h in range(8)], writes=["Obf"])
        for fc in range(4):
            bk = gen2.next()
            pst = PS[bk].bitcast(BF16)
            for qt in range(4):
                p.op("pe", lambda e, pst=pst, qt=qt, fc=fc: e.transpose(pst[:, qt * 128:(qt + 1) * 128], Obf[:, qt, fc * 128:(fc + 1) * 128], idb[:]),
                     reads=["Obf", "idb"], writes=[psr(bk)])
            p.op("dve", lambda e, pst=pst, fc=fc, b=b: e.tensor_tensor(out=OAT[:, fc, b * 512:(b + 1) * 512], in0=pst[:, 0:512],
                                                                       in1=SGA[:, fc, b * 512:(b + 1) * 512], op=ALU.mult),
                 reads=[psr(bk), "SGA.%d.%d" % (fc, b)], writes=["OAT.%d.%d" % (fc, b)])

    if not do_sample:
        p.op("pool", lambda e: e.memset(OAT[:, :, T:TA], 0.0), writes=["OAT.%d.%d" % (fc, NB) for fc in range(4)])
    else:
        U32 = mybir.dt.uint32
        pt = din("pt", [512, 1], I32)
        pk_c = din("pk_c", [5120, 16384])
        pv_c = din("pv_c", [5120, 16384])
        pk_s = din("pk_s", [5120, 16384]).rearrange("n (h x) -> (n h) x", h=2)
        pv_s = din("pv_s", [5120, 16384]).rearrange("n (h x) -> (n h) x", h=2)
        scr_idx = nc.dram_tensor("scr_idx", [4, 128, 2], I32).ap()
        p.op("dve", lambda e: e.tensor_copy(out=QTs[:], in_=QT[:, :, T:TA]), reads=["QT.%d.%d" % (c_, NB) for c_ in range(4)], writes=["QTs"])
        p.op("dve", lambda e: e.tensor_copy(out=KTs[:], in_=KT[:, :, T:TA]), reads=["KT.%d.%d" % (c_, NB) for c_ in range(4)], writes=["KTs"])
        SGAs = p.sbuf("SGAs", [128, 4, NS], BF16)
        p.op("dve", lambda e: e.tensor_copy(out=SGAs[:], in_=SGA[:, :, T:TA]), reads=["SGA.%d.%d" % (c_, NB) for c_ in range(4)], writes=["SGAs"])
        p.arena_reset(keep=keep_s)
        NG = 3
        Gt = [p.ar("Gt%d" % i, [128, 2048], BF16) for i in range(NG)]
        gtc = [0]
        BGB = p.ar("BGB", [128, 16384], BF16)
        KTr = BGB.rearrange("p (c j q) -> p c j q", c=8, j=16)
        KsT = p.ar("KsT", [128, 64, 128], BF16)
        VGb = p.ar("VGb", [128, 8192], BF16)
        hs_s = p.ar("hs_s", [128, 1024], BF16)
        kcT_s = p.ar("kcT_s", [128, 1024], BF16)
        vc_s = p.ar("vc_s", [128, 8, 128], BF16)
        aggS = p.ar("aggS", [128, 8, 257], BF16)
        PTc = p.ar("PTc", [128, 8, 64], BF16)
        PTs = [p.ar("PTs%d" % i, [128, 8, 64], BF16) for i in range(2)]
        PTw = p.ar("PTw", [128, 4, 64], BF16)
        PTn = [p.ar("PTn%d" % i, [NS, 64], BF16) for i in range(2)]
        Kwn = p.ar("Kwn", [128, 4, 128], BF16)
        KwT = p.ar("KwT", [128, 4, 128], BF16)
        Vw_s = p.ar("Vw_s", [128, 4, 128], BF16)
        imp_n = p.ar("imp_n", [64, 257], F32)
        scs2 = imp_n[0:8, :]
        scs = p.ar("scs", [8, 257], F32)
        m16s = p.ar("m16s", [8, 16], F32)
        i16 = p.ar("i16", [8, 16], U32)
        idxw = p.ar("idxw", [8, 16, 2], I32)
        idxp = p.ar("idxp", [128, 2], I32)
        pgid = p.ar("pgid", [128, 1], I32)
        hpx = p.ar("hpx", [128, 1], I32)
        pti = p.ar("pti", [128, 1], I32)
        maskS = p.ar("maskS", [128, 64], BF16)
        maskW = p.ar("maskW", [128, 64], BF16)
        maskN = p.ar("maskN", [NS, 4, 64], BF16)
        SelE = p.ar("SelE", [64, 16], F32)
        SelO = p.ar("SelO", [64, 16], F32)
        Hsum = p.ar("Hsum", [64, 8], F32)
        pmask = p.ar("pmask", [128, 1], F32)
        Qz = p.ar("Qz", [128, 64], BF16)
        Osamp = p.ar("Osamp", [64, 64], F32)
        OW = p.ar("OW", [64, 128], F32)
        gcol = p.ar("gcol", [64, 3], F32)
        eps_ = p.ar("eps_", [64, 8], F32)
        sO = Banks([0, 1])
        gs6 = Banks([2, 3, 4, 5, 6, 7])

        def build_sample_consts():
            for cc in range(8):
                p.op("pool", lambda e, cc=cc: e.memset(aggS[:, cc, :], 1.0), reads=["aggS"], writes=["aggS"])
                p.op("pool", lambda e, cc=cc: e.affine_select(out=aggS[:, cc, :], in_=aggS[:, cc, :], compare_op=ALU.is_ge, fill=0.0, base=cc + 1,
                                                             pattern=[[-4, 257]], channel_multiplier=8), reads=["aggS"], writes=["aggS"])
                p.op("pool", lambda e, cc=cc: e.affine_select(out=aggS[:, cc, :], in_=aggS[:, cc, :], compare_op=ALU.is_ge, fill=0.0, base=3 - cc,
                                                             pattern=[[4, 257]], channel_multiplier=-8), reads=["aggS"], writes=["aggS"])
            p.op("pool", lambda e: e.memset(pmask[:], 1.0), writes=["pmask"])
            p.op("pool", lambda e: e.affine_select(out=pmask[:], in_=pmask[:], compare_op=ALU.not_equal, fill=0.0, base=-127,
                                                   pattern=[[0, 1]], channel_multiplier=1), reads=["pmask"], writes=["pmask"])
            for tl in (Hsum, SelE, SelO):
                p.op("pool", lambda e, tl=tl: e.memset(tl[:], 0.0), writes=["cst"])
            for g in range(2):
                for hq in range(4):
                    r0 = g * 32 + hq * 4
                    p.op("pool", lambda e, g=g, r0=r0: e.affine_select(out=Hsum[:, g * 4:g * 4 + 4], in_=Hsum[:, g * 4:g * 4 + 4], compare_op=ALU.not_equal, fill=1.0,
                                                                     base=-r0, pattern=[[-1, 4]], channel_multiplier=1), reads=["cst"], writes=["cst"])
                    fc = g * 2 + hq // 2
                    tl = SelE if hq % 2 == 0 else SelO
                    p.op("pool", lambda e, tl=tl, fc=fc, r0=r0: e.affine_select(out=tl[:, fc * 4:fc * 4 + 4], in_=tl[:, fc * 4:fc * 4 + 4], compare_op=ALU.not_equal, fill=1.0,
                                                                                base=-r0, pattern=[[-1, 4]], channel_multiplier=1), reads=["cst"], writes=["cst"])
            p.op("pool", lambda e: e.memset(maskS[:], 0.0), writes=["maskS"])
            mS4 = maskS.rearrange("p (g h t) -> p g h t", g=2, h=8)
            for g in range(2):
                for t_ in range(4):
                    cb = g * 4 + t_
                    v = mS4[:, g, 0:4, t_]
                    p.op("pool", lambda e, v=v: e.memset(v, 1.0), reads=["maskS"], writes=["maskS"])
                    p.op("pool", lambda e, v=v, cb=cb: e.affine_select(out=v, in_=v, compare_op=ALU.is_ge, fill=0.0, base=-cb * 16, pattern=[[0, 4]], channel_multiplier=1),
                         reads=["maskS"], writes=["maskS"])
                    p.op("pool", lambda e, v=v, cb=cb: e.affine_select(out=v, in_=v, compare_op=ALU.is_ge, fill=0.0, base=cb * 16 + 14, pattern=[[0, 4]], channel_multiplier=-1),
                         reads=["maskS"], writes=["maskS"])
            p.op("pool", lambda e: e.memset(maskW[:], 1.0), writes=["maskW"])
            p.op("pool", lambda e: e.affine_select(out=maskW.rearrange("p (a t) -> p a t", t=4), in_=maskW.rearrange("p (a t) -> p a t", t=4), compare_op=ALU.is_ge, fill=0.0,
                                                   base=0, pattern=[[0, 16], [-1, 4]], channel_multiplier=1), reads=["maskW"], writes=["maskW"])
            p.op("pool", lambda e: e.memset(maskN[:], 1.0), writes=["maskN"])
            for sq in range(4):
                v = maskN[:, sq, :].rearrange("p (a t) -> p a t", t=4)
                p.op("pool", lambda e, v=v, sq=sq: e.affine_select(out=v, in_=v, compare_op=ALU.is_ge, fill=0.0, base=-4 * sq, pattern=[[0, 16], [0, 4]], channel_multiplier=1),
                     reads=["maskN"], writes=["maskN"])
                p.op("pool", lambda e, v=v, sq=sq: e.affine_select(out=v, in_=v, compare_op=ALU.is_ge, fill=0.0, base=4 * sq + 3, pattern=[[0, 16], [0, 4]], channel_multiplier=-1),
                     reads=["maskN"], writes=["maskN"])
                p.op("pool", lambda e, v=v, sq=sq: e.affine_select(out=v, in_=v, compare_op=ALU.is_ge, fill=0.0, base=4 * sq, pattern=[[0, 16], [1, 4]], channel_multiplier=-1),
                     reads=["maskN"], writes=["maskN"])


        OATS = ["OAT.%d.%d" % (fc, NB) for fc in range(4)]
        evs = [0]

        def evac2(out_ap, in_ap, bank, writes, reads=()):
            eng = "act" if evs[0] % 2 == 0 else "dve"
            evs[0] += 1
            if eng == "act":
                p.op("act", lambda e: e.copy(out=out_ap, in_=in_ap), reads=[psr(bank)] + list(reads), writes=writes)
            else:
                p.op("dve", lambda e: e.tensor_copy(out=out_ap, in_=in_ap), reads=[psr(bank)] + list(reads), writes=writes)

        def branch_epilogue(bo, br, first):
            p.op("dve", lambda e: e.tensor_scalar_max(out=eps_[:, 0:1], in0=PS[bo][0:64, 128:129], scalar1=1e-30), reads=[psr(bo)], writes=["eps_"])
            p.op("dve", lambda e: e.reciprocal(out=eps_[:, 1:2], in_=eps_[:, 0:1]), reads=["eps_"], writes=["eps_"])
            p.op("dve", lambda e: e.tensor_tensor(out=eps_[:, 2:3], in0=eps_[:, 1:2], in1=gcol[:, br:br + 1], op=ALU.mult), reads=["eps_", "gcol"], writes=["eps_"])
            for g in range(2):
                rs_ = slice(g * 32, g * 32 + 32)
                if first:
                    p.op("dve", lambda e, rs_=rs_, g=g: e.tensor_scalar_mul(out=Osamp[rs_, :], in0=PS[bo][rs_, g * 64:(g + 1) * 64], scalar1=eps_[rs_, 2:3]),
                         reads=[psr(bo), "eps_"], writes=["Osamp"])
                else:
                    p.op("dve", lambda e, rs_=rs_, g=g: e.scalar_tensor_tensor(out=Osamp[rs_, :], in0=PS[bo][rs_, g * 64:(g + 1) * 64], scalar=eps_[rs_, 2:3],
                                                                               in1=Osamp[rs_, :], op0=ALU.mult, op1=ALU.add),
                         reads=[psr(bo), "eps_", "Osamp"], writes=["Osamp"])

        def new_keys(bo, slot, Vn, vnn, pidx, sq):
            bk = gs6.next()
            p.op("pe", lambda e, bk=bk: e.matmul(PS[bk][0:NS, 0:64], KTs[:, slot, :], Qz[:, :], start=True, stop=True), reads=["KTs", "Qz"], writes=[psr(bk)])
            Pn = PTn[pidx]
            p.op("act", lambda e, bk=bk, Pn=Pn: e.activation(out=Pn[:, :], in_=PS[bk][0:NS, 0:64], func=AF.Exp), reads=[psr(bk)], writes=["PTn%d" % pidx])
            p.op("dve", lambda e, Pn=Pn: e.tensor_tensor(out=Pn[:, :], in0=Pn[:, :], in1=maskN[:, sq, :], op=ALU.mult), reads=["PTn%d" % pidx, "maskN"], writes=["PTn%d" % pidx])
            p.op("pe", lambda e, Pn=Pn: e.matmul(PS[bo][0:64, 0:128], Pn[:, :], Vn[:, :], start=False, stop=True, skip_group_check=True),
                 reads=["PTn%d" % pidx, vnn], writes=[psr(bo)])
            p.op("pe", lambda e, Pn=Pn: e.matmul(PS[bo][0:64, 128:129], Pn[:, :], ones_b[0:NS, 0:1], start=False, stop=True, skip_group_check=True),
                 reads=["PTn%d" % pidx, "ones_b"], writes=[psr(bo)])

        def sec_pti(sq):
            p.dma("sp", lambda e, sq=sq: e.dma_start(out=pti[:, :], in_=pt[sq * 128:(sq + 1) * 128, :]), writes=["pti"])
        gissued = {}
        pti_done = set()

        def issue_gather(sq, pi_, rc):
            if (sq, pi_, rc) in gissued:
                return gissued[(sq, pi_, rc)]
            pool_ = (pk_c, pv_c)[pi_]
            G = Gt[gtc[0] % NG]
            gn = "Gt%d" % (gtc[0] % NG)
            gtc[0] += 1
            p.dma("pool", lambda e, G=G, pool_=pool_, rc=rc: e.indirect_dma_start(
                out=G[:, :], out_offset=None, in_=pool_, in_offset=bass.IndirectOffsetOnAxis(ap=pti[:, :], axis=0), element_offset=rc * 2048),
                reads=["pti"], writes=[gn])
            gissued[(sq, pi_, rc)] = (G, gn)
            return G, gn

        def prefetch(sq):
            if sq not in pti_done:
                pti_done.add(sq)
                sec_pti(sq)
            for rc in range(NG):
                issue_gather(sq, 0, rc)

        def sec_comp(sq):
            if sq not in pti_done:
                pti_done.add(sq)
                sec_pti(sq)
            for pi_, pool_ in enumerate((pk_c, pv_c)):
                for rc in range(8):
                    G, gn = issue_gather(sq, pi_, rc)
                    for hf in range(2):
                        bk = gs6.next()
                        pst = PS[bk].bitcast(BF16)
                        for r in range(8):
                            rr = hf * 8 + r
                            p.op("pe", lambda e, pst=pst, r=r, rr=rr, G=G: e.transpose(pst[:, r * 128:(r + 1) * 128], G[:, rr * 128:(rr + 1) * 128], idb[:]),
                                 reads=[gn, "idb"], writes=[psr(bk)])
                        evac2(KTr[:, rc, hf * 8:(hf + 1) * 8, :], pst[:, :].rearrange("p (j q) -> p j q", j=8), bk, ["BGB"])
                bA = gs6.next()
                bB = gs6.next()
                for j in range(32):
                    if j < 16:
                        p.op("pe", lambda e, j=j, pi_=pi_, bA=bA: e.matmul(PS[bA][:, :].rearrange("p (c q) -> p c q", c=4), W1bd[pi_][:, j, :], KTr[:, 0:4, j, :],
                                                                             start=(j == 0), stop=False, skip_group_check=True), reads=["BGB", "W1bd%d" % pi_], writes=[psr(bA)])
                        p.op("pe", lambda e, j=j, pi_=pi_, bB=bB: e.matmul(PS[bB][:, :].rearrange("p (c q) -> p c q", c=4), W1bd[pi_][:, j, :], KTr[:, 4:8, j, :],
                                                                             start=(j == 0), stop=False, skip_group_check=True), reads=["BGB", "W1bd%d" % pi_], writes=[psr(bB)])
                    else:
                        jj = j - 16
                        p.op("pe", lambda e, j=j, jj=jj, pi_=pi_, bA=bA: e.matmul(PS[bA][:, :].rearrange("p (c q) -> p c q", c=4), W1bd[pi_][:, j, :], KTr[:, 1:5, jj, :],
                                                                                    start=False, stop=(j == 31), skip_group_check=True), reads=["BGB", "W1bd%d" % pi_], writes=[psr(bA)])
                        p.op("pe", lambda e, j=j, jj=jj, pi_=pi_, bB=bB: e.matmul(PS[bB][:, 0:384].rearrange("p (c q) -> p c q", c=3), W1bd[pi_][:, j, :], KTr[:, 5:8, jj, :],
                                                                                    start=False, stop=False, skip_group_check=True), reads=["BGB", "W1bd%d" % pi_], writes=[psr(bB)])
                        p.op("pe", lambda e, j=j, jj=jj, pi_=pi_, bB=bB: e.matmul(PS[bB][:, 384:511], W1bd[pi_][:, j, :], KTr[:, 0, jj, 1:128],
                                                                                    start=False, stop=(j == 31), skip_group_check=True), reads=["BGB", "W1bd%d" % pi_], writes=[psr(bB)])
                p.op("act", lambda e, pi_=pi_, bA=bA: e.activation(out=hs_s[:, 0:512], in_=PS[bA][:, :], func=AF.Silu, bias=H0[pi_][:, 0:1]),
                     reads=[psr(bA), "H0_%d" % pi_], writes=["hs_s.a"])
                p.op("act", lambda e, pi_=pi_, bB=bB: e.activation(out=hs_s[:, 512:1024], in_=PS[bB][:, :], func=AF.Silu, bias=H0[pi_][:, 0:1]),
                     reads=[psr(bB), "H0_%d" % pi_], writes=["hs_s.b"])
                if pi_ == 0:
                    for hf in range(2):
                        bk = gs6.next()
                        p.op("pe", lambda e, hf=hf, bk=bk: e.matmul(PS[bk][:, :], W2bd[0][:, :], hs_s[:, hf * 512:(hf + 1) * 512], start=True, stop=True),
                             reads=["W2bd0", "hs_s.a", "hs_s.b"], writes=[psr(bk)])
                        evac2(kcT_s[:, hf * 512:(hf + 1) * 512], PS[bk][:, :], bk, ["kcT_s.%d" % hf])
                else:
                    for hf in range(2):
                        bk = gs6.next()
                        for c4 in range(4):
                            cc = hf * 4 + c4
                            p.op("pe", lambda e, cc=cc, c4=c4, bk=bk: e.matmul(PS[bk][:, c4 * 128:(c4 + 1) * 128], hs_s[:, cc * 128:(cc + 1) * 128], W2bd[1][:, :],
                                                                                 start=(c4 == 0), stop=True, skip_group_check=True),
                                 reads=["W2bd1", "hs_s.a", "hs_s.b"], writes=[psr(bk)])
                        evac2(vc_s[:, hf * 4:(hf + 1) * 4, :], PS[bk][:, :].rearrange("p (c q) -> p c q", c=4), bk, ["vc_s.%d" % hf])

        def sec_cmp(sq):
            p.op("pool", lambda e: e.memset(Qz[:], 0.0), writes=["Qz"])
            p.op("dve", lambda e, sq=sq: e.tensor_copy(out=Qz[0:64, 0:16].rearrange("p (h t) -> p h t", h=4), in_=QTs[0:64, :, 4 * sq:4 * sq + 4]),
                 reads=["QTs", "Qz"], writes=["Qz"])
            p.op("dve", lambda e, sq=sq: e.tensor_copy(out=Qz[64:128, 32:48].rearrange("p (h t) -> p h t", h=4), in_=QTs[64:128, :, 4 * sq:4 * sq + 4]),
                 reads=["QTs", "Qz"], writes=["Qz"])
            p.op("pool", lambda e: e.memset(gcol[:], 0.0), writes=["gcol"])
            for g in range(2):
                for hq in range(4):
                    r0 = g * 32 + hq * 4
                    h = g * 4 + hq
                    p.dma("sp", lambda e, r0=r0, h=h, sq=sq: e.dma_start(out=gcol[r0:r0 + 4, 0:3], in_=gates_s[4 * sq:4 * sq + 4, h * 3:h * 3 + 3]),
                          reads=["gates_s", "gcol"], writes=["gcol"])
            bk = gs6.next()
            for cc in range(8):
                p.op("pe", lambda e, cc=cc, bk=bk: e.matmul(PS[bk][:, cc * 64:(cc + 1) * 64], kcT_s[:, cc * 128:(cc + 1) * 128], Qz[:, :],
                                                             start=(cc == 0), stop=True, skip_group_check=True),
                     reads=["kcT_s.%d" % (cc // 4), "Qz"], writes=[psr(bk)])
            p.op("act", lambda e, bk=bk: e.activation(out=PTc[:].rearrange("p c q -> p (c q)"), in_=PS[bk][:, :], func=AF.Exp), reads=[psr(bk)], writes=["PTc"])
            p.op("dve", lambda e: e.tensor_scalar_mul(out=PTc[:, 7, :], in0=PTc[:, 7, :], scalar1=pmask[:, 0:1]), reads=["PTc", "pmask"], writes=["PTc"])
            bo = sO.next()
            for cc in range(8):
                p.op("pe", lambda e, cc=cc, bo=bo: e.matmul(PS[bo][0:64, 0:128], PTc[:, cc, :], vc_s[:, cc, :], start=(cc == 0), stop=(cc == 7), skip_group_check=True),
                     reads=["PTc", "vc_s.%d" % (cc // 4)], writes=[psr(bo)])
                p.op("pe", lambda e, cc=cc, bo=bo: e.matmul(PS[bo][0:64, 128:129], PTc[:, cc, :], ones_b[:, 0:1], start=False, stop=(cc == 7), skip_group_check=True),
                     reads=["PTc", "ones_b"], writes=[psr(bo)])
                p.op("pe", lambda e, cc=cc, bo=bo: e.matmul(PS[bo][0:64, 129:386], PTc[:, cc, :], aggS[:, cc, :], start=False, stop=(cc == 7), skip_group_check=True),
                     reads=["PTc", "aggS"], writes=[psr(bo)])
            branch_epilogue(bo, 0, True)
            p.op("dve", lambda e, bo=bo: e.tensor_scalar_mul(out=imp_n[:, :], in0=PS[bo][0:64, 129:386], scalar1=eps_[:, 1:2]), reads=[psr(bo), "eps_"], writes=["imp_n"])
            bk = gs6.next()
            p.op("pe", lambda e, bk=bk: e.matmul(PS[bk][0:8, 0:257], Hsum[:, :], imp_n[:, :], start=True, stop=True), reads=["cst", "imp_n"], writes=[psr(bk)])
            p.op("dve", lambda e, bk=bk: e.tensor_copy(out=scs[:, :], in_=PS[bk][0:8, 0:257]), reads=[psr(bk)], writes=["scs"])
            p.op("dve", lambda e: e.memset(scs[:, 0:1], -1.0), reads=["scs"], writes=["scs"])
            p.op("dve", lambda e: e.memset(scs[:, 255:257], -1.0), reads=["scs"], writes=["scs"])
            p.op("dve", lambda e: e.max(out=m16s[:, 0:8], in_=scs[:, :]), reads=["scs"], writes=["m16s.a"])
            p.op("dve", lambda e: e.max_index(out=i16[:, 0:8], in_max=m16s[:, 0:8], in_values=scs[:, :]), reads=["scs", "m16s.a"], writes=["i16.a"])
            p.op("dve", lambda e: e.match_replace(out=scs2[:, :], in_to_replace=m16s[:, 0:8], in_values=scs[:, :], imm_value=-1e9), reads=["scs", "m16s.a", "imp_n"], writes=["imp_n"])
            p.op("dve", lambda e: e.max(out=m16s[:, 8:16], in_=scs2[:, :]), reads=["imp_n"], writes=["m16s.b"])
            p.op("dve", lambda e: e.max_index(out=i16[:, 8:16], in_max=m16s[:, 8:16], in_values=scs2[:, :]), reads=["imp_n", "m16s.b"], writes=["i16.b"])
            i16i = i16.bitcast(I32)
            p.op("dve", lambda e, i16i=i16i: e.memset(i16i[:, 13:14], 0), reads=["i16.a", "i16.b"], writes=["i16.c"])
            p.op("dve", lambda e, i16i=i16i: e.memset(i16i[:, 14:15], 255), reads=["i16.c"], writes=["i16.c"])
            p.op("dve", lambda e, i16i=i16i: e.memset(i16i[:, 15:16], 0), reads=["i16.c"], writes=["i16.c"])
            p.op("dve", lambda e, i16i=i16i: e.tensor_single_scalar(out=idxw[:, :, 0], in_=i16i[:, :], scalar=1, op=ALU.arith_shift_right),
                 reads=["i16.a", "i16.b", "i16.c"], writes=["idxw.a"])
            p.op("dve", lambda e, sq=sq: e.tensor_single_scalar(out=idxw[:, :, 0], in_=idxw[:, :, 0], scalar=sq * 128, op=ALU.add),
                 reads=["idxw.a"], writes=["idxw.a"])
            p.op("dve", lambda e, i16i=i16i: e.tensor_single_scalar(out=idxw[:, :, 1], in_=i16i[:, :], scalar=1, op=ALU.bitwise_and),
                 reads=["i16.a", "i16.b", "i16.c"], writes=["idxw.b"])
            p.dma("sp", lambda e, sq=sq: e.dma_start(out=scr_idx[sq].rearrange("(r k) c -> r (k c)", k=16), in_=idxw[:].rearrange("p k c -> p (k c)")),
                  reads=["idxw.a", "idxw.b"], writes=["scr_idx"])
            p.dma("sp", lambda e, sq=sq: e.dma_start(out=idxp[:, :], in_=scr_idx[sq]), reads=["scr_idx"], writes=["idxp"])
            p.dma("pool", lambda e: e.indirect_dma_start(out=pgid[:, :], out_offset=None, in_=pt, in_offset=bass.IndirectOffsetOnAxis(ap=idxp[:, 0:1], axis=0)),
                  reads=["idxp"], writes=["pgid"])
            p.op("dve", lambda e: e.scalar_tensor_tensor(out=hpx[:, :], in0=pgid[:, :], scalar=2, in1=idxp[:, 1:2], op0=ALU.mult, op1=ALU.add),
                 reads=["pgid", "idxp"], writes=["hpx"])
        def sec_gath(sq):
            for hf in range(2):
                p.dma("pool", lambda e, hf=hf: e.indirect_dma_start(out=wbuf[hf][:].rearrange("p a b -> p (a b)"), out_offset=None, in_=pk_s,
                                                                     in_offset=bass.IndirectOffsetOnAxis(ap=hpx[:, :], axis=0), element_offset=hf * 4096),
                      reads=["hpx"], writes=["wbuf%d" % hf])
            p.dma("pool", lambda e: e.indirect_dma_start(out=VGb[:, :], out_offset=None, in_=pv_s, in_offset=bass.IndirectOffsetOnAxis(ap=hpx[:, :], axis=0)),
                  reads=["hpx"], writes=["VGb"])

        def sec_sel(sq):
            for k8 in range(8):
                bk = gs6.next()
                pst = PS[bk].bitcast(BF16)
                for r in range(8):
                    k = k8 * 8 + r
                    kw_ = wbuf[k // 32][:].rearrange("p a b -> p (a b)")
                    p.op("pe", lambda e, pst=pst, r=r, k=k, kw_=kw_: e.transpose(pst[:, r * 128:(r + 1) * 128], kw_[:, (k % 32) * 128:(k % 32 + 1) * 128], idb[:]),
                         reads=["wbuf%d" % (k // 32), "idb"], writes=[psr(bk)])
                evac2(KsT[:, k8 * 8:(k8 + 1) * 8, :], pst[:, :].rearrange("p (j q) -> p j q", j=8), bk, ["KsT.%d" % k8])
            bo = sO.next()
            for k8 in range(8):
                bk = gs6.next()
                for r in range(8):
                    k = k8 * 8 + r
                    p.op("pe", lambda e, r=r, k=k, bk=bk: e.matmul(PS[bk][:, r * 64:(r + 1) * 64], KsT[:, k, :], Qz[:, :], start=(r == 0), stop=True, skip_group_check=True),
                         reads=["KsT.%d" % k8, "Qz"], writes=[psr(bk)])
                P_ = PTs[k8 % 2]
                pn = "PTs%d" % (k8 % 2)
                p.op("act", lambda e, bk=bk, P_=P_: e.activation(out=P_[:].rearrange("p c q -> p (c q)"), in_=PS[bk][:, :], func=AF.Exp), reads=[psr(bk)], writes=[pn])
                p.op("dve", lambda e, P_=P_: e.tensor_tensor(out=P_[:], in0=P_[:], in1=maskS[:, :].unsqueeze(1).to_broadcast([128, 8, 64]), op=ALU.mult),
                     reads=[pn, "maskS"], writes=[pn])
                for r in range(8):
                    k = k8 * 8 + r
                    first = (k == 0)
                    p.op("pe", lambda e, r=r, k=k, bo=bo, P_=P_, first=first: e.matmul(PS[bo][0:64, 0:128], P_[:, r, :], VGb[:, k * 128:(k + 1) * 128],
                                                                                         start=first, stop=False, skip_group_check=True), reads=[pn, "VGb"], writes=[psr(bo)])
                    p.op("pe", lambda e, r=r, bo=bo, P_=P_: e.matmul(PS[bo][0:64, 128:129], P_[:, r, :], ones_b[:, 0:1], start=False, stop=False, skip_group_check=True),
                         reads=[pn, "ones_b"], writes=[psr(bo)])

            new_keys(bo, 2, VNs, "VNs", 0, sq)
            branch_epilogue(bo, 1, False)

        def sec_win(sq):
            p.dma("pool", lambda e, sq=sq: e.dma_start(out=Kwn[:], in_=ckw[sq].rearrange("(i r) c -> r i c", r=128)), writes=["Kwn"])
            p.dma("pool", lambda e, sq=sq: e.dma_start(out=Vw_s[:], in_=cvw[sq].rearrange("(i r) c -> r i c", r=128)), writes=["Vw_s"])
            bk = gs6.next()
            pst = PS[bk].bitcast(BF16)
            for i in range(4):
                p.op("pe", lambda e, i=i, pst=pst: e.transpose(pst[:, i * 128:(i + 1) * 128], Kwn[:, i, :], idb[:]), reads=["Kwn", "idb"], writes=[psr(bk)])
            evac2(KwT[:], pst[:, 0:512].rearrange("p (j q) -> p j q", j=4), bk, ["KwT"])
            bk = gs6.next()
            for i in range(4):
                p.op("pe", lambda e, i=i, bk=bk: e.matmul(PS[bk][:, i * 64:(i + 1) * 64], KwT[:, i, :], Qz[:, :], start=(i == 0), stop=True, skip_group_check=True),
                     reads=["KwT", "Qz"], writes=[psr(bk)])
            p.op("act", lambda e, bk=bk: e.activation(out=PTw[:].rearrange("p c q -> p (c q)"), in_=PS[bk][:, 0:256], func=AF.Exp), reads=[psr(bk)], writes=["PTw"])
            p.op("dve", lambda e: e.tensor_tensor(out=PTw[:, 0, :], in0=PTw[:, 0, :], in1=maskW[:, :], op=ALU.mult), reads=["PTw", "maskW"], writes=["PTw"])
            bo = sO.next()
            for i in range(4):
                p.op("pe", lambda e, i=i, bo=bo: e.matmul(PS[bo][0:64, 0:128], PTw[:, i, :], Vw_s[:, i, :], start=(i == 0), stop=False, skip_group_check=True),
                     reads=["PTw", "Vw_s"], writes=[psr(bo)])
                p.op("pe", lambda e, i=i, bo=bo: e.matmul(PS[bo][0:64, 128:129], PTw[:, i, :], ones_b[:, 0:1], start=False, stop=False, skip_group_check=True),
                     reads=["PTw", "ones_b"], writes=[psr(bo)])
            new_keys(bo, 3, VNw, "VNw", 1, sq)
            branch_epilogue(bo, 2, False)

        def sec_place(sq):
            p.op("dve", lambda e: e.tensor_copy(out=OW[:, 0:64], in_=Osamp[:, :]), reads=["Osamp"], writes=["OW"])
            p.op("dve", lambda e: e.tensor_copy(out=OW[:, 64:128], in_=Osamp[:, :]), reads=["Osamp", "OW"], writes=["OW"])
            b1 = gs6.next()
            b2 = gs6.next()
            p.op("pe", lambda e, b1=b1: e.matmul(PS[b1][0:64, 0:16], OW[:, 0:64], SelE[:, :], start=True, stop=True), reads=["OW", "cst"], writes=[psr(b1)])
            p.op("pe", lambda e, b2=b2: e.matmul(PS[b2][:, 0:16], OW[:, :], SelO[:, :], start=True, stop=True), reads=["OW", "cst"], writes=[psr(b2)])
            p.op("dve", lambda e, b1=b1, sq=sq: e.tensor_tensor(out=OAT[0:64, :, T + 4 * sq:T + 4 * sq + 4], in0=PS[b1][0:64, 0:16].rearrange("p (f t) -> p f t", f=4),
                                                                 in1=SGAs[0:64, :, 4 * sq:4 * sq + 4], op=ALU.mult), reads=[psr(b1), "SGAs"] + OATS, writes=OATS)
            p.op("dve", lambda e, b2=b2, sq=sq: e.tensor_tensor(out=OAT[64:128, :, T + 4 * sq:T + 4 * sq + 4], in0=PS[b2][64:128, 0:16].rearrange("p (f t) -> p f t", f=4),
                                                                 in1=SGAs[64:128, :, 4 * sq:4 * sq + 4], op=ALU.mult), reads=[psr(b2), "SGAs"] + OATS, writes=OATS)
        sec_comp(0)
        build_sample_consts()
        prefetch(1)
        sec_cmp(0)
        sec_win(0)
        for sq in range(4):
            if sq + 1 < 4:
                sec_comp(sq + 1)
            sec_gath(sq)
            sec_sel(sq)
            sec_place(sq)
            if sq + 1 < 4:
                if sq + 2 < 4:
                    prefetch(sq + 2)
                sec_cmp(sq + 1)
                sec_win(sq + 1)
    if stop_after <= 3:
        dbg = dout("dbg", [128, 4 * T], BF16)
        p.dma("sp", lambda e: e.dma_start(out=dbg.rearrange("p (c t) -> p c t", c=4), in_=OAT[:, :, 0:T]),
              reads=["OAT.%d.%d" % (fc, b) for fc in range(4) for b in range(NB)], is_output=True)
        return nc, p, locals()


    p.arena_reset()
    CBT = p.ar("CBT", [128, 4, TA], BF16)
    keep45 = p.ar_off
    UT = p.ar("UT", [128, 4, 30 + T], BF16)
    DG = p.ar("DG", [128, 4, 31, 128], BF16)
    accA2 = [[p.ar("accA%d_%d" % (j, i), [128, 512], F32) for i in range(4)] for j in range(2)]
    accA = accA2[0]
    xb16 = p.ar("xb16", [128, 4, 512], BF16)
    xsq16 = p.ar("xsq16", [128, 4, 512], BF16)
    mean = p.ar("mean", [128, 512], F32)
    rstd = p.ar("rstd", [128, 512], F32)
    msq = p.ar("msq", [128, 512], F32)
    sgt = [p.ar("sgt%d" % i, [128, 512], F32) for i in range(2)]
    YS = p.ar("YS", [128, 4, 512], BF16)
    PW = p.ar("PW", [128, 4, 512], BF16)
    vecn = p.ar("vecn", [35, 512], F32)
    vecT = p.ar("vecT", [128, 4, 35], F32)
    utok = p.ar("utok", [128, 512], F32)

    p.dma("sp", lambda e: e.dma_start(out=vecn[:, :], in_=vecs), writes=["vecn"])
    for ct in range(4):
        bk = gen.next()
        p.op("pe", lambda e, bk=bk, ct=ct: e.transpose(PS[bk][:, 0:35], vecn[:, ct * 128:(ct + 1) * 128], idf[0:35, 0:35]),
             reads=["vecn", "idf"], writes=[psr(bk)])
        p.op("dve", lambda e, bk=bk, ct=ct: e.tensor_copy(out=vecT[:, ct, :], in_=PS[bk][:, 0:35]), reads=[psr(bk)], writes=["vecT"])
    p.dma("pool", lambda e: e.dma_start(out=PW[:], in_=pw_w.rearrange("(kt p) c -> p kt c", p=128)), writes=["PW"])
    for ct in range(4):
        for w in range(31):
            if w % 2 == 0:
                p.op("dve", lambda e, ct=ct, w=w: e.tensor_scalar(out=DG[:, ct, w, :], in0=idb[:, :], scalar1=vecT[:, ct, w:w + 1], scalar2=None, op0=ALU.mult),
                     reads=["idb", "vecT"], writes=["DG.%d" % ct])
            else:
                p.op("act", lambda e, ct=ct, w=w: e.activation(out=DG[:, ct, w, :], in_=idb[:, :], func=AF.Copy, scale=vecT[:, ct, w:w + 1]),
                     reads=["idb", "vecT"], writes=["DG.%d" % ct])
    p.op("pool", lambda e: e.memset(UT[:, :, 0:30], 0.0), writes=["UT.h"])

    UTs = p.ar("UTs", [128, 4, 4, 34], F32)
    sct = p.ar("sct", [120, 512], F32)
    utok_s = p.ar("utok_s", [NS, 512], F32)
    p.dma("sp", lambda e: e.dma_start(out=sct[:, :], in_=sconv.rearrange("s r c -> (s r) c")), writes=["sct"])
    for ct in range(4):
        bk = gen.next()
        p.op("pe", lambda e, bk=bk, ct=ct: e.transpose(PS[bk][:, 0:120], sct[:, ct * 128:(ct + 1) * 128], idf[0:120, 0:120]),
             reads=["sct", "idf"], writes=[psr(bk)])
        p.op("dve", lambda e, bk=bk, ct=ct: e.tensor_copy(out=UTs[:, ct, :, 0:30], in_=PS[bk][:, 0:120].rearrange("p (s r) -> p s r", s=4)),
             reads=[psr(bk)], writes=["UTs.h%d" % ct])
    for sq in range(4):
        p.dma("sp", lambda e, sq=sq: e.dma_start(out=s_conv[sq, 0:26, :], in_=sconv[sq, 4:30, :]), is_output=True)

    wv, wvn = load_w(C_GLU)
    wg, wgn = load_w(C_GLU + 512)
    sgc = [0]
    for ct in range(4):
        for blk in range(NBA):
            n = bn(blk)
            bk = gen.next()
            proj_fm(wg, wgn, ct, blk, bk)
            k = sgc[0] % 2
            sgc[0] += 1
            p.op("act", lambda e, bk=bk, k=k, n=n: e.activation(out=sgt[k][:, 0:n], in_=PS[bk][:, 0:n], func=AF.Sigmoid), reads=[psr(bk)], writes=["sgt%d" % k])
            bk2 = gen.next()
            proj_fm(wv, wvn, ct, blk, bk2)
            if blk < NB:
                p.op("dve", lambda e, bk2=bk2, k=k, ct=ct, blk=blk: e.tensor_tensor(out=UT[:, ct, 30 + blk * 512:30 + (blk + 1) * 512], in0=PS[bk2][:, :],
                                                                                     in1=sgt[k][:], op=ALU.mult),
                     reads=[psr(bk2), "sgt%d" % k], writes=["UT.%d.%d" % (ct, blk)])
            else:
                p.op("dve", lambda e, bk2=bk2, k=k, ct=ct: e.tensor_tensor(out=UTs[:, ct, :, 30:34], in0=PS[bk2][:, 0:NS].rearrange("p (s t) -> p s t", s=4),
                                                                            in1=sgt[k][:, 0:NS].rearrange("p (s t) -> p s t", s=4), op=ALU.mult),
                     reads=[psr(bk2), "sgt%d" % k, "UTs.h%d" % ct], writes=["UTs.n%d" % ct])
    for (lo, hi, m, ut_, un, rd) in ((T - 128, T, 128, utok, "utok", "xnT.%d" % (NT - 1)), (T, TA, NS, utok_s, "utok_s", "xnT.s")):
        b1 = gen.next()
        b2 = gen.next()
        for kt in range(8):
            p.op("pe", lambda e, b1=b1, kt=kt, lo=lo, hi=hi, m=m: e.matmul(PS[b1][0:m, :], xnT[:, kt, lo:hi], wv[:, kt, :], start=(kt == 0), stop=(kt == 7)),
                 reads=[wvn, rd], writes=[psr(b1)])
        for kt in range(8):
            p.op("pe", lambda e, b2=b2, kt=kt, lo=lo, hi=hi, m=m: e.matmul(PS[b2][0:m, :], xnT[:, kt, lo:hi], wg[:, kt, :], start=(kt == 0), stop=(kt == 7)),
                 reads=[wgn, rd], writes=[psr(b2)])
        p.op("act", lambda e, b2=b2, m=m, ut_=ut_: e.activation(out=ut_[0:m, :], in_=PS[b2][0:m, :], func=AF.Sigmoid), reads=[psr(b2)], writes=[un])
        p.op("dve", lambda e, b1=b1, m=m, ut_=ut_: e.tensor_tensor(out=ut_[0:m, :], in0=PS[b1][0:m, :], in1=ut_[0:m, :], op=ALU.mult), reads=[psr(b1), un], writes=[un])
    p.dma("sp", lambda e: e.dma_start(out=o_conv, in_=utok[98:128, :]), reads=["utok"], is_output=True)
    for sq in range(4):
        p.dma("sp", lambda e, sq=sq: e.dma_start(out=s_conv[sq, 26:30, :], in_=utok_s[4 * sq:4 * sq + 4, :]), reads=["utok_s"], is_output=True)

    wgb, wgbn = load_w(C_GB)
    NTA = 20

    def ln_pw(blk, acc, an):
        n = bn(blk)
        for ct in range(4):
            p.op("act", lambda e, ct=ct: e.copy(out=xb16[:, ct, 0:n], in_=acc[ct][:, 0:n]), reads=[an % ct], writes=["xb16.%d" % ct])
            p.op("act", lambda e, ct=ct: e.activation(out=xsq16[:, ct, 0:n], in_=acc[ct][:, 0:n], func=AF.Square), reads=[an % ct], writes=["xsq16.%d" % ct])
        b1 = gen.next()
        b2 = gen.next()
        for ct in range(4):
            p.op("pe", lambda e, ct=ct: e.matmul(PS[b1][:, 0:n], ones_b[:, :], xb16[:, ct, 0:n], start=(ct == 0), stop=(ct == 3)),
                 reads=["ones_b", "xb16.%d" % ct], writes=[psr(b1)])
        for ct in range(4):
            p.op("pe", lambda e, ct=ct: e.matmul(PS[b2][:, 0:n], ones_b[:, :], xsq16[:, ct, 0:n], start=(ct == 0), stop=(ct == 3)),
                 reads=["ones_b", "xsq16.%d" % ct], writes=[psr(b2)])
        p.op("dve", lambda e: e.tensor_scalar_mul(out=mean[:, 0:n], in0=PS[b1][:, 0:n], scalar1=1.0 / 512), reads=[psr(b1)], writes=["mean"])
        p.op("dve", lambda e: e.tensor_tensor(out=msq[:, 0:n], in0=mean[:, 0:n], in1=mean[:, 0:n], op=ALU.mult), reads=["mean"], writes=["msq"])
        p.op("dve", lambda e: e.scalar_tensor_tensor(out=rstd[:, 0:n], in0=PS[b2][:, 0:n], scalar=1.0 / 512, in1=msq[:, 0:n], op0=ALU.mult, op1=ALU.subtract),
             reads=[psr(b2), "msq"], writes=["rstd"])
        p.op("dve", lambda e: e.tensor_scalar_add(out=rstd[:, 0:n], in0=rstd[:, 0:n], scalar1=EPS), reads=["rstd"], writes=["rstd"])
        p.op("act", lambda e: e.activation(out=rstd[:, 0:n], in_=rstd[:, 0:n], func=AF.Sqrt), reads=["rstd"], writes=["rstd"])
        p.op("dve", lambda e: e.reciprocal(out=rstd[:, 0:n], in_=rstd[:, 0:n]), reads=["rstd"], writes=["rstd"])
        for ct in range(4):
            p.op("dve", lambda e, ct=ct: e.tensor_tensor(out=acc[ct][:, 0:n], in0=acc[ct][:, 0:n], in1=mean[:, 0:n], op=ALU.subtract),
                 reads=[an % ct, "mean"], writes=[an % ct])
            p.op("pool", lambda e, ct=ct: e.tensor_tensor(out=acc[ct][:, 0:n], in0=acc[ct][:, 0:n], in1=rstd[:, 0:n], op=ALU.mult),
                 reads=[an % ct, "rstd"], writes=[an % ct])
            p.op("act", lambda e, ct=ct: e.activation(out=YS[:, ct, 0:n], in_=acc[ct][:, 0:n], func=AF.Silu, scale=vecT[:, ct, 32:33], bias=vecT[:, ct, 33:34]),
                 reads=[an % ct, "vecT"], writes=["YS.%d" % ct])
        for co in range(4):
            bk = gen.next()
            proj_fm(wgb, wgbn, co, blk, bk)
            k = sgc[0] % 2
            sgc[0] += 1
            p.op("act", lambda e, bk=bk, k=k: e.activation(out=sgt[k][:, 0:n], in_=PS[bk][:, 0:n], func=AF.Silu), reads=[psr(bk)], writes=["sgt%d" % k])
            bk2 = gen.next()
            for ci in range(4):
                p.op("pe", lambda e, bk2=bk2, ci=ci, co=co: e.matmul(PS[bk2][:, 0:n], PW[:, ci, co * 128:(co + 1) * 128], YS[:, ci, 0:n], start=(ci == 0), stop=(ci == 3)),
                     reads=["PW", "YS.%d" % ci], writes=[psr(bk2)])
            p.op("dve", lambda e, bk2=bk2, k=k, co=co: e.scalar_tensor_tensor(out=CBT[:, co, bsl(blk)], in0=PS[bk2][:, 0:n],
                                                                               scalar=vecT[:, co, 34:35], in1=sgt[k][:, 0:n], op0=ALU.add, op1=ALU.mult),
                 reads=[psr(bk2), "sgt%d" % k, "vecT"], writes=["CBT.%d.%d" % (co, blk)])

    def conv_blk(blk, acc, an):
        def u_sl(ct, w):
            return UT[:, ct, blk * 512 + w: blk * 512 + w + 512]

        def u_reads(ct):
            r = ["UT.%d.%d" % (ct, blk)]
            r.append("UT.%d.%d" % (ct, blk - 1) if blk > 0 else "UT.h")
            return r
        for ct in range(4):
            bk = gen.next()
            for w in range(31):
                uw_ = u_sl(ct, w)
                p.op("pe", lambda e, ct=ct, w=w, uw_=uw_, bk=bk: e.matmul(PS[bk][:, :], DG[:, ct, w, :], uw_, start=(w == 0), stop=(w == 30)),
                     reads=u_reads(ct) + ["DG.%d" % ct], writes=[psr(bk)])
            if ct % 2 == 0:
                p.op("dve", lambda e, ct=ct, bk=bk: e.tensor_scalar(out=acc[ct][:], in0=PS[bk][:, :], scalar1=vecT[:, ct, 31:32], scalar2=None, op0=ALU.add),
                     reads=[psr(bk), "vecT"], writes=[an % ct])
            else:
                p.op("act", lambda e, ct=ct, bk=bk: e.activation(out=acc[ct][:], in_=PS[bk][:, :], func=AF.Identity, bias=vecT[:, ct, 31:32]),
                     reads=[psr(bk), "vecT"], writes=[an % ct])

    accS = [p.ar("accS%d" % i, [128, NS], F32) for i in range(4)]
    for w in range(31):
        for ct in range(4):
            uw_ = UTs[:, ct, :, w:w + 4]
            av = accS[ct][:, 0:NS].rearrange("p (s t) -> p s t", s=4)
            rd = ["UTs.h%d" % ct, "UTs.n%d" % ct, "vecT"]
            if w == 0:
                p.op("dve", lambda e, ct=ct, uw_=uw_, av=av: e.tensor_scalar(out=av, in0=uw_, scalar1=vecT[:, ct, 0:1], scalar2=vecT[:, ct, 31:32],
                                                                             op0=ALU.mult, op1=ALU.add), reads=rd, writes=["accS%d" % ct])
            else:
                p.op("dve", lambda e, ct=ct, w=w, uw_=uw_, av=av: e.scalar_tensor_tensor(out=av, in0=uw_, scalar=vecT[:, ct, w:w + 1], in1=av,
                                                                                          op0=ALU.mult, op1=ALU.add), reads=rd + ["accS%d" % ct], writes=["accS%d" % ct])

    ANS = ["accA0_%d", "accA1_%d"]
    conv_blk(0, accA2[0], ANS[0])
    for blk in range(NB):
        if blk + 1 < NB:
            conv_blk(blk + 1, accA2[(blk + 1) % 2], ANS[(blk + 1) % 2])
        ln_pw(blk, accA2[blk % 2], ANS[blk % 2])

    ln_pw(NB, accS, "accS%d")

    if stop_after <= 4:
        dbg4 = dout("dbg4", [128, 4 * 512], BF16)
        p.dma("sp", lambda e: e.dma_start(out=dbg4, in_=YS[:].rearrange("p c t -> p (c t)")), reads=["YS.%d" % c_ for c_ in range(4)], is_output=True)
        dbg5 = dout("dbg5", [128, 3 * 512], F32)
        p.dma("sp", lambda e: e.dma_start(out=dbg5[:, 0:512], in_=mean[:]), reads=["mean"], is_output=True)
        p.dma("sp", lambda e: e.dma_start(out=dbg5[:, 512:1024], in_=rstd[:]), reads=["rstd"], is_output=True)
        p.dma("sp", lambda e: e.dma_start(out=dbg5[:, 1024:1536], in_=accA[0][:]), reads=["accA0_0"], is_output=True)
        dbg = dout("dbg", [128, 4 * T], BF16)
        p.dma("sp", lambda e: e.dma_start(out=dbg.rearrange("p (c t) -> p c t", c=4), in_=CBT[:, :, 0:T]),
              reads=["CBT.%d.%d" % (fc, b) for fc in range(4) for b in range(NB)], is_output=True)
        return nc, p, locals()

    p.arena_reset(keep=keep45)
    WPA = p.ar("WPA", [128, 4, D], BF16)
    WPB = p.ar("WPB", [128, 4, D], BF16)
    HT = p.ar("HT", [128, 8, TA], BF16)
    WO = p.ar("WO", [128, 8, D], BF16)
    fing_b = p.ar("fing_b", [128, D], F32)
    smt = [p.ar("smt%d" % i, [128, 512], F32) for i in range(4)]
    xr = [p.ar("xr%d" % i, [128, D], F32) for i in range(2)]
    yo = [p.ar("yo%d" % i, [128, D], F32) for i in range(2)]
    st2 = p.sbuf("st2", [128, NT + 1, 4], F32)
    p.dma("pool", lambda e: e.dma_start(out=WPA[:], in_=w_pa.rearrange("(kt p) c -> p kt c", p=128)), writes=["WPA"])
    p.dma("pool", lambda e: e.dma_start(out=WPB[:], in_=w_pb.rearrange("(kt p) c -> p kt c", p=128)), writes=["WPB"])
    for half in range(2):
        p.dma("pool", lambda e, half=half: e.dma_start(out=WO[:, half * 4:(half + 1) * 4, :],
                                                        in_=w_o.rearrange("(kt p) c -> p kt c", p=128)[:, half * 4:(half + 1) * 4, :]), writes=["WO"])
    p.dma("sp", lambda e: e.dma_start(out=fing_b[:], in_=final_g.broadcast_to([128, D])), writes=["fing_b"])
    for half in range(2):
        wma, wman = load_w(C_MA + half * 512)
        wmb, wmbn = load_w(C_MB + half * 512)
        for blk in range(NBA):
            n = bn(blk)
            for ii in range(4):
                i = half * 4 + ii
                bka = gen.next()
                proj_fm(wma, wman, ii, blk, bka)
                p.op("act", lambda e, bka=bka, n=n: e.activation(out=smt[0][:, 0:n], in_=PS[bka][:, 0:n], func=AF.Sigmoid), reads=[psr(bka)], writes=["smt0"])
                bkb = gen.next()
                proj_fm(wmb, wmbn, ii, blk, bkb)
                p.op("act", lambda e, bkb=bkb, n=n: e.activation(out=smt[1][:, 0:n], in_=PS[bkb][:, 0:n], func=AF.Sigmoid), reads=[psr(bkb)], writes=["smt1"])
                bk1 = gen.next()
                for f_ in range(4):
                    p.op("pe", lambda e, f_=f_, i=i, bk1=bk1, blk=blk, n=n: e.matmul(PS[bk1][:, 0:n], WPA[:, f_, i * 128:(i + 1) * 128], OAT[:, f_, bsl(blk)],
                                                                                      start=(f_ == 0), stop=(f_ == 3)),
                         reads=["WPA", "OAT.%d.%d" % (f_, blk)], writes=[psr(bk1)])
                p.op("dve", lambda e, bk1=bk1, n=n: e.tensor_tensor(out=smt[2][:, 0:n], in0=PS[bk1][:, 0:n], in1=smt[0][:, 0:n], op=ALU.mult),
                     reads=[psr(bk1), "smt0"], writes=["smt2"])
                bk2 = gen.next()
                for f_ in range(4):
                    p.op("pe", lambda e, f_=f_, i=i, bk2=bk2, blk=blk, n=n: e.matmul(PS[bk2][:, 0:n], WPB[:, f_, i * 128:(i + 1) * 128], CBT[:, f_, bsl(blk)],
                                                                                      start=(f_ == 0), stop=(f_ == 3)),
                         reads=["WPB", "CBT.%d.%d" % (f_, blk)], writes=[psr(bk2)])
                p.op("dve", lambda e, bk2=bk2, n=n: e.tensor_tensor(out=smt[3][:, 0:n], in0=PS[bk2][:, 0:n], in1=smt[1][:, 0:n], op=ALU.mult),
                     reads=[psr(bk2), "smt1"], writes=["smt3"])
                p.op("pool", lambda e, i=i, blk=blk, n=n: e.tensor_tensor(out=HT[:, i, bsl(blk)], in0=smt[2][:, 0:n], in1=smt[3][:, 0:n], op=ALU.add),
                     reads=["smt2", "smt3"], writes=["HT.%d.%d" % (i, blk)])

    for t in range(NT + 1):
        xr_, yo_ = xr[t % 2], yo[t % 2]
        xrn, yon = "xr%d" % (t % 2), "yo%d" % (t % 2)
        if t < NT:
            m, lo, hi, src, dst, hblk = 128, t * 128, (t + 1) * 128, x_p[t * 128:(t + 1) * 128, :], y_p[t * 128:(t + 1) * 128, :], t // 4
        else:
            m, lo, hi, src, dst, hblk = NS, T, TA, x_s, y_s, NB
        p.dma("sp", lambda e, xr_=xr_, m=m, src=src: e.dma_start(out=xr_[0:m, :], in_=src), writes=[xrn])
        for half in range(2):
            bk = gen.next()
            for kt in range(8):
                p.op("pe", lambda e, kt=kt, half=half, bk=bk, m=m, lo=lo, hi=hi: e.matmul(PS[bk][0:m, :], HT[:, kt, lo:hi], WO[:, kt, half * 512:(half + 1) * 512],
                                                                                           start=(kt == 0), stop=(kt == 7)),
                     reads=["WO", "HT.%d.%d" % (kt, hblk)], writes=[psr(bk)])
            p.op("dve", lambda e, half=half, bk=bk, xr_=xr_, m=m: e.tensor_tensor(out=xr_[0:m, half * 512:(half + 1) * 512], in0=PS[bk][0:m, :],
                                                                                  in1=xr_[0:m, half * 512:(half + 1) * 512], op=ALU.add),
                 reads=[psr(bk), xrn], writes=[xrn])
        tag = "st2_%d" % t
        rms_stats(xr_[0:m, :], m, st2[0:m, t, 0:1], st2[0:m, t, 1:2], st2[0:m, t, 2:3], [xrn], tag)
        p.op("dve", lambda e, t=t, xr_=xr_, yo_=yo_, m=m: e.scalar_tensor_tensor(out=yo_[0:m, :], in0=xr_[0:m, :], scalar=st2[0:m, t, 2:3], in1=fing_b[0:m, :],
                                                                                 op0=ALU.mult, op1=ALU.mult),
             reads=[xrn, tag + "c", "fing_b"], writes=[yon])
        p.dma("sp", lambda e, yo_=yo_, m=m, dst=dst: e.dma_start(out=dst, in_=yo_[0:m, :]), reads=[yon], is_output=True)

    return nc, p, locals()


_CACHE = {}


def kernel(x_prompt, x_sample, cache_k_cmp, cache_v_cmp, cache_k_slc, cache_v_slc,
           cache_k_win, cache_v_win, state_conv, page_table,
           ln_g, w_in, pe_k, w1_k, w2_k, pe_v, w1_v, w2_v,
           dw_k, dw_b, cln_g, cln_b, pw_w, pw_b, w_pa, w_pb, w_o, final_g, _stop=99, _sample=True):
    f = lambda a: np.ascontiguousarray(np.asarray(a, dtype=np.float32))
    import os
    if os.environ.get("KDEV_NOSAMPLE"):
        _sample = False
    nc, p, env = build_program(stop_after=_stop, do_sample=_sample)
    if "fin" not in env:
        p.finish()
    vecs = np.concatenate([f(dw_k)[0], f(dw_b), f(cln_g), f(cln_b), f(pw_b)], axis=0)
    shared = {
        "w_in": f(w_in)[0], "ln_g": f(ln_g), "final_g": f(final_g).reshape(1, D),
        "w1_k": f(w1_k)[0], "w1_v": f(w1_v)[0], "w2_k": f(w2_k)[0], "w2_v": f(w2_v)[0],
        "pe_k": f(pe_k)[0], "pe_v": f(pe_v)[0], "vecs": vecs, "pw_w": f(pw_w)[0],
        "w_pa": f(w_pa)[0], "w_pb": f(w_pb)[0], "w_o": f(w_o)[0],
    }
    in_maps = []
    xp = f(x_prompt)
    pools = [f(a)[0].reshape(5120, 16384) for a in (cache_k_cmp, cache_v_cmp, cache_k_slc, cache_v_slc)] if _sample else [None] * 4
    for c in range(8):
        m = dict(shared)
        m["x_p"] = xp[c]
        m["x_s"] = f(x_sample)[4 * c:4 * c + 4].reshape(NS, D)
        m["ckw"] = f(cache_k_win)[0, 4 * c:4 * c + 4].reshape(4, 512, 128)
        m["cvw"] = f(cache_v_win)[0, 4 * c:4 * c + 4].reshape(4, 512, 128)
        m["sconv"] = f(state_conv)[0, 4 * c:4 * c + 4]
        m["pt"] = np.ascontiguousarray(np.asarray(page_table, dtype=np.int32)[4 * c:4 * c + 4].reshape(512, 1))
        m["pk_c"] = pools[0]
        m["pv_c"] = pools[1]
        m["pk_s"] = pools[2]
        m["pv_s"] = pools[3]
        in_maps.append(m)
    names = set(env["in_names"])
    in_maps = [{k: v for k, v in m.items() if k in names} for m in in_maps]
    res = run_bass_kernel_spmd(nc, in_maps, core_ids=list(range(8)))
    R = res.results
    global _LAST
    _LAST = R

    def get(name, shape):
        if name in R[0]:
            return np.stack([np.asarray(R[c][name], dtype=np.float32).reshape(shape) for c in range(8)], 0)
        return np.zeros((8,) + tuple(shape), np.float32)

    y_prompt = get("y_p", (T, D))
    pk = [get("o_kv%d" % i, (T, 2, 64))[None] for i in range(4)]
    p_kw = get("o_kw", (512, 2, 64))[None]
    p_vw = get("o_vw", (512, 2, 64))[None]
    p_conv = get("o_conv", (30, 512))[None]
    def gets(name, shape):
        if name in R[0]:
            a = np.stack([np.asarray(R[c][name], dtype=np.float32).reshape((4,) + tuple(shape)) for c in range(8)], 0)
            return a.reshape((32,) + tuple(shape))
        return np.zeros((32,) + tuple(shape), np.float32)
    y_sample = gets("y_s", (4, D))
    sk = [gets("s_kv%d" % i, (4, 2, 64))[None] for i in range(4)]
    s_kw = gets("s_kw", (512, 2, 64))[None]
    s_vw = gets("s_vw", (512, 2, 64))[None]
    s_conv = gets("s_conv", (30, 512))[None]
    return (y_prompt, y_sample, pk[0], pk[1], pk[2], pk[3], p_kw, p_vw, p_conv,
            sk[0], sk[1], sk[2], sk[3], s_kw, s_vw, s_conv)
```

```python
import contextlib
import numpy as np
import concourse.bass as bass
import concourse.mybir as mybir
from concourse.bass_utils import run_bass_kernel_spmd

F32 = mybir.dt.float32
BF16 = mybir.dt.bfloat16
I32 = mybir.dt.int32
AF = mybir.ActivationFunctionType
ALU = mybir.AluOpType
AX = mybir.AxisListType

ENGS = ("pe", "act", "dve", "pool", "sp")
SEM_LIMIT = 30000
N_DMA_SEMS = 12


class Prog:
    def __init__(self, nc):
        self.nc = nc
        self.stack = contextlib.ExitStack()
        self.ops = {e: [] for e in ENGS}
        self.sems = {}
        self.res = {}
        self.seen = {e: {} for e in ENGS}
        self.cur = {}
        self.cnt = {}
        self.epoch = {e: 0 for e in ENGS}
        for e in ENGS:
            self._new_eng_sem(e)
        self.dma_pool = {}
        self.dma_rr = {}
        self.all_dma = []
        self.out_waits = []

    def _sem(self, name):
        h = self.stack.enter_context(self.nc.semaphore(name))
        self.sems[name] = h
        return name

    def _new_eng_sem(self, e):
        key = self._sem("s_%s_%d" % (e, self.epoch[e]))
        self.epoch[e] += 1
        self.cur[e] = key
        self.cnt[key] = 0

    def sbuf(self, name, shape, dt):
        return self.stack.enter_context(self.nc.sbuf_tensor(name, list(shape), dt))

    def psum(self, name, shape, dt):
        return self.stack.enter_context(self.nc.psum_tensor(name, list(shape), dt))

    def arena_init(self, nbytes):
        self.AR = self.sbuf("AR", [128, nbytes // 2], BF16)
        self.ar_off = 0
        self.ar_size = nbytes
        self.ar_peak = 0

    def ar(self, name, shape, dt):
        esz = 2 if dt == BF16 else 4
        n = esz
        for d in shape[1:]:
            n *= d
        n_al = (n + 63) // 64 * 64
        assert self.ar_off + n_al <= self.ar_size, ("arena overflow", name, self.ar_off, n_al, self.ar_size)
        v = self.AR[0:shape[0], self.ar_off // 2:(self.ar_off + n) // 2]
        self.ar_off += n_al
        self.ar_peak = max(self.ar_peak, self.ar_off)
        if esz == 4:
            v = v.bitcast(dt)
        if len(shape) == 3:
            v = v.rearrange("p (a b) -> p a b", a=shape[1])
        elif len(shape) == 4:
            v = v.rearrange("p (a b c) -> p a b c", a=shape[1], b=shape[2])
        return v

    def barrier(self):
        self.bar_snap = {k: v for k, v in self.cnt.items() if v > 0}
        self.bar_pending = set(ENGS)

    def arena_reset(self, keep=0):
        self.barrier()
        self.ar_off = keep

    def _need(self, eng, dep, waits):
        if dep is None:
            return
        key, val = dep
        if self.seen[eng].get(key, 0) >= val:
            return
        self.seen[eng][key] = val
        waits.append((key, val))

    def _deps(self, eng, reads, writes, is_dma):
        waits = []
        own = "s_%s_" % eng
        if getattr(self, "bar_pending", None) and eng in self.bar_pending:
            self.bar_pending.discard(eng)
            for k, v in self.bar_snap.items():
                if k.startswith(own):
                    continue
                self._need(eng, (k, v), waits)
        for r in reads:
            st = self.res.get(r)
            if st:
                self._need(eng, st[0], waits)
                if r.startswith("ps"):
                    for rd in st[1]:
                        if not rd[0].startswith(own):
                            self._need(eng, rd, waits)
        for w in writes:
            st = self.res.get(w)
            if st:
                if not (st[0] is not None and st[0][0].startswith(own) and not is_dma):
                    self._need(eng, st[0], waits)
                for rd in st[1]:
                    if rd[0].startswith(own) and not is_dma:
                        continue
                    self._need(eng, rd, waits)
        return waits

    def _mark(self, reads, writes, done):
        for r in reads:
            st = self.res.setdefault(r, [None, []])
            st[1].append(done)
            if len(st[1]) > 64:
                best = {}
                for k, v in st[1]:
                    best[k] = max(best.get(k, 0), v)
                st[1] = list(best.items())
        for w in writes:
            self.res[w] = [done, []]

    @staticmethod
    def _snap(fn):
        cl = fn.__closure__ or ()
        out = []
        for c in cl:
            try:
                out.append(id(c.cell_contents))
            except ValueError:
                out.append(None)
        return out

    def op(self, eng, fn, reads=(), writes=()):
        fn._snap = self._snap(fn)
        waits = self._deps(eng, reads, writes, False)
        key = self.cur[eng]
        self.cnt[key] += 1
        done = (key, self.cnt[key])
        self.ops[eng].append((waits, fn, key, 1))
        self._mark(reads, writes, done)
        if self.cnt[key] >= SEM_LIMIT:
            self._new_eng_sem(eng)
        return done

    def dma(self, eng, fn, reads=(), writes=(), is_output=False):
        fn._snap = self._snap(fn)
        waits = self._deps(eng, reads, writes, True)
        pool = self.dma_pool.setdefault(eng, [])
        if len(pool) < N_DMA_SEMS:
            key = self._sem("d_%s_%d" % (eng, len(pool)))
            pool.append(key)
            self.all_dma.append(key)
            self.cnt[key] = 0
            self.dma_rr[eng] = len(pool) % N_DMA_SEMS
        else:
            i = self.dma_rr[eng]
            key = pool[i]
            self.dma_rr[eng] = (i + 1) % N_DMA_SEMS
            if self.cnt[key] >= SEM_LIMIT:
                key = self._sem("d_%s_%d_%d" % (eng, i, len(self.sems)))
                pool[i] = key
                self.all_dma.append(key)
                self.cnt[key] = 0
        if self.cnt[key] > 0:
            self._need(eng, (key, self.cnt[key]), waits)
        self.cnt[key] += 16
        done = (key, self.cnt[key])
        self.ops[eng].append((waits, fn, key, 16))
        self._mark(reads, writes, done)
        if is_output:
            self.out_waits.append(done)
        return done

    def finish(self):
        fin = []
        best = {}
        for k, v in self.out_waits:
            best[k] = max(best.get(k, 0), v)
        for e in ENGS:
            for k in [kk for kk in self.cnt if kk.startswith("s_%s_" % e)]:
                if self.cnt[k] > 0:
                    best[k] = max(best.get(k, 0), self.cnt[k])
        for k in self.all_dma:
            if self.cnt[k] > 0:
                best[k] = max(best.get(k, 0), self.cnt[k])
        fin = list(best.items())
        nc = self.nc
        sems = self.sems
        ops = self.ops

        def emit(engobj, lst, final):
            for waits, fn, key, inc in lst:
                for (k, v) in waits:
                    engobj.wait_ge(sems[k], v)
                if fn._snap != self._snap(fn):
                    raise RuntimeError("late-bound closure variable changed: %s %s" % (fn.__code__.co_freevars, fn.__code__.co_firstlineno))
                inst = fn(engobj)
                inst.then_inc(sems[key], inc)
            if final:
                for (k, v) in fin:
                    engobj.wait_ge(sems[k], v)

        with nc.Block() as block:
            @block.sync
            def _(e):
                emit(e, ops["sp"], True)

            @block.tensor
            def _(e):
                emit(e, ops["pe"], False)

            @block.scalar
            def _(e):
                emit(e, ops["act"], False)

            @block.vector
            def _(e):
                emit(e, ops["dve"], False)

            @block.gpsimd
            def _(e):
                emit(e, ops["pool"], False)
        self.stack.close()


D = 1024
T = 2048
NT = T // 128
NB = T // 512
NS = 16
TA = T + NS
NBA = NB + 1
DIN = 5400
C_Q, C_KV, C_GN, C_GA, C_GLU, C_GB, C_MA, C_MB = 0, 512, 1280, 1304, 1816, 2840, 3352, 4376
BIG = 32768.0
EPS = 1e-6


class Banks:
    def __init__(self, ids):
        self.ids = list(ids)
        self.i = 0

    def next(self):
        b = self.ids[self.i % len(self.ids)]
        self.i += 1
        return b


def build_program(do_sample=True, stop_after=99):
    nc = bass.Bass("TRN2", target_bir_lowering=False)
    p = Prog(nc)

    in_names = []

    def din(name, shape, dt=F32):
        in_names.append(name)
        return nc.dram_tensor(name, list(shape), dt, kind="ExternalInput").ap()

    def dout(name, shape, dt=F32):
        return nc.dram_tensor(name, list(shape), dt, kind="ExternalOutput").ap()

    x_p = din("x_p", [T, D])
    w_in = din("w_in", [D, DIN])
    ln_g = din("ln_g", [1, D])
    final_g = din("final_g", [1, D])
    w1_k = din("w1_k", [2048, 64])
    w1_v = din("w1_v", [2048, 64])
    w2_k = din("w2_k", [64, 64])
    w2_v = din("w2_v", [64, 64])
    pe_k = din("pe_k", [32, 64])
    pe_v = din("pe_v", [32, 64])
    vecs = din("vecs", [35, 512])
    pw_w = din("pw_w", [512, 512])
    w_pa = din("w_pa", [512, D])
    w_pb = din("w_pb", [512, D])
    w_o = din("w_o", [D, D])

    x_s = din("x_s", [NS, D])
    ckw = din("ckw", [4, 512, 128])
    cvw = din("cvw", [4, 512, 128])
    sconv = din("sconv", [4, 30, 512])
    y_s = dout("y_s", [NS, D])
    s_kv = [dout("s_kv%d" % i, [NS, 128]) for i in range(4)]
    s_kw = dout("s_kw", [4, 512, 128])
    s_vw = dout("s_vw", [4, 512, 128])
    s_conv = dout("s_conv", [4, 30, 512])
    y_p = dout("y_p", [T, D])
    o_kv = [dout("o_kv%d" % i, [T, 128]) for i in range(4)]
    o_kw = dout("o_kw", [512, 128])
    o_vw = dout("o_vw", [512, 128])
    o_conv = dout("o_conv", [30, 512])

    p.arena_init(118 * 1024)

    PS = [p.psum("psb%d" % i, [128, 512], F32) for i in range(8)]

    def psr(i):
        return "ps%d" % i

    gen = Banks(range(8))

    idf = p.sbuf("idf", [128, 128], F32)
    idb = p.sbuf("idb", [128, 128], BF16)
    ones_b = p.sbuf("ones_b", [128, 128], BF16)
    p.op("pool", lambda e: e.memset(idf[:], 0.0), writes=["idf"])
    p.op("pool", lambda e: e.affine_select(out=idf[:], in_=idf[:], compare_op=ALU.not_equal, fill=1.0,
                                           base=0, pattern=[[-1, 128]], channel_multiplier=1),
         reads=["idf"], writes=["idf"])
    p.op("pool", lambda e: e.tensor_copy(out=idb[:], in_=idf[:]), reads=["idf"], writes=["idb"])
    p.op("pool", lambda e: e.memset(ones_b[:], 1.0), writes=["ones_b"])

    lng_b = p.sbuf("lng_b", [128, D], F32)
    p.dma("sp", lambda e: e.dma_start(out=lng_b[:], in_=ln_g.broadcast_to([128, D])), writes=["lng_b"])

    xnT = p.sbuf("xnT", [128, 8, TA], BF16)
    xt = [p.sbuf("xt%d" % i, [128, D], F32) for i in range(2)]
    xs = [p.sbuf("xs%d" % i, [128, D], BF16) for i in range(2)]
    sq_junk = p.sbuf("sq_junk", [128, D], BF16)
    st = p.sbuf("st", [128, NT, 4], F32)

    def rms_stats(src_ap, n, ss, tmp, rstd, reads, tag):
        p.op("act", lambda e: e.activation(out=sq_junk[0:n, :], in_=src_ap, func=AF.Square, accum_out=ss),
             reads=reads, writes=["sq_junk", tag + "a"])
        p.op("dve", lambda e: e.tensor_scalar(out=tmp, in0=ss, scalar1=1.0 / D, scalar2=EPS,
                                              op0=ALU.mult, op1=ALU.add), reads=[tag + "a"], writes=[tag + "b"])
        p.op("act", lambda e: e.activation(out=tmp, in_=tmp, func=AF.Sqrt), reads=[tag + "b"], writes=[tag + "b"])
        p.op("dve", lambda e: e.reciprocal(out=rstd, in_=tmp), reads=[tag + "b"], writes=[tag + "c"])

    for t in range(NT):
        xb_ = xt[t % 2]
        xs_ = xs[t % 2]
        rx, rs_ = "xt%d" % (t % 2), "xs%d" % (t % 2)
        p.dma("sp", lambda e, t=t, xb_=xb_: e.dma_start(out=xb_[:], in_=x_p[t * 128:(t + 1) * 128, :]), writes=[rx])
        tag = "st%d" % t
        rms_stats(xb_[:], 128, st[:, t, 0:1], st[:, t, 1:2], st[:, t, 2:3], [rx], tag)
        p.op("dve", lambda e, t=t, xb_=xb_, xs_=xs_: e.scalar_tensor_tensor(
            out=xs_[:], in0=xb_[:], scalar=st[:, t, 2:3], in1=lng_b[:], op0=ALU.mult, op1=ALU.mult),
            reads=[rx, tag + "c", "lng_b"], writes=[rs_])
        b = gen.next()
        pst = PS[b].bitcast(BF16)
        for kt in range(8):
            p.op("pe", lambda e, kt=kt, pst=pst, xs_=xs_: e.transpose(pst[:, kt * 128:(kt + 1) * 128],
                                                                     xs_[:, kt * 128:(kt + 1) * 128], idb[:]),
                 reads=[rs_, "idb"], writes=[psr(b)])
        eng = "act" if t % 2 == 0 else "dve"
        dst = xnT[:, :, t * 128:(t + 1) * 128]
        src = pst[:, :].rearrange("p (k t) -> p k t", k=8)
        if eng == "act":
            p.op("act", lambda e, dst=dst, src=src: e.copy(out=dst, in_=src), reads=[psr(b)], writes=["xnT.%d" % t])
        else:
            p.op("dve", lambda e, dst=dst, src=src: e.tensor_copy(out=dst, in_=src), reads=[psr(b)], writes=["xnT.%d" % t])

    st_s = p.sbuf("st_s", [128, 4], F32)
    p.dma("sp", lambda e: e.dma_start(out=xt[0][0:NS, :], in_=x_s), writes=["xt0"])
    rms_stats(xt[0][0:NS, :], NS, st_s[0:NS, 0:1], st_s[0:NS, 1:2], st_s[0:NS, 2:3], ["xt0"], "sts")
    p.op("dve", lambda e: e.scalar_tensor_tensor(out=xs[0][0:NS, :], in0=xt[0][0:NS, :], scalar=st_s[0:NS, 2:3], in1=lng_b[0:NS, :],
                                                 op0=ALU.mult, op1=ALU.mult), reads=["xt0", "stsc", "lng_b"], writes=["xs0"])
    b = gen.next()
    pst = PS[b].bitcast(BF16)
    for kt in range(8):
        p.op("pe", lambda e, kt=kt, pst=pst: e.transpose(pst[:, kt * 128:kt * 128 + NS], xs[0][0:NS, kt * 128:(kt + 1) * 128], idb[0:NS, 0:NS]),
             reads=["xs0", "idb"], writes=[psr(b)])
    p.op("dve", lambda e, pst=pst: e.tensor_copy(out=xnT[:, :, T:TA], in_=pst[:, :].rearrange("p (k t) -> p k t", k=8)[:, :, 0:NS]),
         reads=[psr(b)], writes=["xnT.s"])
    XN_ALL = ["xnT.%d" % t for t in range(NT)]
    if stop_after == 0:
        dbg = dout("dbg", [128, 8 * T], BF16)
        p.dma("sp", lambda e: e.dma_start(out=dbg, in_=xnT[:].rearrange("p k t -> p (k t)")), reads=XN_ALL, is_output=True)
        return nc, p, locals()

    def xn_blk(b):
        if b == NB:
            return ["xnT.s"]
        return ["xnT.%d" % t for t in range(4 * b, 4 * b + 4)]

    def bn(b):
        return NS if b == NB else 512

    def bsl(b):
        return slice(T, TA) if b == NB else slice(b * 512, (b + 1) * 512)

    NWB = 2
    wbuf = [p.sbuf("wbuf%d" % i, [128, 8, 512], BF16) for i in range(NWB)]
    wctr = [0]
    w_in_v = w_in.rearrange("(kt p) c -> p kt c", p=128)

    def load_w(c0, ncols=512, qperm=False):
        i = wctr[0] % NWB
        wctr[0] += 1
        wb = wbuf[i]
        name = "wbuf%d" % i
        if qperm:
            for h in range(4):
                for g in range(2):
                    srcv = w_in_v[:, :, g * 256 + h * 64: g * 256 + h * 64 + 64]
                    dstv = wb[:, :, h * 128 + g * 64: h * 128 + g * 64 + 64]
                    p.dma("pool", lambda e, srcv=srcv, dstv=dstv: e.dma_start(out=dstv, in_=srcv), writes=[name])
        else:
            for half in range(2):
                ks = slice(half * 4, half * 4 + 4)
                p.dma("pool", lambda e, ks=ks: e.dma_start(out=wb[:, ks, 0:ncols], in_=w_in_v[:, ks, c0:c0 + ncols]),
                      writes=[name])
        return wb, name

    def proj_fm(wb, wname, cc, blk, bank):
        for kt in range(8):
            p.op("pe", lambda e, kt=kt: e.matmul(PS[bank][:, 0:bn(blk)], wb[:, kt, cc * 128:(cc + 1) * 128],
                                                  xnT[:, kt, bsl(blk)],
                                                  start=(kt == 0), stop=(kt == 7)),
                 reads=[wname] + xn_blk(blk), writes=[psr(bank)])

    evac_rr = [0]

    def evac(out_ap, bank, writes, func=None, scale=1.0, eng=None, in_ap=None, extra_reads=(), n=512):
        src = PS[bank][:, 0:n] if in_ap is None else in_ap
        if func is not None:
            eng = "act"
        if eng is None:
            eng = "act" if evac_rr[0] % 2 == 0 else "dve"
            evac_rr[0] += 1
        rd = [psr(bank)] + list(extra_reads)
        if eng == "act":
            f = AF.Copy if func is None else func
            p.op("act", lambda e: e.activation(out=out_ap, in_=src, func=f, scale=scale), reads=rd, writes=writes)
        else:
            if scale == 1.0:
                p.op("dve", lambda e: e.tensor_copy(out=out_ap, in_=src), reads=rd, writes=writes)
            else:
                p.op("dve", lambda e: e.tensor_scalar_mul(out=out_ap, in0=src, scalar1=scale), reads=rd, writes=writes)

    W1bd = [p.ar("W1bd%d" % i, [128, 32, 128], BF16) for i in range(2)]
    kvss = p.ar("kvss", [NS, 768], F32)
    gates_s = p.ar("gates_s", [NS, 24], F32)
    VNs = p.ar("VNs", [NS, 128], BF16)
    VNw = p.ar("VNw", [NS, 128], BF16)
    QTs = p.ar("QTs", [128, 4, NS], BF16)
    KTs = p.ar("KTs", [128, 4, NS], BF16)
    keep_s = p.ar_off
    QT = p.ar("QT", [128, 4, TA], BF16)
    KT = p.ar("KT", [128, 4, TA], BF16)
    SGA = p.ar("SGA", [128, 4, TA], BF16)
    VS = p.ar("VS", [128, NT, 2, 65], BF16)
    VW = p.ar("VW", [128, NT, 2, 65], BF16)
    gates = p.ar("gates", [128, NT, 24], F32)
    kvst = [p.ar("kvst%d" % i, [128, 768], F32) for i in range(1)]
    p.op("pool", lambda e: e.memset(VS[:], 1.0), writes=["VS.ones"])
    p.op("pool", lambda e: e.memset(VW[:], 1.0), writes=["VW.ones"])

    wb, wn = load_w(0, qperm=True)
    for cc in range(4):
        for blk in range(NBA):
            bk = gen.next()
            proj_fm(wb, wn, cc, blk, bk)
            evac(QT[:, cc, bsl(blk)], bk, ["QT.%d.%d" % (cc, blk)], scale=0.125, n=bn(blk))
    if stop_after <= 0.5:
        return nc, p, locals()
    wkv1, wkv1n = load_w(C_KV)
    wkv2, wkv2n = load_w(C_KV + 512, 280)
    Cm = p.ar("Cm", [128, 512], BF16)
    Lm = p.ar("Lm", [128, 512], BF16)
    Em = p.ar("Em", [128, 16, 128], BF16)
    p.op("pool", lambda e: e.memset(Cm[:], 0.0), writes=["Cm"])
    p.op("pool", lambda e: e.affine_select(out=Cm[:], in_=Cm[:], compare_op=ALU.is_ge, fill=-BIG, base=0,
                                           pattern=[[1, 512]], channel_multiplier=-1), reads=["Cm"], writes=["Cm"])
    p.op("pool", lambda e: e.memset(Lm[:], 0.0), writes=["Lm"])
    p.op("pool", lambda e: e.affine_select(out=Lm[:], in_=Lm[:], compare_op=ALU.is_ge, fill=-BIG, base=384,
                                           pattern=[[-1, 512]], channel_multiplier=1), reads=["Lm"], writes=["Lm"])
    p.op("pool", lambda e: e.memset(Em[:], 0.0), writes=["Em"])
    p.op("pool", lambda e: e.affine_select(out=Em[0:32].rearrange("p t (a b) -> p t a b", a=2), in_=Em[0:32].rearrange("p t (a b) -> p t a b", a=2),
                                           compare_op=ALU.not_equal, fill=BIG, base=0,
                                           pattern=[[-2, 16], [-1, 2], [0, 64]], channel_multiplier=1), reads=["Em"], writes=["Em"])
    Asc = p.ar("Asc", [128, 8, 32], F32)
    Bsc = p.ar("Bsc", [128, 8, 32], F32)
    p.op("pool", lambda e: e.memset(Asc[:], 0.0), writes=["Asc"])
    p.op("pool", lambda e: e.memset(Bsc[:], 0.0), writes=["Bsc"])
    for t in range(8, 16):
        for half in range(2):
            cur = 2 * t + half
            ps_ = slice(half * 64, half * 64 + 64)
            p.op("pool", lambda e, t=t, ps_=ps_, cur=cur: e.memset(Asc[ps_, t - 8, 1:cur - 1], 1.0), reads=["Asc"], writes=["Asc"])
            p.op("pool", lambda e, t=t, ps_=ps_: e.memset(Bsc[ps_, t - 8, 0:1], 1e4), reads=["Bsc"], writes=["Bsc"])
            p.op("pool", lambda e, t=t, ps_=ps_, cur=cur: e.memset(Bsc[ps_, t - 8, cur - 1:cur + 1], 1e4), reads=["Bsc"], writes=["Bsc"])
            if cur + 1 < 32:
                p.op("pool", lambda e, t=t, ps_=ps_, cur=cur: e.memset(Bsc[ps_, t - 8, cur + 1:32], -1.0), reads=["Bsc"], writes=["Bsc"])

    for (wbx, wnx, cc, slot) in ((wkv1, wkv1n, 0, 0), (wkv1, wkv1n, 1, 1), (wkv1, wkv1n, 2, 2), (wkv2, wkv2n, 0, 3)):
        for blk in range(NBA):
            bk = gen.next()
            proj_fm(wbx, wnx, cc, blk, bk)
            evac(KT[:, slot, bsl(blk)], bk, ["KT.%d.%d" % (slot, blk)], n=bn(blk))
    if stop_after <= 0.6:
        return nc, p, locals()
    for t in range(NT):
        b1 = gen.next()
        b2 = gen.next()
        for kt in range(8):
            p.op("pe", lambda e, b1=b1, kt=kt, t=t: e.matmul(PS[b1][:, :], xnT[:, kt, t * 128:(t + 1) * 128], wkv1[:, kt, 0:512],
                                                      start=(kt == 0), stop=(kt == 7)),
                 reads=[wkv1n, "xnT.%d" % t], writes=[psr(b1)])
        for kt in range(8):
            p.op("pe", lambda e, b2=b2, kt=kt, t=t: e.matmul(PS[b2][:, 0:280], xnT[:, kt, t * 128:(t + 1) * 128], wkv2[:, kt, 0:280],
                                                      start=(kt == 0), stop=(kt == 7)),
                 reads=[wkv2n, "xnT.%d" % t], writes=[psr(b2)])
        ks_ = kvst[0]
        kn = "kvst0"
        p.op("dve", lambda e, b1=b1, ks_=ks_: e.tensor_copy(out=ks_[:, 0:512], in_=PS[b1][:, :]), reads=[psr(b1)], writes=[kn + "a"])
        p.op("act", lambda e, b2=b2, ks_=ks_: e.copy(out=ks_[:, 512:768], in_=PS[b2][:, 0:256]), reads=[psr(b2)], writes=[kn + "b"])
        p.op("act", lambda e, b2=b2, t=t: e.activation(out=gates[:, t, :], in_=PS[b2][:, 256:280], func=AF.Sigmoid),
             reads=[psr(b2)], writes=["gates.%d" % t])
        p.op("dve", lambda e, b1=b1, t=t: e.tensor_copy(out=VS[:, t, :, 0:64], in_=PS[b1][:, 384:512].rearrange("p (g d) -> p g d", g=2)),
             reads=[psr(b1), "VS.ones"], writes=["VS.%d" % t])
        p.op("dve", lambda e, b2=b2, t=t: e.tensor_copy(out=VW[:, t, :, 0:64], in_=PS[b2][:, 128:256].rearrange("p (g d) -> p g d", g=2)),
             reads=[psr(b2), "VW.ones"], writes=["VW.%d" % t])
        for i in range(4):
            p.dma("sp", lambda e, i=i, t=t, ks_=ks_: e.dma_start(out=o_kv[i][t * 128:(t + 1) * 128, :], in_=ks_[:, i * 128:(i + 1) * 128]),
                  reads=[kn + "a"], is_output=True)
        if t >= NT - 4:
            r0 = (t - (NT - 4)) * 128
            p.dma("sp", lambda e, r0=r0, ks_=ks_: e.dma_start(out=o_kw[r0:r0 + 128, :], in_=ks_[:, 512:640]), reads=[kn + "b"], is_output=True)
            p.dma("sp", lambda e, r0=r0, ks_=ks_: e.dma_start(out=o_vw[r0:r0 + 128, :], in_=ks_[:, 640:768]), reads=[kn + "b"], is_output=True)
    b1 = gen.next()
    b2 = gen.next()
    for kt in range(8):
        p.op("pe", lambda e, b1=b1, kt=kt: e.matmul(PS[b1][0:NS, :], xnT[:, kt, T:TA], wkv1[:, kt, 0:512], start=(kt == 0), stop=(kt == 7)),
             reads=[wkv1n, "xnT.s"], writes=[psr(b1)])
    for kt in range(8):
        p.op("pe", lambda e, b2=b2, kt=kt: e.matmul(PS[b2][0:NS, 0:280], xnT[:, kt, T:TA], wkv2[:, kt, 0:280], start=(kt == 0), stop=(kt == 7)),
             reads=[wkv2n, "xnT.s"], writes=[psr(b2)])
    p.op("dve", lambda e, b1=b1: e.tensor_copy(out=kvss[:, 0:512], in_=PS[b1][0:NS, :]), reads=[psr(b1)], writes=["kvss.a"])
    p.op("dve", lambda e, b1=b1: e.tensor_copy(out=VNs[:, :], in_=PS[b1][0:NS, 384:512]), reads=[psr(b1)], writes=["VNs"])
    p.op("act", lambda e, b2=b2: e.copy(out=kvss[:, 512:768], in_=PS[b2][0:NS, 0:256]), reads=[psr(b2)], writes=["kvss.b"])
    p.op("act", lambda e, b2=b2: e.copy(out=VNw[:, :], in_=PS[b2][0:NS, 128:256]), reads=[psr(b2)], writes=["VNw"])
    p.op("act", lambda e, b2=b2: e.activation(out=gates_s[:, :], in_=PS[b2][0:NS, 256:280], func=AF.Sigmoid), reads=[psr(b2)], writes=["gates_s"])
    for i in range(4):
        p.dma("sp", lambda e, i=i: e.dma_start(out=s_kv[i][:, :], in_=kvss[:, i * 128:(i + 1) * 128]), reads=["kvss.a"], is_output=True)
    for sq in range(4):
        p.dma("sp", lambda e, sq=sq: e.dma_start(out=s_kw[sq, 0:508, :], in_=ckw[sq, 4:512, :]), is_output=True)
        p.dma("sp", lambda e, sq=sq: e.dma_start(out=s_vw[sq, 0:508, :], in_=cvw[sq, 4:512, :]), is_output=True)
        p.dma("sp", lambda e, sq=sq: e.dma_start(out=s_kw[sq, 508:512, :], in_=kvss[4 * sq:4 * sq + 4, 512:640]), reads=["kvss.b"], is_output=True)
        p.dma("sp", lambda e, sq=sq: e.dma_start(out=s_vw[sq, 508:512, :], in_=kvss[4 * sq:4 * sq + 4, 640:768]), reads=["kvss.b"], is_output=True)
    if stop_after <= 0.7:
        return nc, p, locals()
    wb, wn = load_w(C_GA)
    for cc in range(4):
        for blk in range(NBA):
            bk = gen.next()
            proj_fm(wb, wn, cc, blk, bk)
            evac(SGA[:, cc, bsl(blk)], bk, ["SGA.%d.%d" % (cc, blk)], func=AF.Silu, n=bn(blk))


    if stop_after <= 1:
        return nc, p, locals()

    W2bd = [p.sbuf("W2bd%d" % i, [128, 128], BF16) for i in range(2)]
    PET = [p.sbuf("PET%d" % i, [128, 32], BF16) for i in range(2)]
    H0 = [p.sbuf("H0_%d" % i, [128, 1], F32) for i in range(2)]
    pen = p.sbuf("pen", [32, 128], F32)
    for i, (w1, w2, pe) in enumerate(((w1_k, w2_k, pe_k), (w1_v, w2_v, pe_v))):
        p.op("pool", lambda e, i=i: e.memset(W1bd[i][:], 0.0), writes=["W1bd%d" % i])
        p.op("pool", lambda e, i=i: e.memset(W2bd[i][:], 0.0), writes=["W2bd%d" % i])
        w1v = w1.rearrange("(j d) h -> d j h", d=64)
        for g in range(2):
            p.dma("pool", lambda e, i=i, g=g, w1v=w1v: e.dma_start(out=W1bd[i][g * 64:(g + 1) * 64, :, g * 64:(g + 1) * 64], in_=w1v),
                  writes=["W1bd%d" % i])
            p.dma("pool", lambda e, i=i, g=g, w2=w2: e.dma_start(out=W2bd[i][g * 64:(g + 1) * 64, g * 64:(g + 1) * 64], in_=w2),
                  writes=["W2bd%d" % i])
            p.dma("sp", lambda e, g=g, pe=pe: e.dma_start(out=pen[:, g * 64:(g + 1) * 64], in_=pe), writes=["pen"])
        bk = gen.next()
        p.op("pe", lambda e, bk=bk: e.transpose(PS[bk][:, 0:32], pen[:, :], idf[0:32, 0:32]), reads=["pen", "idf"], writes=[psr(bk)])
        p.op("dve", lambda e, bk=bk, i=i: e.tensor_copy(out=PET[i][:], in_=PS[bk][:, 0:32]), reads=[psr(bk)], writes=["PET%d" % i])
        bk = gen.next()
        for j in range(32):
            p.op("pe", lambda e, bk=bk, i=i, j=j: e.matmul(PS[bk][:, 0:1], W1bd[i][:, j, :], PET[i][:, j:j + 1], start=(j == 0), stop=(j == 31)),
                 reads=["W1bd%d" % i, "PET%d" % i], writes=[psr(bk)])
        p.op("dve", lambda e, bk=bk, i=i: e.tensor_copy(out=H0[i][:], in_=PS[bk][:, 0:1]), reads=[psr(bk)], writes=["H0_%d" % i])

    kcmpT = p.sbuf("kcmpT", [128, 128], BF16)
    vaug = p.sbuf("vaug", [128, 2, 97], BF16)
    hs = p.sbuf("hs", [128, 128], BF16)
    aggf = p.sbuf("aggf", [128, 32], F32)
    p.op("pool", lambda e: e.memset(aggf[:], 1.0), writes=["aggf"])
    p.op("pool", lambda e: e.affine_select(out=aggf[:], in_=aggf[:], compare_op=ALU.is_ge, fill=0.0, base=1,
                                           pattern=[[-4, 32]], channel_multiplier=1), reads=["aggf"], writes=["aggf"])
    p.op("pool", lambda e: e.affine_select(out=aggf[:], in_=aggf[:], compare_op=ALU.is_ge, fill=0.0, base=3,
                                           pattern=[[4, 32]], channel_multiplier=-1), reads=["aggf"], writes=["aggf"])
    p.op("pool", lambda e: e.memset(vaug[:], 1.0), writes=["vaug"])
    for g in range(2):
        p.op("pool", lambda e, g=g: e.tensor_copy(out=vaug[:, g, 65:97], in_=aggf[:]), reads=["aggf", "vaug"], writes=["vaug"])

    KT_ALL = lambda slot: ["KT.%d.%d" % (slot, b) for b in range(NB)]
    NCMP = 127
    for i in range(2):
        bk = gen.next()
        srcv = KT[:, i, :].rearrange("p (c j) -> p j c", j=16)
        for j in range(32):
            r, jj = j // 16, j % 16
            p.op("pe", lambda e, bk=bk, i=i, j=j, r=r, jj=jj, srcv=srcv: e.matmul(
                PS[bk][:, 0:NCMP], W1bd[i][:, j, :], srcv[:, jj, r:r + NCMP], start=(j == 0), stop=(j == 31)),
                reads=["W1bd%d" % i] + KT_ALL(i), writes=[psr(bk)])
        p.op("act", lambda e, bk=bk, i=i: e.activation(out=hs[:, 0:NCMP], in_=PS[bk][:, 0:NCMP], func=AF.Silu, bias=H0[i][:, 0:1]),
             reads=[psr(bk), "H0_%d" % i], writes=["hs"])
        bk2 = gen.next()
        if i == 0:
            p.op("pe", lambda e, bk2=bk2: e.matmul(PS[bk2][:, 0:NCMP], W2bd[0][:, :], hs[:, 0:NCMP], start=True, stop=True),
                 reads=["W2bd0", "hs"], writes=[psr(bk2)])
            p.op("dve", lambda e, bk2=bk2: e.tensor_copy(out=kcmpT[:, 0:NCMP], in_=PS[bk2][:, 0:NCMP]), reads=[psr(bk2)], writes=["kcmpT"])
        else:
            p.op("pe", lambda e, bk2=bk2: e.matmul(PS[bk2][0:NCMP, 0:128], hs[:, 0:NCMP], W2bd[1][:, :], start=True, stop=True),
                 reads=["W2bd1", "hs"], writes=[psr(bk2)])
            p.op("dve", lambda e, bk2=bk2: e.tensor_copy(out=vaug[0:NCMP, :, 0:64], in_=PS[bk2][0:NCMP, 0:128].rearrange("p (g d) -> p g d", g=2)),
                 reads=[psr(bk2), "vaug"], writes=["vaug"])

    if stop_after <= 2:
        dbg = dout("dbg", [128, 128], BF16)
        dbg2 = dout("dbg2", [128, 194], BF16)
        p.dma("sp", lambda e: e.dma_start(out=dbg, in_=kcmpT[:]), reads=["kcmpT"], is_output=True)
        p.dma("sp", lambda e: e.dma_start(out=dbg2, in_=vaug[:].rearrange("p g c -> p (g c)")), reads=["vaug"], is_output=True)
        return nc, p, locals()

    mcmp = p.ar("mcmp", [128, 512], BF16)
    PTb = [p.ar("PT%d" % i, [128, 512], BF16) for i in range(4)]
    Oacc = p.ar("Oacc", [128, 4, 512], F32)
    Obf = p.ar("Obf", [128, 4, 512], BF16)
    imp = p.ar("imp", [128, 4, 2, 32], F32)
    MT = p.ar("MT", [128, 2, 512], BF16)
    QZ = p.ar("QZ", [128, 8, 512], BF16)
    p.op("pool", lambda e: e.memset(MT[:], 0.0), writes=["MT.0", "MT.1"])
    p.op("pool", lambda e: e.memset(QZ[:], 0.0), writes=["QZ.%d" % h_ for h_ in range(8)])
    OAT = p.sbuf("OAT", [128, 4, TA], BF16)
    ep = [p.sbuf("ep%d" % i, [128, 16], F32) for i in range(2)]
    eptmp = [p.sbuf("eptmp%d" % i, [128, 4, 64], F32) for i in range(2)]
    sc = p.sbuf("sc", [128, 32], F32)
    scr = p.sbuf("scr", [128, 32], F32)
    m16 = p.sbuf("m16", [128, 16], F32)
    msel = p.sbuf("msel", [128, 32], BF16)
    sb_S = Banks([0, 1, 4])
    sb_O = Banks([2, 3])
    gen2 = Banks([5, 6, 7])
    ptc = [0]
    epc = [0]

    for b in range(NB):
        qs = slice(b * 512, (b + 1) * 512)
        gate_r = ["gates.%d" % t for t in range(4 * b, 4 * b + 4)]
        p.op("pool", lambda e: e.memset(mcmp[:], 0.0), writes=["mcmp"])
        p.op("pool", lambda e, b=b: e.affine_select(out=mcmp[:], in_=mcmp[:], compare_op=ALU.is_ge, fill=-BIG, base=512 * b - 31,
                                                    pattern=[[1, 512]], channel_multiplier=-16), reads=["mcmp"], writes=["mcmp"])
        for h_ in range(8):
            cp_, hh_ = h_ % 4, h_ // 4
            psl_ = slice(hh_ * 64, hh_ * 64 + 64)
            eng_ = "pool" if h_ % 2 == 0 else "dve"
            p.op(eng_, lambda e, h_=h_, cp_=cp_, psl_=psl_, b=b: e.tensor_copy(out=QZ[psl_, h_, :], in_=QT[psl_, cp_, b * 512:(b + 1) * 512]),
                 reads=["QT.%d.%d" % (cp_, b), "QZ.%d" % h_], writes=["QZ.%d" % h_])
        def run_tiles(tiles):
            sbk = [None] * len(tiles)
            def issue_S(n):
                sbk[n] = sb_S.next()
                tiles[n]["S"](sbk[n])
            for n0 in range(min(2, len(tiles))):
                issue_S(n0)
            for n, tl in enumerate(tiles):
                if n + 2 < len(tiles):
                    issue_S(n + 2)
                pi = ptc[0] % 4
                ptc[0] += 1
                lo, hi = tl["cols"]
                nr = tl["rows"]
                bs = sbk[n]
                p.op("act", lambda e, pi=pi, lo=lo, hi=hi, nr=nr, bs=bs: e.activation(out=PTb[pi][0:nr, lo:hi], in_=PS[bs][0:nr, lo:hi], func=AF.Exp),
                     reads=[psr(bs)], writes=["PT%d" % pi])
                tl["PV"](pi)
                if tl.get("epi"):
                    tl["epi"]()

        def head_info(cp, hh):
            h = cp + 4 * hh
            return h, hh, slice(hh * 64, hh * 64 + 64)

        tiles = []
        for cp in range(4):
            for hh in range(2):
                h, g, psl = head_info(cp, hh)
                bo = sb_O.next()

                def S(bs, cp=cp, psl=psl, b=b, h=h):
                    p.op("pe", lambda e: e.matmul(PS[bs][0:NCMP, :], kcmpT[:, 0:NCMP], QZ[:, h, :], start=True, stop=False),
                         reads=["kcmpT", "QZ.%d" % h], writes=[psr(bs)])
                    p.op("pe", lambda e: e.matmul(PS[bs][0:NCMP, :], idb[0:NCMP, 0:NCMP], mcmp[0:NCMP, :], start=False, stop=True),
                         reads=["idb", "mcmp"], writes=[psr(bs)])

                def PV(pi, g=g, bo=bo):
                    for qt in range(4):
                        p.op("pe", lambda e, qt=qt: e.matmul(PS[bo][:, qt * 97:(qt + 1) * 97], PTb[pi][0:NCMP, qt * 128:(qt + 1) * 128],
                                                              vaug[0:NCMP, g, :], start=(qt == 0), stop=True, skip_group_check=True),
                             reads=["PT%d" % pi, "vaug"], writes=[psr(bo)])

                def epi(h=h, g=g, bo=bo, b=b, first_in_group=(cp == 0)):
                    k = epc[0] % 2
                    epc[0] += 1
                    Ov = PS[bo][:, 0:388].rearrange("p (q c) -> p q c", q=4)
                    e_, et = ep[k], eptmp[k]
                    en, etn = "ep%d" % k, "eptmp%d" % k
                    p.op("dve", lambda e: e.tensor_scalar_max(out=e_[:, 0:4].unsqueeze(2), in0=Ov[:, :, 64:65], scalar1=1e-30), reads=[psr(bo)], writes=[en])
                    p.op("dve", lambda e: e.reciprocal(out=e_[:, 4:8], in_=e_[:, 0:4]), reads=[en], writes=[en])
                    p.op("dve", lambda e: e.tensor_tensor(out=e_[:, 8:12], in0=e_[:, 4:8], in1=gates[:, 4 * b:4 * b + 4, 3 * h + 0], op=ALU.mult),
                         reads=[en] + gate_r, writes=[en])
                    p.op("dve", lambda e: e.tensor_tensor(out=Oacc[:, :, h * 64:(h + 1) * 64], in0=Ov[:, :, 0:64],
                                                          in1=e_[:, 8:12].unsqueeze(2).to_broadcast([128, 4, 64]), op=ALU.mult),
                         reads=[psr(bo), en], writes=["Oacc.%d" % h])
                    if first_in_group:
                        p.op("dve", lambda e: e.tensor_tensor(out=imp[:, :, g, :], in0=Ov[:, :, 65:97],
                                                              in1=e_[:, 4:8].unsqueeze(2).to_broadcast([128, 4, 32]), op=ALU.mult),
                             reads=[psr(bo), en], writes=["imp.%d" % g])
                    else:
                        p.op("dve", lambda e: e.tensor_tensor(out=et[:, :, 0:32], in0=Ov[:, :, 65:97],
                                                              in1=e_[:, 4:8].unsqueeze(2).to_broadcast([128, 4, 32]), op=ALU.mult),
                             reads=[psr(bo), en], writes=[etn])
                        p.op("dve", lambda e: e.tensor_tensor(out=imp[:, :, g, :], in0=imp[:, :, g, :], in1=et[:, :, 0:32], op=ALU.add),
                             reads=[etn, "imp.%d" % g], writes=["imp.%d" % g])

                tiles.append(dict(S=S, rows=NCMP, cols=(0, 512), PV=PV, epi=epi))
        run_tiles(tiles)

        use_sel = b >= 2
        if use_sel:
            for qt in range(4):
                t = 4 * b + qt
                for g in range(2):
                    p.op("dve", lambda e, qt=qt, g=g, t=t: e.tensor_tensor(out=sc[:], in0=imp[:, qt, g, :], in1=Asc[:, t - 8, :], op=ALU.mult),
                         reads=["imp.%d" % g, "Asc"], writes=["sc"])
                    p.op("dve", lambda e, t=t: e.tensor_tensor(out=sc[:], in0=sc[:], in1=Bsc[:, t - 8, :], op=ALU.add), reads=["sc", "Bsc"], writes=["sc"])
                    p.op("dve", lambda e: e.max(out=m16[:, 0:8], in_=sc[:]), reads=["sc"], writes=["m16a"])
                    p.op("dve", lambda e: e.match_replace(out=scr[:], in_to_replace=m16[:, 0:8], in_values=sc[:], imm_value=-1e9),
                         reads=["sc", "m16a"], writes=["scr"])
                    p.op("dve", lambda e: e.max(out=m16[:, 8:16], in_=scr[:]), reads=["scr"], writes=["m16b"])
                    p.op("dve", lambda e: e.tensor_scalar(out=msel[:], in0=sc[:], scalar1=m16[:, 15:16], scalar2=1.0, op0=ALU.is_ge, op1=ALU.subtract),
                         reads=["sc", "m16b"], writes=["msel"])
                    bk = gen2.next()
                    pst = PS[bk].bitcast(BF16)
                    p.op("pe", lambda e, pst=pst, bk=bk: e.transpose(pst[0:32, 0:128], msel[:, :], idb[:]), reads=["msel", "idb"], writes=[psr(bk)])
                    p.op("dve", lambda e, pst=pst, g=g, qt=qt: e.tensor_copy(out=MT[0:32, g, qt * 128:(qt + 1) * 128], in_=pst[0:32, 0:128]),
                         reads=[psr(bk)], writes=["MT.%d" % g])

        tiles = []
        for cp in range(4):
            for hh in range(2):
                h, g, psl = head_info(cp, hh)
                for br, slot, Vt, vname in ((1, 2, VS, "VS"), (2, 3, VW, "VW")):
                    bo = sb_O.next()
                    if br == 1:
                        kts = list(range(0, 4 * b + 4))
                    else:
                        kts = [kt for kt in range(4 * b - 4, 4 * b + 4) if kt >= 0]
                    state = {"first": True}
                    for kt in kts:
                        i = kt - 4 * b
                        if i >= 0:
                            lo, hi = 128 * i, 512
                            qts = list(range(i, 4))
                            mask = ("C", 0, 512 - 128 * i)
                        elif br == 2:
                            lo, hi = 0, 128 * (5 + i)
                            qts = list(range(0, 5 + i))
                            c0 = -128 * i - 128
                            mask = ("L", c0, c0 + hi)
                        else:
                            lo, hi = 0, 512
                            qts = list(range(4))
                            mask = None
                        selm = (br == 1 and use_sel)

                        def S(bs, cp=cp, psl=psl, b=b, slot=slot, kt=kt, lo=lo, hi=hi, mask=mask, selm=selm, g=g, h=h):
                            last = (mask is None and not selm)
                            p.op("pe", lambda e: e.matmul(PS[bs][:, lo:hi], KT[:, slot, kt * 128:(kt + 1) * 128],
                                                          QZ[:, h, lo:hi], start=True, stop=last),
                                 reads=["KT.%d.%d" % (slot, kt // 4), "QZ.%d" % h], writes=[psr(bs)])
                            if selm:
                                p.op("pe", lambda e: e.matmul(PS[bs][:, lo:hi], Em[:, kt, :], MT[:, g, lo:hi], start=False, stop=(mask is None)),
                                     reads=["Em", "MT.%d" % g], writes=[psr(bs)])
                            if mask is not None:
                                mt = Cm if mask[0] == "C" else Lm
                                p.op("pe", lambda e: e.matmul(PS[bs][:, lo:hi], idb[:, :], mt[:, mask[1]:mask[2]], start=False, stop=True),
                                     reads=["idb", "Cm", "Lm"], writes=[psr(bs)])

                        def PV(pi, kt=kt, qts=qts, g=g, bo=bo, Vt=Vt, vname=vname, state=state, b=b):
                            for qt in qts:
                                first = state["first"]
                                state["first"] = False
                                p.op("pe", lambda e, qt=qt, first=first: e.matmul(PS[bo][:, qt * 65:(qt + 1) * 65], PTb[pi][:, qt * 128:(qt + 1) * 128],
                                                                                   Vt[:, kt, g, :], start=first, stop=(kt == 4 * b + qt), skip_group_check=True),
                                     reads=["PT%d" % pi, "%s.%d" % (vname, kt)], writes=[psr(bo)])

                        epi = None
                        if kt == kts[-1]:
                            def epi(h=h, bo=bo, b=b, br=br):
                                k = epc[0] % 2
                                epc[0] += 1
                                Ov = PS[bo][:, 0:260].rearrange("p (q c) -> p q c", q=4)
                                e_, et = ep[k], eptmp[k]
                                en, etn = "ep%d" % k, "eptmp%d" % k
                                p.op("dve", lambda e: e.reciprocal(out=e_[:, 4:8].unsqueeze(2), in_=Ov[:, :, 64:65]), reads=[psr(bo)], writes=[en])
                                p.op("dve", lambda e: e.tensor_tensor(out=e_[:, 8:12], in0=e_[:, 4:8], in1=gates[:, 4 * b:4 * b + 4, 3 * h + br], op=ALU.mult),
                                     reads=[en] + gate_r, writes=[en])
                                p.op("dve", lambda e: e.tensor_tensor(out=et[:, :, :], in0=Ov[:, :, 0:64],
                                                                      in1=e_[:, 8:12].unsqueeze(2).to_broadcast([128, 4, 64]), op=ALU.mult),
                                     reads=[psr(bo), en], writes=[etn])
                                p.op("pool", lambda e: e.tensor_tensor(out=Oacc[:, :, h * 64:(h + 1) * 64], in0=Oacc[:, :, h * 64:(h + 1) * 64],
                                                                       in1=et[:, :, :], op=ALU.add),
                                     reads=[etn, "Oacc.%d" % h], writes=["Oacc.%d" % h])
                        tiles.append(dict(S=S, rows=128, cols=(lo, hi), PV=PV, epi=epi))
        run_tiles(tiles)

        p.op("act", lambda e: e.copy(out=Obf[:], in_=Oacc[:]), reads=["Oacc.%d" % h for h in range(8)], writes=["Obf"])
        for fc in range(4):
            bk = gen2.next()
            pst = PS[bk].bitcast(BF16)
            for qt in range(4):
                p.op("pe", lambda e, pst=pst, qt=qt, fc=fc: e.transpose(pst[:, qt * 128:(qt + 1) * 128], Obf[:, qt, fc * 128:(fc + 1) * 128], idb[:]),
                     reads=["Obf", "idb"], writes=[psr(bk)])
            p.op("dve", lambda e, pst=pst, fc=fc, b=b: e.tensor_tensor(out=OAT[:, fc, b * 512:(b + 1) * 512], in0=pst[:, 0:512],
                                                                       in1=SGA[:, fc, b * 512:(b + 1) * 512], op=ALU.mult),
                 reads=[psr(bk), "SGA.%d.%d" % (fc, b)], writes=["OAT.%d.%d" % (fc, b)])

    if not do_sample:
        p.op("pool", lambda e: e.memset(OAT[:, :, T:TA], 0.0), writes=["OAT.%d.%d" % (fc, NB) for fc in range(4)])
    else:
        U32 = mybir.dt.uint32
        pt = din("pt", [512, 1], I32)
        pk_c = din("pk_c", [5120, 16384])
        pv_c = din("pv_c", [5120, 16384])
        pk_s = din("pk_s", [5120, 16384]).rearrange("n (h x) -> (n h) x", h=2)
        pv_s = din("pv_s", [5120, 16384]).rearrange("n (h x) -> (n h) x", h=2)
        scr_idx = nc.dram_tensor("scr_idx", [4, 128, 2], I32).ap()
        p.op("dve", lambda e: e.tensor_copy(out=QTs[:], in_=QT[:, :, T:TA]), reads=["QT.%d.%d" % (c_, NB) for c_ in range(4)], writes=["QTs"])
        p.op("dve", lambda e: e.tensor_copy(out=KTs[:], in_=KT[:, :, T:TA]), reads=["KT.%d.%d" % (c_, NB) for c_ in range(4)], writes=["KTs"])
        SGAs = p.sbuf("SGAs", [128, 4, NS], BF16)
        p.op("dve", lambda e: e.tensor_copy(out=SGAs[:], in_=SGA[:, :, T:TA]), reads=["SGA.%d.%d" % (c_, NB) for c_ in range(4)], writes=["SGAs"])
        p.arena_reset(keep=keep_s)
        NG = 3
        Gt = [p.ar("Gt%d" % i, [128, 2048], BF16) for i in range(NG)]
        gtc = [0]
        BGB = p.ar("BGB", [128, 16384], BF16)
        KTr = BGB.rearrange("p (c j q) -> p c j q", c=8, j=16)
        KsT = p.ar("KsT", [128, 64, 128], BF16)
        VGb = p.ar("VGb", [128, 8192], BF16)
        hs_s = p.ar("hs_s", [128, 1024], BF16)
        kcT_s = p.ar("kcT_s", [128, 1024], BF16)
        vc_s = p.ar("vc_s", [128, 8, 128], BF16)
        aggS = p.ar("aggS", [128, 8, 257], BF16)
        PTc = p.ar("PTc", [128, 8, 64], BF16)
        PTs = [p.ar("PTs%d" % i, [128, 8, 64], BF16) for i in range(2)]
        PTw = p.ar("PTw", [128, 4, 64], BF16)
        PTn = [p.ar("PTn%d" % i, [NS, 64], BF16) for i in range(2)]
        Kwn = p.ar("Kwn", [128, 4, 128], BF16)
        KwT = p.ar("KwT", [128, 4, 128], BF16)
        Vw_s = p.ar("Vw_s", [128, 4, 128], BF16)
        imp_n = p.ar("imp_n", [64, 257], F32)
        scs2 = imp_n[0:8, :]
        scs = p.ar("scs", [8, 257], F32)
        m16s = p.ar("m16s", [8, 16], F32)
        i16 = p.ar("i16", [8, 16], U32)
        idxw = p.ar("idxw", [8, 16, 2], I32)
        idxp = p.ar("idxp", [128, 2], I32)
        pgid = p.ar("pgid", [128, 1], I32)
        hpx = p.ar("hpx", [128, 1], I32)
        pti = p.ar("pti", [128, 1], I32)
        maskS = p.ar("maskS", [128, 64], BF16)
        maskW = p.ar("maskW", [128, 64], BF16)
        maskN = p.ar("maskN", [NS, 4, 64], BF16)
        SelE = p.ar("SelE", [64, 16], F32)
        SelO = p.ar("SelO", [64, 16], F32)
        Hsum = p.ar("Hsum", [64, 8], F32)
        pmask = p.ar("pmask", [128, 1], F32)
        Qz = p.ar("Qz", [128, 64], BF16)
        Osamp = p.ar("Osamp", [64, 64], F32)
        OW = p.ar("OW", [64, 128], F32)
        gcol = p.ar("gcol", [64, 3], F32)
        eps_ = p.ar("eps_", [64, 8], F32)
        sO = Banks([0, 1])
        gs6 = Banks([2, 3, 4, 5, 6, 7])

        def build_sample_consts():
            for cc in range(8):
                p.op("pool", lambda e, cc=cc: e.memset(aggS[:, cc, :], 1.0), reads=["aggS"], writes=["aggS"])
                p.op("pool", lambda e, cc=cc: e.affine_select(out=aggS[:, cc, :], in_=aggS[:, cc, :], compare_op=ALU.is_ge, fill=0.0, base=cc + 1,
                                                             pattern=[[-4, 257]], channel_multiplier=8), reads=["aggS"], writes=["aggS"])
                p.op("pool", lambda e, cc=cc: e.affine_select(out=aggS[:, cc, :], in_=aggS[:, cc, :], compare_op=ALU.is_ge, fill=0.0, base=3 - cc,
                                                             pattern=[[4, 257]], channel_multiplier=-8), reads=["aggS"], writes=["aggS"])
            p.op("pool", lambda e: e.memset(pmask[:], 1.0), writes=["pmask"])
            p.op("pool", lambda e: e.affine_select(out=pmask[:], in_=pmask[:], compare_op=ALU.not_equal, fill=0.0, base=-127,
                                                   pattern=[[0, 1]], channel_multiplier=1), reads=["pmask"], writes=["pmask"])
            for tl in (Hsum, SelE, SelO):
                p.op("pool", lambda e, tl=tl: e.memset(tl[:], 0.0), writes=["cst"])
            for g in range(2):
                for hq in range(4):
                    r0 = g * 32 + hq * 4
                    p.op("pool", lambda e, g=g, r0=r0: e.affine_select(out=Hsum[:, g * 4:g * 4 + 4], in_=Hsum[:, g * 4:g * 4 + 4], compare_op=ALU.not_equal, fill=1.0,
                                                                     base=-r0, pattern=[[-1, 4]], channel_multiplier=1), reads=["cst"], writes=["cst"])
                    fc = g * 2 + hq // 2
                    tl = SelE if hq % 2 == 0 else SelO
                    p.op("pool", lambda e, tl=tl, fc=fc, r0=r0: e.affine_select(out=tl[:, fc * 4:fc * 4 + 4], in_=tl[:, fc * 4:fc * 4 + 4], compare_op=ALU.not_equal, fill=1.0,
                                                                                base=-r0, pattern=[[-1, 4]], channel_multiplier=1), reads=["cst"], writes=["cst"])
            p.op("pool", lambda e: e.memset(maskS[:], 0.0), writes=["maskS"])
            mS4 = maskS.rearrange("p (g h t) -> p g h t", g=2, h=8)
            for g in range(2):
                for t_ in range(4):
                    cb = g * 4 + t_
                    v = mS4[:, g, 0:4, t_]
                    p.op("pool", lambda e, v=v: e.memset(v, 1.0), reads=["maskS"], writes=["maskS"])
                    p.op("pool", lambda e, v=v, cb=cb: e.affine_select(out=v, in_=v, compare_op=ALU.is_ge, fill=0.0, base=-cb * 16, pattern=[[0, 4]], channel_multiplier=1),
                         reads=["maskS"], writes=["maskS"])
                    p.op("pool", lambda e, v=v, cb=cb: e.affine_select(out=v, in_=v, compare_op=ALU.is_ge, fill=0.0, base=cb * 16 + 14, pattern=[[0, 4]], channel_multiplier=-1),
                         reads=["maskS"], writes=["maskS"])
            p.op("pool", lambda e: e.memset(maskW[:], 1.0), writes=["maskW"])
            p.op("pool", lambda e: e.affine_select(out=maskW.rearrange("p (a t) -> p a t", t=4), in_=maskW.rearrange("p (a t) -> p a t", t=4), compare_op=ALU.is_ge, fill=0.0,
                                                   base=0, pattern=[[0, 16], [-1, 4]], channel_multiplier=1), reads=["maskW"], writes=["maskW"])
            p.op("pool", lambda e: e.memset(maskN[:], 1.0), writes=["maskN"])
            for sq in range(4):
                v = maskN[:, sq, :].rearrange("p (a t) -> p a t", t=4)
                p.op("pool", lambda e, v=v, sq=sq: e.affine_select(out=v, in_=v, compare_op=ALU.is_ge, fill=0.0, base=-4 * sq, pattern=[[0, 16], [0, 4]], channel_multiplier=1),
                     reads=["maskN"], writes=["maskN"])
                p.op("pool", lambda e, v=v, sq=sq: e.affine_select(out=v, in_=v, compare_op=ALU.is_ge, fill=0.0, base=4 * sq + 3, pattern=[[0, 16], [0, 4]], channel_multiplier=-1),
                     reads=["maskN"], writes=["maskN"])
                p.op("pool", lambda e, v=v, sq=sq: e.affine_select(out=v, in_=v, compare_op=ALU.is_ge, fill=0.0, base=4 * sq, pattern=[[0, 16], [1, 4]], channel_multiplier=-1),
                     reads=["maskN"], writes=["maskN"])


        OATS = ["OAT.%d.%d" % (fc, NB) for fc in range(4)]
        evs = [0]

        def evac2(out_ap, in_ap, bank, writes, reads=()):
            eng = "act" if evs[0] % 2 == 0 else "dve"
            evs[0] += 1
            if eng == "act":
                p.op("act", lambda e: e.copy(out=out_ap, in_=in_ap), reads=[psr(bank)] + list(reads), writes=writes)
            else:
                p.op("dve", lambda e: e.tensor_copy(out=out_ap, in_=in_ap), reads=[psr(bank)] + list(reads), writes=writes)

        def branch_epilogue(bo, br, first):
            p.op("dve", lambda e: e.tensor_scalar_max(out=eps_[:, 0:1], in0=PS[bo][0:64, 128:129], scalar1=1e-30), reads=[psr(bo)], writes=["eps_"])
            p.op("dve", lambda e: e.reciprocal(out=eps_[:, 1:2], in_=eps_[:, 0:1]), reads=["eps_"], writes=["eps_"])
            p.op("dve", lambda e: e.tensor_tensor(out=eps_[:, 2:3], in0=eps_[:, 1:2], in1=gcol[:, br:br + 1], op=ALU.mult), reads=["eps_", "gcol"], writes=["eps_"])
            for g in range(2):
                rs_ = slice(g * 32, g * 32 + 32)
                if first:
                    p.op("dve", lambda e, rs_=rs_, g=g: e.tensor_scalar_mul(out=Osamp[rs_, :], in0=PS[bo][rs_, g * 64:(g + 1) * 64], scalar1=eps_[rs_, 2:3]),
                         reads=[psr(bo), "eps_"], writes=["Osamp"])
                else:
                    p.op("dve", lambda e, rs_=rs_, g=g: e.scalar_tensor_tensor(out=Osamp[rs_, :], in0=PS[bo][rs_, g * 64:(g + 1) * 64], scalar=eps_[rs_, 2:3],
                                                                               in1=Osamp[rs_, :], op0=ALU.mult, op1=ALU.add),
                         reads=[psr(bo), "eps_", "Osamp"], writes=["Osamp"])

        def new_keys(bo, slot, Vn, vnn, pidx, sq):
            bk = gs6.next()
            p.op("pe", lambda e, bk=bk: e.matmul(PS[bk][0:NS, 0:64], KTs[:, slot, :], Qz[:, :], start=True, stop=True), reads=["KTs", "Qz"], writes=[psr(bk)])
            Pn = PTn[pidx]
            p.op("act", lambda e, bk=bk, Pn=Pn: e.activation(out=Pn[:, :], in_=PS[bk][0:NS, 0:64], func=AF.Exp), reads=[psr(bk)], writes=["PTn%d" % pidx])
            p.op("dve", lambda e, Pn=Pn: e.tensor_tensor(out=Pn[:, :], in0=Pn[:, :], in1=maskN[:, sq, :], op=ALU.mult), reads=["PTn%d" % pidx, "maskN"], writes=["PTn%d" % pidx])
            p.op("pe", lambda e, Pn=Pn: e.matmul(PS[bo][0:64, 0:128], Pn[:, :], Vn[:, :], start=False, stop=True, skip_group_check=True),
                 reads=["PTn%d" % pidx, vnn], writes=[psr(bo)])
            p.op("pe", lambda e, Pn=Pn: e.matmul(PS[bo][0:64, 128:129], Pn[:, :], ones_b[0:NS, 0:1], start=False, stop=True, skip_group_check=True),
                 reads=["PTn%d" % pidx, "ones_b"], writes=[psr(bo)])

        def sec_pti(sq):
            p.dma("sp", lambda e, sq=sq: e.dma_start(out=pti[:, :], in_=pt[sq * 128:(sq + 1) * 128, :]), writes=["pti"])
        def sec_comp(sq):
            for pi_, pool_ in enumerate((pk_c, pv_c)):
                for rc in range(8):
                    G = Gt[gtc[0] % NG]
                    gn = "Gt%d" % (gtc[0] % NG)
                    gtc[0] += 1
                    p.dma("pool", lambda e, G=G, pool_=pool_, rc=rc: e.indirect_dma_start(
                        out=G[:, :], out_offset=None, in_=pool_, in_offset=bass.IndirectOffsetOnAxis(ap=pti[:, :], axis=0), element_offset=rc * 2048),
                        reads=["pti"], writes=[gn])
                    for hf in range(2):
                        bk = gs6.next()
                        pst = PS[bk].bitcast(BF16)
                        for r in range(8):
                            rr = hf * 8 + r
                            p.op("pe", lambda e, pst=pst, r=r, rr=rr, G=G: e.transpose(pst[:, r * 128:(r + 1) * 128], G[:, rr * 128:(rr + 1) * 128], idb[:]),
                                 reads=[gn, "idb"], writes=[psr(bk)])
                        evac2(KTr[:, rc, hf * 8:(hf + 1) * 8, :], pst[:, :].rearrange("p (j q) -> p j q", j=8), bk, ["BGB"])
                bA = gs6.next()
                bB = gs6.next()
                for j in range(32):
                    if j < 16:
                        p.op("pe", lambda e, j=j, pi_=pi_, bA=bA: e.matmul(PS[bA][:, :].rearrange("p (c q) -> p c q", c=4), W1bd[pi_][:, j, :], KTr[:, 0:4, j, :],
                                                                             start=(j == 0), stop=False, skip_group_check=True), reads=["BGB", "W1bd%d" % pi_], writes=[psr(bA)])
                        p.op("pe", lambda e, j=j, pi_=pi_, bB=bB: e.matmul(PS[bB][:, :].rearrange("p (c q) -> p c q", c=4), W1bd[pi_][:, j, :], KTr[:, 4:8, j, :],
                                                                             start=(j == 0), stop=False, skip_group_check=True), reads=["BGB", "W1bd%d" % pi_], writes=[psr(bB)])
                    else:
                        jj = j - 16
                        p.op("pe", lambda e, j=j, jj=jj, pi_=pi_, bA=bA: e.matmul(PS[bA][:, :].rearrange("p (c q) -> p c q", c=4), W1bd[pi_][:, j, :], KTr[:, 1:5, jj, :],
                                                                                    start=False, stop=(j == 31), skip_group_check=True), reads=["BGB", "W1bd%d" % pi_], writes=[psr(bA)])
                        p.op("pe", lambda e, j=j, jj=jj, pi_=pi_, bB=bB: e.matmul(PS[bB][:, 0:384].rearrange("p (c q) -> p c q", c=3), W1bd[pi_][:, j, :], KTr[:, 5:8, jj, :],
                                                                                    start=False, stop=False, skip_group_check=True), reads=["BGB", "W1bd%d" % pi_], writes=[psr(bB)])
                        p.op("pe", lambda e, j=j, jj=jj, pi_=pi_, bB=bB: e.matmul(PS[bB][:, 384:511], W1bd[pi_][:, j, :], KTr[:, 0, jj, 1:128],
                                                                                    start=False, stop=(j == 31), skip_group_check=True), reads=["BGB", "W1bd%d" % pi_], writes=[psr(bB)])
                p.op("act", lambda e, pi_=pi_, bA=bA: e.activation(out=hs_s[:, 0:512], in_=PS[bA][:, :], func=AF.Silu, bias=H0[pi_][:, 0:1]),
                     reads=[psr(bA), "H0_%d" % pi_], writes=["hs_s.a"])
                p.op("act", lambda e, pi_=pi_, bB=bB: e.activation(out=hs_s[:, 512:1024], in_=PS[bB][:, :], func=AF.Silu, bias=H0[pi_][:, 0:1]),
                     reads=[psr(bB), "H0_%d" % pi_], writes=["hs_s.b"])
                if pi_ == 0:
                    for hf in range(2):
                        bk = gs6.next()
                        p.op("pe", lambda e, hf=hf, bk=bk: e.matmul(PS[bk][:, :], W2bd[0][:, :], hs_s[:, hf * 512:(hf + 1) * 512], start=True, stop=True),
                             reads=["W2bd0", "hs_s.a", "hs_s.b"], writes=[psr(bk)])
                        evac2(kcT_s[:, hf * 512:(hf + 1) * 512], PS[bk][:, :], bk, ["kcT_s.%d" % hf])
                else:
                    for hf in range(2):
                        bk = gs6.next()
                        for c4 in range(4):
                            cc = hf * 4 + c4
                            p.op("pe", lambda e, cc=cc, c4=c4, bk=bk: e.matmul(PS[bk][:, c4 * 128:(c4 + 1) * 128], hs_s[:, cc * 128:(cc + 1) * 128], W2bd[1][:, :],
                                                                                 start=(c4 == 0), stop=True, skip_group_check=True),
                                 reads=["W2bd1", "hs_s.a", "hs_s.b"], writes=[psr(bk)])
                        evac2(vc_s[:, hf * 4:(hf + 1) * 4, :], PS[bk][:, :].rearrange("p (c q) -> p c q", c=4), bk, ["vc_s.%d" % hf])

        def sec_cmp(sq):
            p.op("pool", lambda e: e.memset(Qz[:], 0.0), writes=["Qz"])
            p.op("dve", lambda e, sq=sq: e.tensor_copy(out=Qz[0:64, 0:16].rearrange("p (h t) -> p h t", h=4), in_=QTs[0:64, :, 4 * sq:4 * sq + 4]),
                 reads=["QTs", "Qz"], writes=["Qz"])
            p.op("dve", lambda e, sq=sq: e.tensor_copy(out=Qz[64:128, 32:48].rearrange("p (h t) -> p h t", h=4), in_=QTs[64:128, :, 4 * sq:4 * sq + 4]),
                 reads=["QTs", "Qz"], writes=["Qz"])
            p.op("pool", lambda e: e.memset(gcol[:], 0.0), writes=["gcol"])
            for g in range(2):
                for hq in range(4):
                    r0 = g * 32 + hq * 4
                    h = g * 4 + hq
                    p.dma("sp", lambda e, r0=r0, h=h, sq=sq: e.dma_start(out=gcol[r0:r0 + 4, 0:3], in_=gates_s[4 * sq:4 * sq + 4, h * 3:h * 3 + 3]),
                          reads=["gates_s", "gcol"], writes=["gcol"])
            bk = gs6.next()
            for cc in range(8):
                p.op("pe", lambda e, cc=cc, bk=bk: e.matmul(PS[bk][:, cc * 64:(cc + 1) * 64], kcT_s[:, cc * 128:(cc + 1) * 128], Qz[:, :],
                                                             start=(cc == 0), stop=True, skip_group_check=True),
                     reads=["kcT_s.%d" % (cc // 4), "Qz"], writes=[psr(bk)])
            p.op("act", lambda e, bk=bk: e.activation(out=PTc[:].rearrange("p c q -> p (c q)"), in_=PS[bk][:, :], func=AF.Exp), reads=[psr(bk)], writes=["PTc"])
            p.op("dve", lambda e: e.tensor_scalar_mul(out=PTc[:, 7, :], in0=PTc[:, 7, :], scalar1=pmask[:, 0:1]), reads=["PTc", "pmask"], writes=["PTc"])
            bo = sO.next()
            for cc in range(8):
                p.op("pe", lambda e, cc=cc, bo=bo: e.matmul(PS[bo][0:64, 0:128], PTc[:, cc, :], vc_s[:, cc, :], start=(cc == 0), stop=(cc == 7), skip_group_check=True),
                     reads=["PTc", "vc_s.%d" % (cc // 4)], writes=[psr(bo)])
                p.op("pe", lambda e, cc=cc, bo=bo: e.matmul(PS[bo][0:64, 128:129], PTc[:, cc, :], ones_b[:, 0:1], start=False, stop=(cc == 7), skip_group_check=True),
                     reads=["PTc", "ones_b"], writes=[psr(bo)])
                p.op("pe", lambda e, cc=cc, bo=bo: e.matmul(PS[bo][0:64, 129:386], PTc[:, cc, :], aggS[:, cc, :], start=False, stop=(cc == 7), skip_group_check=True),
                     reads=["PTc", "aggS"], writes=[psr(bo)])
            branch_epilogue(bo, 0, True)
            p.op("dve", lambda e, bo=bo: e.tensor_scalar_mul(out=imp_n[:, :], in0=PS[bo][0:64, 129:386], scalar1=eps_[:, 1:2]), reads=[psr(bo), "eps_"], writes=["imp_n"])
            bk = gs6.next()
            p.op("pe", lambda e, bk=bk: e.matmul(PS[bk][0:8, 0:257], Hsum[:, :], imp_n[:, :], start=True, stop=True), reads=["cst", "imp_n"], writes=[psr(bk)])
            p.op("dve", lambda e, bk=bk: e.tensor_copy(out=scs[:, :], in_=PS[bk][0:8, 0:257]), reads=[psr(bk)], writes=["scs"])
            p.op("dve", lambda e: e.memset(scs[:, 0:1], -1.0), reads=["scs"], writes=["scs"])
            p.op("dve", lambda e: e.memset(scs[:, 255:257], -1.0), reads=["scs"], writes=["scs"])
            p.op("dve", lambda e: e.max(out=m16s[:, 0:8], in_=scs[:, :]), reads=["scs"], writes=["m16s.a"])
            p.op("dve", lambda e: e.max_index(out=i16[:, 0:8], in_max=m16s[:, 0:8], in_values=scs[:, :]), reads=["scs", "m16s.a"], writes=["i16.a"])
            p.op("dve", lambda e: e.match_replace(out=scs2[:, :], in_to_replace=m16s[:, 0:8], in_values=scs[:, :], imm_value=-1e9), reads=["scs", "m16s.a", "imp_n"], writes=["imp_n"])
            p.op("dve", lambda e: e.max(out=m16s[:, 8:16], in_=scs2[:, :]), reads=["imp_n"], writes=["m16s.b"])
            p.op("dve", lambda e: e.max_index(out=i16[:, 8:16], in_max=m16s[:, 8:16], in_values=scs2[:, :]), reads=["imp_n", "m16s.b"], writes=["i16.b"])
            i16i = i16.bitcast(I32)
            p.op("dve", lambda e, i16i=i16i: e.memset(i16i[:, 13:14], 0), reads=["i16.a", "i16.b"], writes=["i16.c"])
            p.op("dve", lambda e, i16i=i16i: e.memset(i16i[:, 14:15], 255), reads=["i16.c"], writes=["i16.c"])
            p.op("dve", lambda e, i16i=i16i: e.memset(i16i[:, 15:16], 0), reads=["i16.c"], writes=["i16.c"])
            p.op("dve", lambda e, i16i=i16i: e.tensor_single_scalar(out=idxw[:, :, 0], in_=i16i[:, :], scalar=1, op=ALU.arith_shift_right),
                 reads=["i16.a", "i16.b", "i16.c"], writes=["idxw.a"])
            p.op("dve", lambda e, sq=sq: e.tensor_single_scalar(out=idxw[:, :, 0], in_=idxw[:, :, 0], scalar=sq * 128, op=ALU.add),
                 reads=["idxw.a"], writes=["idxw.a"])
            p.op("dve", lambda e, i16i=i16i: e.tensor_single_scalar(out=idxw[:, :, 1], in_=i16i[:, :], scalar=1, op=ALU.bitwise_and),
                 reads=["i16.a", "i16.b", "i16.c"], writes=["idxw.b"])
            p.dma("sp", lambda e, sq=sq: e.dma_start(out=scr_idx[sq].rearrange("(r k) c -> r (k c)", k=16), in_=idxw[:].rearrange("p k c -> p (k c)")),
                  reads=["idxw.a", "idxw.b"], writes=["scr_idx"])
            p.dma("sp", lambda e, sq=sq: e.dma_start(out=idxp[:, :], in_=scr_idx[sq]), reads=["scr_idx"], writes=["idxp"])
            p.dma("pool", lambda e: e.indirect_dma_start(out=pgid[:, :], out_offset=None, in_=pt, in_offset=bass.IndirectOffsetOnAxis(ap=idxp[:, 0:1], axis=0)),
                  reads=["idxp"], writes=["pgid"])
            p.op("dve", lambda e: e.scalar_tensor_tensor(out=hpx[:, :], in0=pgid[:, :], scalar=2, in1=idxp[:, 1:2], op0=ALU.mult, op1=ALU.add),
                 reads=["pgid", "idxp"], writes=["hpx"])
        def sec_gath(sq):
            for hf in range(2):
                p.dma("pool", lambda e, hf=hf: e.indirect_dma_start(out=wbuf[hf][:].rearrange("p a b -> p (a b)"), out_offset=None, in_=pk_s,
                                                                     in_offset=bass.IndirectOffsetOnAxis(ap=hpx[:, :], axis=0), element_offset=hf * 4096),
                      reads=["hpx"], writes=["wbuf%d" % hf])
            p.dma("pool", lambda e: e.indirect_dma_start(out=VGb[:, :], out_offset=None, in_=pv_s, in_offset=bass.IndirectOffsetOnAxis(ap=hpx[:, :], axis=0)),
                  reads=["hpx"], writes=["VGb"])

        def sec_sel(sq):
            for k8 in range(8):
                bk = gs6.next()
                pst = PS[bk].bitcast(BF16)
                for r in range(8):
                    k = k8 * 8 + r
                    kw_ = wbuf[k // 32][:].rearrange("p a b -> p (a b)")
                    p.op("pe", lambda e, pst=pst, r=r, k=k, kw_=kw_: e.transpose(pst[:, r * 128:(r + 1) * 128], kw_[:, (k % 32) * 128:(k % 32 + 1) * 128], idb[:]),
                         reads=["wbuf%d" % (k // 32), "idb"], writes=[psr(bk)])
                evac2(KsT[:, k8 * 8:(k8 + 1) * 8, :], pst[:, :].rearrange("p (j q) -> p j q", j=8), bk, ["KsT.%d" % k8])
            bo = sO.next()
            for k8 in range(8):
                bk = gs6.next()
                for r in range(8):
                    k = k8 * 8 + r
                    p.op("pe", lambda e, r=r, k=k, bk=bk: e.matmul(PS[bk][:, r * 64:(r + 1) * 64], KsT[:, k, :], Qz[:, :], start=(r == 0), stop=True, skip_group_check=True),
                         reads=["KsT.%d" % k8, "Qz"], writes=[psr(bk)])
                P_ = PTs[k8 % 2]
                pn = "PTs%d" % (k8 % 2)
                p.op("act", lambda e, bk=bk, P_=P_: e.activation(out=P_[:].rearrange("p c q -> p (c q)"), in_=PS[bk][:, :], func=AF.Exp), reads=[psr(bk)], writes=[pn])
                p.op("dve", lambda e, P_=P_: e.tensor_tensor(out=P_[:], in0=P_[:], in1=maskS[:, :].unsqueeze(1).to_broadcast([128, 8, 64]), op=ALU.mult),
                     reads=[pn, "maskS"], writes=[pn])
                for r in range(8):
                    k = k8 * 8 + r
                    first = (k == 0)
                    p.op("pe", lambda e, r=r, k=k, bo=bo, P_=P_, first=first: e.matmul(PS[bo][0:64, 0:128], P_[:, r, :], VGb[:, k * 128:(k + 1) * 128],
                                                                                         start=first, stop=False, skip_group_check=True), reads=[pn, "VGb"], writes=[psr(bo)])
                    p.op("pe", lambda e, r=r, bo=bo, P_=P_: e.matmul(PS[bo][0:64, 128:129], P_[:, r, :], ones_b[:, 0:1], start=False, stop=False, skip_group_check=True),
                         reads=[pn, "ones_b"], writes=[psr(bo)])

            new_keys(bo, 2, VNs, "VNs", 0, sq)
            branch_epilogue(bo, 1, False)

        def sec_win(sq):
            p.dma("pool", lambda e, sq=sq: e.dma_start(out=Kwn[:], in_=ckw[sq].rearrange("(i r) c -> r i c", r=128)), writes=["Kwn"])
            p.dma("pool", lambda e, sq=sq: e.dma_start(out=Vw_s[:], in_=cvw[sq].rearrange("(i r) c -> r i c", r=128)), writes=["Vw_s"])
            bk = gs6.next()
            pst = PS[bk].bitcast(BF16)
            for i in range(4):
                p.op("pe", lambda e, i=i, pst=pst: e.transpose(pst[:, i * 128:(i + 1) * 128], Kwn[:, i, :], idb[:]), reads=["Kwn", "idb"], writes=[psr(bk)])
            evac2(KwT[:], pst[:, 0:512].rearrange("p (j q) -> p j q", j=4), bk, ["KwT"])
            bk = gs6.next()
            for i in range(4):
                p.op("pe", lambda e, i=i, bk=bk: e.matmul(PS[bk][:, i * 64:(i + 1) * 64], KwT[:, i, :], Qz[:, :], start=(i == 0), stop=True, skip_group_check=True),
                     reads=["KwT", "Qz"], writes=[psr(bk)])
            p.op("act", lambda e, bk=bk: e.activation(out=PTw[:].rearrange("p c q -> p (c q)"), in_=PS[bk][:, 0:256], func=AF.Exp), reads=[psr(bk)], writes=["PTw"])
            p.op("dve", lambda e: e.tensor_tensor(out=PTw[:, 0, :], in0=PTw[:, 0, :], in1=maskW[:, :], op=ALU.mult), reads=["PTw", "maskW"], writes=["PTw"])
            bo = sO.next()
            for i in range(4):
                p.op("pe", lambda e, i=i, bo=bo: e.matmul(PS[bo][0:64, 0:128], PTw[:, i, :], Vw_s[:, i, :], start=(i == 0), stop=False, skip_group_check=True),
                     reads=["PTw", "Vw_s"], writes=[psr(bo)])
                p.op("pe", lambda e, i=i, bo=bo: e.matmul(PS[bo][0:64, 128:129], PTw[:, i, :], ones_b[:, 0:1], start=False, stop=False, skip_group_check=True),
                     reads=["PTw", "ones_b"], writes=[psr(bo)])
            new_keys(bo, 3, VNw, "VNw", 1, sq)
            branch_epilogue(bo, 2, False)

        def sec_place(sq):
            p.op("dve", lambda e: e.tensor_copy(out=OW[:, 0:64], in_=Osamp[:, :]), reads=["Osamp"], writes=["OW"])
            p.op("dve", lambda e: e.tensor_copy(out=OW[:, 64:128], in_=Osamp[:, :]), reads=["Osamp", "OW"], writes=["OW"])
            b1 = gs6.next()
            b2 = gs6.next()
            p.op("pe", lambda e, b1=b1: e.matmul(PS[b1][0:64, 0:16], OW[:, 0:64], SelE[:, :], start=True, stop=True), reads=["OW", "cst"], writes=[psr(b1)])
            p.op("pe", lambda e, b2=b2: e.matmul(PS[b2][:, 0:16], OW[:, :], SelO[:, :], start=True, stop=True), reads=["OW", "cst"], writes=[psr(b2)])
            p.op("dve", lambda e, b1=b1, sq=sq: e.tensor_tensor(out=OAT[0:64, :, T + 4 * sq:T + 4 * sq + 4], in0=PS[b1][0:64, 0:16].rearrange("p (f t) -> p f t", f=4),
                                                                 in1=SGAs[0:64, :, 4 * sq:4 * sq + 4], op=ALU.mult), reads=[psr(b1), "SGAs"] + OATS, writes=OATS)
            p.op("dve", lambda e, b2=b2, sq=sq: e.tensor_tensor(out=OAT[64:128, :, T + 4 * sq:T + 4 * sq + 4], in0=PS[b2][64:128, 0:16].rearrange("p (f t) -> p f t", f=4),
                                                                 in1=SGAs[64:128, :, 4 * sq:4 * sq + 4], op=ALU.mult), reads=[psr(b2), "SGAs"] + OATS, writes=OATS)
        sec_pti(0)
        sec_comp(0)
        build_sample_consts()
        sec_cmp(0)
        sec_win(0)
        for sq in range(4):
            if sq + 1 < 4:
                sec_pti(sq + 1)
                sec_comp(sq + 1)
            sec_gath(sq)
            sec_sel(sq)
            sec_place(sq)
            if sq + 1 < 4:
                sec_cmp(sq + 1)
                sec_win(sq + 1)
    if stop_after <= 3:
        dbg = dout("dbg", [128, 4 * T], BF16)
        p.dma("sp", lambda e: e.dma_start(out=dbg.rearrange("p (c t) -> p c t", c=4), in_=OAT[:, :, 0:T]),
              reads=["OAT.%d.%d" % (fc, b) for fc in range(4) for b in range(NB)], is_output=True)
        return nc, p, locals()


    p.arena_reset()
    CBT = p.ar("CBT", [128, 4, TA], BF16)
    keep45 = p.ar_off
    UT = p.ar("UT", [128, 4, 30 + T], BF16)
    DG = p.ar("DG", [128, 4, 31, 128], BF16)
    accA2 = [[p.ar("accA%d_%d" % (j, i), [128, 512], F32) for i in range(4)] for j in range(2)]
    accA = accA2[0]
    xb16 = p.ar("xb16", [128, 4, 512], BF16)
    xsq16 = p.ar("xsq16", [128, 4, 512], BF16)
    mean = p.ar("mean", [128, 512], F32)
    rstd = p.ar("rstd", [128, 512], F32)
    msq = p.ar("msq", [128, 512], F32)
    sgt = [p.ar("sgt%d" % i, [128, 512], F32) for i in range(2)]
    YS = p.ar("YS", [128, 4, 512], BF16)
    PW = p.ar("PW", [128, 4, 512], BF16)
    vecn = p.ar("vecn", [35, 512], F32)
    vecT = p.ar("vecT", [128, 4, 35], F32)
    utok = p.ar("utok", [128, 512], F32)

    p.dma("sp", lambda e: e.dma_start(out=vecn[:, :], in_=vecs), writes=["vecn"])
    for ct in range(4):
        bk = gen.next()
        p.op("pe", lambda e, bk=bk, ct=ct: e.transpose(PS[bk][:, 0:35], vecn[:, ct * 128:(ct + 1) * 128], idf[0:35, 0:35]),
             reads=["vecn", "idf"], writes=[psr(bk)])
        p.op("dve", lambda e, bk=bk, ct=ct: e.tensor_copy(out=vecT[:, ct, :], in_=PS[bk][:, 0:35]), reads=[psr(bk)], writes=["vecT"])
    p.dma("pool", lambda e: e.dma_start(out=PW[:], in_=pw_w.rearrange("(kt p) c -> p kt c", p=128)), writes=["PW"])
    for ct in range(4):
        eng = "dve" if ct % 2 == 0 else "pool"
        p.op(eng, lambda e, ct=ct: e.tensor_tensor(out=DG[:, ct, :, :], in0=idb[:, :].unsqueeze(1).to_broadcast([128, 31, 128]),
                                                   in1=vecT[:, ct, 0:31].unsqueeze(2).to_broadcast([128, 31, 128]), op=ALU.mult),
             reads=["idb", "vecT"], writes=["DG.%d" % ct])
    p.op("pool", lambda e: e.memset(UT[:, :, 0:30], 0.0), writes=["UT.h"])

    UTs = p.ar("UTs", [128, 4, 4, 34], F32)
    sct = p.ar("sct", [120, 512], F32)
    utok_s = p.ar("utok_s", [NS, 512], F32)
    p.dma("sp", lambda e: e.dma_start(out=sct[:, :], in_=sconv.rearrange("s r c -> (s r) c")), writes=["sct"])
    for ct in range(4):
        bk = gen.next()
        p.op("pe", lambda e, bk=bk, ct=ct: e.transpose(PS[bk][:, 0:120], sct[:, ct * 128:(ct + 1) * 128], idf[0:120, 0:120]),
             reads=["sct", "idf"], writes=[psr(bk)])
        p.op("dve", lambda e, bk=bk, ct=ct: e.tensor_copy(out=UTs[:, ct, :, 0:30], in_=PS[bk][:, 0:120].rearrange("p (s r) -> p s r", s=4)),
             reads=[psr(bk)], writes=["UTs.h%d" % ct])
    for sq in range(4):
        p.dma("sp", lambda e, sq=sq: e.dma_start(out=s_conv[sq, 0:26, :], in_=sconv[sq, 4:30, :]), is_output=True)

    wv, wvn = load_w(C_GLU)
    wg, wgn = load_w(C_GLU + 512)
    sgc = [0]
    for ct in range(4):
        for blk in range(NBA):
            n = bn(blk)
            bk = gen.next()
            proj_fm(wg, wgn, ct, blk, bk)
            k = sgc[0] % 2
            sgc[0] += 1
            p.op("act", lambda e, bk=bk, k=k, n=n: e.activation(out=sgt[k][:, 0:n], in_=PS[bk][:, 0:n], func=AF.Sigmoid), reads=[psr(bk)], writes=["sgt%d" % k])
            bk2 = gen.next()
            proj_fm(wv, wvn, ct, blk, bk2)
            if blk < NB:
                p.op("dve", lambda e, bk2=bk2, k=k, ct=ct, blk=blk: e.tensor_tensor(out=UT[:, ct, 30 + blk * 512:30 + (blk + 1) * 512], in0=PS[bk2][:, :],
                                                                                     in1=sgt[k][:], op=ALU.mult),
                     reads=[psr(bk2), "sgt%d" % k], writes=["UT.%d.%d" % (ct, blk)])
            else:
                p.op("dve", lambda e, bk2=bk2, k=k, ct=ct: e.tensor_tensor(out=UTs[:, ct, :, 30:34], in0=PS[bk2][:, 0:NS].rearrange("p (s t) -> p s t", s=4),
                                                                            in1=sgt[k][:, 0:NS].rearrange("p (s t) -> p s t", s=4), op=ALU.mult),
                     reads=[psr(bk2), "sgt%d" % k, "UTs.h%d" % ct], writes=["UTs.n%d" % ct])
    for (lo, hi, m, ut_, un, rd) in ((T - 128, T, 128, utok, "utok", "xnT.%d" % (NT - 1)), (T, TA, NS, utok_s, "utok_s", "xnT.s")):
        b1 = gen.next()
        b2 = gen.next()
        for kt in range(8):
            p.op("pe", lambda e, b1=b1, kt=kt, lo=lo, hi=hi, m=m: e.matmul(PS[b1][0:m, :], xnT[:, kt, lo:hi], wv[:, kt, :], start=(kt == 0), stop=(kt == 7)),
                 reads=[wvn, rd], writes=[psr(b1)])
        for kt in range(8):
            p.op("pe", lambda e, b2=b2, kt=kt, lo=lo, hi=hi, m=m: e.matmul(PS[b2][0:m, :], xnT[:, kt, lo:hi], wg[:, kt, :], start=(kt == 0), stop=(kt == 7)),
                 reads=[wgn, rd], writes=[psr(b2)])
        p.op("act", lambda e, b2=b2, m=m, ut_=ut_: e.activation(out=ut_[0:m, :], in_=PS[b2][0:m, :], func=AF.Sigmoid), reads=[psr(b2)], writes=[un])
        p.op("dve", lambda e, b1=b1, m=m, ut_=ut_: e.tensor_tensor(out=ut_[0:m, :], in0=PS[b1][0:m, :], in1=ut_[0:m, :], op=ALU.mult), reads=[psr(b1), un], writes=[un])
    p.dma("sp", lambda e: e.dma_start(out=o_conv, in_=utok[98:128, :]), reads=["utok"], is_output=True)
    for sq in range(4):
        p.dma("sp", lambda e, sq=sq: e.dma_start(out=s_conv[sq, 26:30, :], in_=utok_s[4 * sq:4 * sq + 4, :]), reads=["utok_s"], is_output=True)

    wgb, wgbn = load_w(C_GB)
    NTA = 20

    def ln_pw(blk, acc, an):
        n = bn(blk)
        for ct in range(4):
            p.op("act", lambda e, ct=ct: e.copy(out=xb16[:, ct, 0:n], in_=acc[ct][:, 0:n]), reads=[an % ct], writes=["xb16.%d" % ct])
            p.op("act", lambda e, ct=ct: e.activation(out=xsq16[:, ct, 0:n], in_=acc[ct][:, 0:n], func=AF.Square), reads=[an % ct], writes=["xsq16.%d" % ct])
        b1 = gen.next()
        b2 = gen.next()
        for ct in range(4):
            p.op("pe", lambda e, ct=ct: e.matmul(PS[b1][:, 0:n], ones_b[:, :], xb16[:, ct, 0:n], start=(ct == 0), stop=(ct == 3)),
                 reads=["ones_b", "xb16.%d" % ct], writes=[psr(b1)])
        for ct in range(4):
            p.op("pe", lambda e, ct=ct: e.matmul(PS[b2][:, 0:n], ones_b[:, :], xsq16[:, ct, 0:n], start=(ct == 0), stop=(ct == 3)),
                 reads=["ones_b", "xsq16.%d" % ct], writes=[psr(b2)])
        p.op("dve", lambda e: e.tensor_scalar_mul(out=mean[:, 0:n], in0=PS[b1][:, 0:n], scalar1=1.0 / 512), reads=[psr(b1)], writes=["mean"])
        p.op("dve", lambda e: e.tensor_tensor(out=msq[:, 0:n], in0=mean[:, 0:n], in1=mean[:, 0:n], op=ALU.mult), reads=["mean"], writes=["msq"])
        p.op("dve", lambda e: e.scalar_tensor_tensor(out=rstd[:, 0:n], in0=PS[b2][:, 0:n], scalar=1.0 / 512, in1=msq[:, 0:n], op0=ALU.mult, op1=ALU.subtract),
             reads=[psr(b2), "msq"], writes=["rstd"])
        p.op("dve", lambda e: e.tensor_scalar_add(out=rstd[:, 0:n], in0=rstd[:, 0:n], scalar1=EPS), reads=["rstd"], writes=["rstd"])
        p.op("act", lambda e: e.activation(out=rstd[:, 0:n], in_=rstd[:, 0:n], func=AF.Sqrt), reads=["rstd"], writes=["rstd"])
        p.op("dve", lambda e: e.reciprocal(out=rstd[:, 0:n], in_=rstd[:, 0:n]), reads=["rstd"], writes=["rstd"])
        for ct in range(4):
            p.op("dve", lambda e, ct=ct: e.tensor_tensor(out=acc[ct][:, 0:n], in0=acc[ct][:, 0:n], in1=mean[:, 0:n], op=ALU.subtract),
                 reads=[an % ct, "mean"], writes=[an % ct])
            p.op("pool", lambda e, ct=ct: e.tensor_tensor(out=acc[ct][:, 0:n], in0=acc[ct][:, 0:n], in1=rstd[:, 0:n], op=ALU.mult),
                 reads=[an % ct, "rstd"], writes=[an % ct])
            p.op("act", lambda e, ct=ct: e.activation(out=YS[:, ct, 0:n], in_=acc[ct][:, 0:n], func=AF.Silu, scale=vecT[:, ct, 32:33], bias=vecT[:, ct, 33:34]),
                 reads=[an % ct, "vecT"], writes=["YS.%d" % ct])
        for co in range(4):
            bk = gen.next()
            proj_fm(wgb, wgbn, co, blk, bk)
            k = sgc[0] % 2
            sgc[0] += 1
            p.op("act", lambda e, bk=bk, k=k: e.activation(out=sgt[k][:, 0:n], in_=PS[bk][:, 0:n], func=AF.Silu), reads=[psr(bk)], writes=["sgt%d" % k])
            bk2 = gen.next()
            for ci in range(4):
                p.op("pe", lambda e, bk2=bk2, ci=ci, co=co: e.matmul(PS[bk2][:, 0:n], PW[:, ci, co * 128:(co + 1) * 128], YS[:, ci, 0:n], start=(ci == 0), stop=(ci == 3)),
                     reads=["PW", "YS.%d" % ci], writes=[psr(bk2)])
            p.op("dve", lambda e, bk2=bk2, k=k, co=co: e.scalar_tensor_tensor(out=CBT[:, co, bsl(blk)], in0=PS[bk2][:, 0:n],
                                                                               scalar=vecT[:, co, 34:35], in1=sgt[k][:, 0:n], op0=ALU.add, op1=ALU.mult),
                 reads=[psr(bk2), "sgt%d" % k, "vecT"], writes=["CBT.%d.%d" % (co, blk)])

    def conv_blk(blk, acc, an):
        def u_sl(ct, w):
            return UT[:, ct, blk * 512 + w: blk * 512 + w + 512]

        def u_reads(ct):
            r = ["UT.%d.%d" % (ct, blk)]
            r.append("UT.%d.%d" % (ct, blk - 1) if blk > 0 else "UT.h")
            return r
        for ct in range(4):
            bk = gen.next()
            for w in range(31):
                uw_ = u_sl(ct, w)
                p.op("pe", lambda e, ct=ct, w=w, uw_=uw_, bk=bk: e.matmul(PS[bk][:, :], DG[:, ct, w, :], uw_, start=(w == 0), stop=(w == 30)),
                     reads=u_reads(ct) + ["DG.%d" % ct], writes=[psr(bk)])
            if ct % 2 == 0:
                p.op("dve", lambda e, ct=ct, bk=bk: e.tensor_scalar(out=acc[ct][:], in0=PS[bk][:, :], scalar1=vecT[:, ct, 31:32], scalar2=None, op0=ALU.add),
                     reads=[psr(bk), "vecT"], writes=[an % ct])
            else:
                p.op("act", lambda e, ct=ct, bk=bk: e.activation(out=acc[ct][:], in_=PS[bk][:, :], func=AF.Identity, bias=vecT[:, ct, 31:32]),
                     reads=[psr(bk), "vecT"], writes=[an % ct])

    accS = [p.ar("accS%d" % i, [128, NS], F32) for i in range(4)]
    for w in range(31):
        for ct in range(4):
            uw_ = UTs[:, ct, :, w:w + 4]
            av = accS[ct][:, 0:NS].rearrange("p (s t) -> p s t", s=4)
            rd = ["UTs.h%d" % ct, "UTs.n%d" % ct, "vecT"]
            if w == 0:
                p.op("dve", lambda e, ct=ct, uw_=uw_, av=av: e.tensor_scalar(out=av, in0=uw_, scalar1=vecT[:, ct, 0:1], scalar2=vecT[:, ct, 31:32],
                                                                             op0=ALU.mult, op1=ALU.add), reads=rd, writes=["accS%d" % ct])
            else:
                p.op("dve", lambda e, ct=ct, w=w, uw_=uw_, av=av: e.scalar_tensor_tensor(out=av, in0=uw_, scalar=vecT[:, ct, w:w + 1], in1=av,
                                                                                          op0=ALU.mult, op1=ALU.add), reads=rd + ["accS%d" % ct], writes=["accS%d" % ct])

    ANS = ["accA0_%d", "accA1_%d"]
    conv_blk(0, accA2[0], ANS[0])
    for blk in range(NB):
        if blk + 1 < NB:
            conv_blk(blk + 1, accA2[(blk + 1) % 2], ANS[(blk + 1) % 2])
        ln_pw(blk, accA2[blk % 2], ANS[blk % 2])

    ln_pw(NB, accS, "accS%d")

    if stop_after <= 4:
        dbg4 = dout("dbg4", [128, 4 * 512], BF16)
        p.dma("sp", lambda e: e.dma_start(out=dbg4, in_=YS[:].rearrange("p c t -> p (c t)")), reads=["YS.%d" % c_ for c_ in range(4)], is_output=True)
        dbg5 = dout("dbg5", [128, 3 * 512], F32)
        p.dma("sp", lambda e: e.dma_start(out=dbg5[:, 0:512], in_=mean[:]), reads=["mean"], is_output=True)
        p.dma("sp", lambda e: e.dma_start(out=dbg5[:, 512:1024], in_=rstd[:]), reads=["rstd"], is_output=True)
        p.dma("sp", lambda e: e.dma_start(out=dbg5[:, 1024:1536], in_=accA[0][:]), reads=["accA0_0"], is_output=True)
        dbg = dout("dbg", [128, 4 * T], BF16)
        p.dma("sp", lambda e: e.dma_start(out=dbg.rearrange("p (c t) -> p c t", c=4), in_=CBT[:, :, 0:T]),
              reads=["CBT.%d.%d" % (fc, b) for fc in range(4) for b in range(NB)], is_output=True)
        return nc, p, locals()

    p.arena_reset(keep=keep45)
    WPA = p.ar("WPA", [128, 4, D], BF16)
    WPB = p.ar("WPB", [128, 4, D], BF16)
    HT = p.ar("HT", [128, 8, TA], BF16)
    WO = p.ar("WO", [128, 8, D], BF16)
    fing_b = p.ar("fing_b", [128, D], F32)
    smt = [p.ar("smt%d" % i, [128, 512], F32) for i in range(4)]
    xr = [p.ar("xr%d" % i, [128, D], F32) for i in range(2)]
    yo = [p.ar("yo%d" % i, [128, D], F32) for i in range(2)]
    st2 = p.sbuf("st2", [128, NT + 1, 4], F32)
    p.dma("pool", lambda e: e.dma_start(out=WPA[:], in_=w_pa.rearrange("(kt p) c -> p kt c", p=128)), writes=["WPA"])
    p.dma("pool", lambda e: e.dma_start(out=WPB[:], in_=w_pb.rearrange("(kt p) c -> p kt c", p=128)), writes=["WPB"])
    for half in range(2):
        p.dma("pool", lambda e, half=half: e.dma_start(out=WO[:, half * 4:(half + 1) * 4, :],
                                                        in_=w_o.rearrange("(kt p) c -> p kt c", p=128)[:, half * 4:(half + 1) * 4, :]), writes=["WO"])
    p.dma("sp", lambda e: e.dma_start(out=fing_b[:], in_=final_g.broadcast_to([128, D])), writes=["fing_b"])
    for half in range(2):
        wma, wman = load_w(C_MA + half * 512)
        wmb, wmbn = load_w(C_MB + half * 512)
        for blk in range(NBA):
            n = bn(blk)
            for ii in range(4):
                i = half * 4 + ii
                bka = gen.next()
                proj_fm(wma, wman, ii, blk, bka)
                p.op("act", lambda e, bka=bka, n=n: e.activation(out=smt[0][:, 0:n], in_=PS[bka][:, 0:n], func=AF.Sigmoid), reads=[psr(bka)], writes=["smt0"])
                bkb = gen.next()
                proj_fm(wmb, wmbn, ii, blk, bkb)
                p.op("act", lambda e, bkb=bkb, n=n: e.activation(out=smt[1][:, 0:n], in_=PS[bkb][:, 0:n], func=AF.Sigmoid), reads=[psr(bkb)], writes=["smt1"])
                bk1 = gen.next()
                for f_ in range(4):
                    p.op("pe", lambda e, f_=f_, i=i, bk1=bk1, blk=blk, n=n: e.matmul(PS[bk1][:, 0:n], WPA[:, f_, i * 128:(i + 1) * 128], OAT[:, f_, bsl(blk)],
                                                                                      start=(f_ == 0), stop=(f_ == 3)),
                         reads=["WPA", "OAT.%d.%d" % (f_, blk)], writes=[psr(bk1)])
                p.op("dve", lambda e, bk1=bk1, n=n: e.tensor_tensor(out=smt[2][:, 0:n], in0=PS[bk1][:, 0:n], in1=smt[0][:, 0:n], op=ALU.mult),
                     reads=[psr(bk1), "smt0"], writes=["smt2"])
                bk2 = gen.next()
                for f_ in range(4):
                    p.op("pe", lambda e, f_=f_, i=i, bk2=bk2, blk=blk, n=n: e.matmul(PS[bk2][:, 0:n], WPB[:, f_, i * 128:(i + 1) * 128], CBT[:, f_, bsl(blk)],
                                                                                      start=(f_ == 0), stop=(f_ == 3)),
                         reads=["WPB", "CBT.%d.%d" % (f_, blk)], writes=[psr(bk2)])
                p.op("dve", lambda e, bk2=bk2, n=n: e.tensor_tensor(out=smt[3][:, 0:n], in0=PS[bk2][:, 0:n], in1=smt[1][:, 0:n], op=ALU.mult),
                     reads=[psr(bk2), "smt1"], writes=["smt3"])
                p.op("pool", lambda e, i=i, blk=blk, n=n: e.tensor_tensor(out=HT[:, i, bsl(blk)], in0=smt[2][:, 0:n], in1=smt[3][:, 0:n], op=ALU.add),
                     reads=["smt2", "smt3"], writes=["HT.%d.%d" % (i, blk)])

    for t in range(NT + 1):
        xr_, yo_ = xr[t % 2], yo[t % 2]
        xrn, yon = "xr%d" % (t % 2), "yo%d" % (t % 2)
        if t < NT:
            m, lo, hi, src, dst, hblk = 128, t * 128, (t + 1) * 128, x_p[t * 128:(t + 1) * 128, :], y_p[t * 128:(t + 1) * 128, :], t // 4
        else:
            m, lo, hi, src, dst, hblk = NS, T, TA, x_s, y_s, NB
        p.dma("sp", lambda e, xr_=xr_, m=m, src=src: e.dma_start(out=xr_[0:m, :], in_=src), writes=[xrn])
        for half in range(2):
            bk = gen.next()
            for kt in range(8):
                p.op("pe", lambda e, kt=kt, half=half, bk=bk, m=m, lo=lo, hi=hi: e.matmul(PS[bk][0:m, :], HT[:, kt, lo:hi], WO[:, kt, half * 512:(half + 1) * 512],
                                                                                           start=(kt == 0), stop=(kt == 7)),
                     reads=["WO", "HT.%d.%d" % (kt, hblk)], writes=[psr(bk)])
            p.op("dve", lambda e, half=half, bk=bk, xr_=xr_, m=m: e.tensor_tensor(out=xr_[0:m, half * 512:(half + 1) * 512], in0=PS[bk][0:m, :],
                                                                                  in1=xr_[0:m, half * 512:(half + 1) * 512], op=ALU.add),
                 reads=[psr(bk), xrn], writes=[xrn])
        tag = "st2_%d" % t
        rms_stats(xr_[0:m, :], m, st2[0:m, t, 0:1], st2[0:m, t, 1:2], st2[0:m, t, 2:3], [xrn], tag)
        p.op("dve", lambda e, t=t, xr_=xr_, yo_=yo_, m=m: e.scalar_tensor_tensor(out=yo_[0:m, :], in0=xr_[0:m, :], scalar=st2[0:m, t, 2:3], in1=fing_b[0:m, :],
                                                                                 op0=ALU.mult, op1=ALU.mult),
             reads=[xrn, tag + "c", "fing_b"], writes=[yon])
        p.dma("sp", lambda e, yo_=yo_, m=m, dst=dst: e.dma_start(out=dst, in_=yo_[0:m, :]), reads=[yon], is_output=True)

    return nc, p, locals()


_CACHE = {}


def kernel(x_prompt, x_sample, cache_k_cmp, cache_v_cmp, cache_k_slc, cache_v_slc,
           cache_k_win, cache_v_win, state_conv, page_table,
           ln_g, w_in, pe_k, w1_k, w2_k, pe_v, w1_v, w2_v,
           dw_k, dw_b, cln_g, cln_b, pw_w, pw_b, w_pa, w_pb, w_o, final_g, _stop=99, _sample=True):
    f = lambda a: np.ascontiguousarray(np.asarray(a, dtype=np.float32))
    import os
    if os.environ.get("KDEV_NOSAMPLE"):
        _sample = False
    nc, p, env = build_program(stop_after=_stop, do_sample=_sample)
    if "fin" not in env:
        p.finish()
    vecs = np.concatenate([f(dw_k)[0], f(dw_b), f(cln_g), f(cln_b), f(pw_b)], axis=0)
    shared = {
        "w_in": f(w_in)[0], "ln_g": f(ln_g), "final_g": f(final_g).reshape(1, D),
        "w1_k": f(w1_k)[0], "w1_v": f(w1_v)[0], "w2_k": f(w2_k)[0], "w2_v": f(w2_v)[0],
        "pe_k": f(pe_k)[0], "pe_v": f(pe_v)[0], "vecs": vecs, "pw_w": f(pw_w)[0],
        "w_pa": f(w_pa)[0], "w_pb": f(w_pb)[0], "w_o": f(w_o)[0],
    }
    in_maps = []
    xp = f(x_prompt)
    pools = [f(a)[0].reshape(5120, 16384) for a in (cache_k_cmp, cache_v_cmp, cache_k_slc, cache_v_slc)] if _sample else [None] * 4
    for c in range(8):
        m = dict(shared)
        m["x_p"] = xp[c]
        m["x_s"] = f(x_sample)[4 * c:4 * c + 4].reshape(NS, D)
        m["ckw"] = f(cache_k_win)[0, 4 * c:4 * c + 4].reshape(4, 512, 128)
        m["cvw"] = f(cache_v_win)[0, 4 * c:4 * c + 4].reshape(4, 512, 128)
        m["sconv"] = f(state_conv)[0, 4 * c:4 * c + 4]
        m["pt"] = np.ascontiguousarray(np.asarray(page_table, dtype=np.int32)[4 * c:4 * c + 4].reshape(512, 1))
        m["pk_c"] = pools[0]
        m["pv_c"] = pools[1]
        m["pk_s"] = pools[2]
        m["pv_s"] = pools[3]
        in_maps.append(m)
    names = set(env["in_names"])
    in_maps = [{k: v for k, v in m.items() if k in names} for m in in_maps]
    res = run_bass_kernel_spmd(nc, in_maps, core_ids=list(range(8)))
    R = res.results
    global _LAST
    _LAST = R

    def get(name, shape):
        if name in R[0]:
            return np.stack([np.asarray(R[c][name], dtype=np.float32).reshape(shape) for c in range(8)], 0)
        return np.zeros((8,) + tuple(shape), np.float32)

    y_prompt = get("y_p", (T, D))
    pk = [get("o_kv%d" % i, (T, 2, 64))[None] for i in range(4)]
    p_kw = get("o_kw", (512, 2, 64))[None]
    p_vw = get("o_vw", (512, 2, 64))[None]
    p_conv = get("o_conv", (30, 512))[None]
    def gets(name, shape):
        if name in R[0]:
            a = np.stack([np.asarray(R[c][name], dtype=np.float32).reshape((4,) + tuple(shape)) for c in range(8)], 0)
            return a.reshape((32,) + tuple(shape))
        return np.zeros((32,) + tuple(shape), np.float32)
    y_sample = gets("y_s", (4, D))
    sk = [gets("s_kv%d" % i, (4, 2, 64))[None] for i in range(4)]
    s_kw = gets("s_kw", (512, 2, 64))[None]
    s_vw = gets("s_vw", (512, 2, 64))[None]
    s_conv = gets("s_conv", (30, 512))[None]
    return (y_prompt, y_sample, pk[0], pk[1], pk[2], pk[3], p_kw, p_vw, p_conv,
            sk[0], sk[1], sk[2], sk[3], s_kw, s_vw, s_conv)
```

```python
import contextlib
import numpy as np
import concourse.bass as bass
import concourse.mybir as mybir
from concourse.bass_utils import run_bass_kernel_spmd

F32 = mybir.dt.float32
BF16 = mybir.dt.bfloat16
I32 = mybir.dt.int32
AF = mybir.ActivationFunctionType
ALU = mybir.AluOpType
AX = mybir.AxisListType

ENGS = ("pe", "act", "dve", "pool", "sp")
SEM_LIMIT = 30000
N_DMA_SEMS = 12


class Prog:
    def __init__(self, nc):
        self.nc = nc
        self.stack = contextlib.ExitStack()
        self.ops = {e: [] for e in ENGS}
        self.sems = {}
        self.res = {}
        self.seen = {e: {} for e in ENGS}
        self.cur = {}
        self.cnt = {}
        self.epoch = {e: 0 for e in ENGS}
        for e in ENGS:
            self._new_eng_sem(e)
        self.dma_pool = {}
        self.dma_rr = {}
        self.all_dma = []
        self.out_waits = []

    def _sem(self, name):
        h = self.stack.enter_context(self.nc.semaphore(name))
        self.sems[name] = h
        return name

    def _new_eng_sem(self, e):
        key = self._sem("s_%s_%d" % (e, self.epoch[e]))
        self.epoch[e] += 1
        self.cur[e] = key
        self.cnt[key] = 0

    def sbuf(self, name, shape, dt):
        return self.stack.enter_context(self.nc.sbuf_tensor(name, list(shape), dt))

    def psum(self, name, shape, dt):
        return self.stack.enter_context(self.nc.psum_tensor(name, list(shape), dt))

    def arena_init(self, nbytes):
        self.AR = self.sbuf("AR", [128, nbytes // 2], BF16)
        self.ar_off = 0
        self.ar_size = nbytes
        self.ar_peak = 0

    def ar(self, name, shape, dt):
        esz = 2 if dt == BF16 else 4
        n = esz
        for d in shape[1:]:
            n *= d
        n_al = (n + 63) // 64 * 64
        assert self.ar_off + n_al <= self.ar_size, ("arena overflow", name, self.ar_off, n_al, self.ar_size)
        v = self.AR[0:shape[0], self.ar_off // 2:(self.ar_off + n) // 2]
        self.ar_off += n_al
        self.ar_peak = max(self.ar_peak, self.ar_off)
        if esz == 4:
            v = v.bitcast(dt)
        if len(shape) == 3:
            v = v.rearrange("p (a b) -> p a b", a=shape[1])
        elif len(shape) == 4:
            v = v.rearrange("p (a b c) -> p a b c", a=shape[1], b=shape[2])
        return v

    def barrier(self):
        self.bar_snap = {k: v for k, v in self.cnt.items() if v > 0}
        self.bar_pending = set(ENGS)

    def arena_reset(self, keep=0):
        self.barrier()
        self.ar_off = keep

    def _need(self, eng, dep, waits):
        if dep is None:
            return
        key, val = dep
        if self.seen[eng].get(key, 0) >= val:
            return
        self.seen[eng][key] = val
        waits.append((key, val))

    def _deps(self, eng, reads, writes, is_dma):
        waits = []
        own = "s_%s_" % eng
        if getattr(self, "bar_pending", None) and eng in self.bar_pending:
            self.bar_pending.discard(eng)
            for k, v in self.bar_snap.items():
                if k.startswith(own):
                    continue
                self._need(eng, (k, v), waits)
        for r in reads:
            st = self.res.get(r)
            if st:
                self._need(eng, st[0], waits)
                if r.startswith("ps"):
                    for rd in st[1]:
                        if not rd[0].startswith(own):
                            self._need(eng, rd, waits)
        for w in writes:
            st = self.res.get(w)
            if st:
                if not (st[0] is not None and st[0][0].startswith(own) and not is_dma):
                    self._need(eng, st[0], waits)
                for rd in st[1]:
                    if rd[0].startswith(own) and not is_dma:
                        continue
                    self._need(eng, rd, waits)
        return waits

    def _mark(self, reads, writes, done):
        for r in reads:
            st = self.res.setdefault(r, [None, []])
            st[1].append(done)
            if len(st[1]) > 64:
                best = {}
                for k, v in st[1]:
                    best[k] = max(best.get(k, 0), v)
                st[1] = list(best.items())
        for w in writes:
            self.res[w] = [done, []]

    @staticmethod
    def _snap(fn):
        cl = fn.__closure__ or ()
        out = []
        for c in cl:
            try:
                out.append(id(c.cell_contents))
            except ValueError:
                out.append(None)
        return out

    def op(self, eng, fn, reads=(), writes=()):
        fn._snap = self._snap(fn)
        waits = self._deps(eng, reads, writes, False)
        key = self.cur[eng]
        self.cnt[key] += 1
        done = (key, self.cnt[key])
        self.ops[eng].append((waits, fn, key, 1))
        self._mark(reads, writes, done)
        if self.cnt[key] >= SEM_LIMIT:
            self._new_eng_sem(eng)
        return done

    def dma(self, eng, fn, reads=(), writes=(), is_output=False):
        fn._snap = self._snap(fn)
        waits = self._deps(eng, reads, writes, True)
        pool = self.dma_pool.setdefault(eng, [])
        if len(pool) < N_DMA_SEMS:
            key = self._sem("d_%s_%d" % (eng, len(pool)))
            pool.append(key)
            self.all_dma.append(key)
            self.cnt[key] = 0
            self.dma_rr[eng] = len(pool) % N_DMA_SEMS
        else:
            i = self.dma_rr[eng]
            key = pool[i]
            self.dma_rr[eng] = (i + 1) % N_DMA_SEMS
            if self.cnt[key] >= SEM_LIMIT:
                key = self._sem("d_%s_%d_%d" % (eng, i, len(self.sems)))
                pool[i] = key
                self.all_dma.append(key)
                self.cnt[key] = 0
        if self.cnt[key] > 0:
            self._need(eng, (key, self.cnt[key]), waits)
        self.cnt[key] += 16
        done = (key, self.cnt[key])
        self.ops[eng].append((waits, fn, key, 16))
        self._mark(reads, writes, done)
        if is_output:
            self.out_waits.append(done)
        return done

    def finish(self):
        fin = []
        best = {}
        for k, v in self.out_waits:
            best[k] = max(best.get(k, 0), v)
        for e in ENGS:
            for k in [kk for kk in self.cnt if kk.startswith("s_%s_" % e)]:
                if self.cnt[k] > 0:
                    best[k] = max(best.get(k, 0), self.cnt[k])
        for k in self.all_dma:
            if self.cnt[k] > 0:
                best[k] = max(best.get(k, 0), self.cnt[k])
        fin = list(best.items())
        nc = self.nc
        sems = self.sems
        ops = self.ops

        def emit(engobj, lst, final):
            for waits, fn, key, inc in lst:
                for (k, v) in waits:
                    engobj.wait_ge(sems[k], v)
                if fn._snap != self._snap(fn):
                    raise RuntimeError("late-bound closure variable changed: %s %s" % (fn.__code__.co_freevars, fn.__code__.co_firstlineno))
                inst = fn(engobj)
                inst.then_inc(sems[key], inc)
            if final:
                for (k, v) in fin:
                    engobj.wait_ge(sems[k], v)

        with nc.Block() as block:
            @block.sync
            def _(e):
                emit(e, ops["sp"], True)

            @block.tensor
            def _(e):
                emit(e, ops["pe"], False)

            @block.scalar
            def _(e):
                emit(e, ops["act"], False)

            @block.vector
            def _(e):
                emit(e, ops["dve"], False)

            @block.gpsimd
            def _(e):
                emit(e, ops["pool"], False)
        self.stack.close()


D = 1024
T = 2048
NT = T // 128
NB = T // 512
NS = 16
TA = T + NS
NBA = NB + 1
DIN = 5400
C_Q, C_KV, C_GN, C_GA, C_GLU, C_GB, C_MA, C_MB = 0, 512, 1280, 1304, 1816, 2840, 3352, 4376
BIG = 32768.0
EPS = 1e-6


class Banks:
    def __init__(self, ids):
        self.ids = list(ids)
        self.i = 0

    def next(self):
        b = self.ids[self.i % len(self.ids)]
        self.i += 1
        return b


def build_program(do_sample=True, stop_after=99):
    nc = bass.Bass("TRN2", target_bir_lowering=False)
    p = Prog(nc)

    in_names = []

    def din(name, shape, dt=F32):
        in_names.append(name)
        return nc.dram_tensor(name, list(shape), dt, kind="ExternalInput").ap()

    def dout(name, shape, dt=F32):
        return nc.dram_tensor(name, list(shape), dt, kind="ExternalOutput").ap()

    x_p = din("x_p", [T, D])
    w_in = din("w_in", [D, DIN])
    ln_g = din("ln_g", [1, D])
    final_g = din("final_g", [1, D])
    w1_k = din("w1_k", [2048, 64])
    w1_v = din("w1_v", [2048, 64])
    w2_k = din("w2_k", [64, 64])
    w2_v = din("w2_v", [64, 64])
    pe_k = din("pe_k", [32, 64])
    pe_v = din("pe_v", [32, 64])
    vecs = din("vecs", [35, 512])
    pw_w = din("pw_w", [512, 512])
    w_pa = din("w_pa", [512, D])
    w_pb = din("w_pb", [512, D])
    w_o = din("w_o", [D, D])

    x_s = din("x_s", [NS, D])
    ckw = din("ckw", [4, 512, 128])
    cvw = din("cvw", [4, 512, 128])
    sconv = din("sconv", [4, 30, 512])
    y_s = dout("y_s", [NS, D])
    s_kv = [dout("s_kv%d" % i, [NS, 128]) for i in range(4)]
    s_kw = dout("s_kw", [4, 512, 128])
    s_vw = dout("s_vw", [4, 512, 128])
    s_conv = dout("s_conv", [4, 30, 512])
    y_p = dout("y_p", [T, D])
    o_kv = [dout("o_kv%d" % i, [T, 128]) for i in range(4)]
    o_kw = dout("o_kw", [512, 128])
    o_vw = dout("o_vw", [512, 128])
    o_conv = dout("o_conv", [30, 512])

    p.arena_init(118 * 1024)

    PS = [p.psum("psb%d" % i, [128, 512], F32) for i in range(8)]

    def psr(i):
        return "ps%d" % i

    gen = Banks(range(8))

    idf = p.sbuf("idf", [128, 128], F32)
    idb = p.sbuf("idb", [128, 128], BF16)
    ones_b = p.sbuf("ones_b", [128, 128], BF16)
    p.op("pool", lambda e: e.memset(idf[:], 0.0), writes=["idf"])
    p.op("pool", lambda e: e.affine_select(out=idf[:], in_=idf[:], compare_op=ALU.not_equal, fill=1.0,
                                           base=0, pattern=[[-1, 128]], channel_multiplier=1),
         reads=["idf"], writes=["idf"])
    p.op("pool", lambda e: e.tensor_copy(out=idb[:], in_=idf[:]), reads=["idf"], writes=["idb"])
    p.op("pool", lambda e: e.memset(ones_b[:], 1.0), writes=["ones_b"])

    lng_b = p.sbuf("lng_b", [128, D], F32)
    p.dma("sp", lambda e: e.dma_start(out=lng_b[:], in_=ln_g.broadcast_to([128, D])), writes=["lng_b"])

    xnT = p.sbuf("xnT", [128, 8, TA], BF16)
    xt = [p.sbuf("xt%d" % i, [128, D], F32) for i in range(2)]
    xs = [p.sbuf("xs%d" % i, [128, D], BF16) for i in range(2)]
    sq_junk = p.sbuf("sq_junk", [128, D], BF16)
    st = p.sbuf("st", [128, NT, 4], F32)

    def rms_stats(src_ap, n, ss, tmp, rstd, reads, tag):
        p.op("act", lambda e: e.activation(out=sq_junk[0:n, :], in_=src_ap, func=AF.Square, accum_out=ss),
             reads=reads, writes=["sq_junk", tag + "a"])
        p.op("dve", lambda e: e.tensor_scalar(out=tmp, in0=ss, scalar1=1.0 / D, scalar2=EPS,
                                              op0=ALU.mult, op1=ALU.add), reads=[tag + "a"], writes=[tag + "b"])
        p.op("act", lambda e: e.activation(out=tmp, in_=tmp, func=AF.Sqrt), reads=[tag + "b"], writes=[tag + "b"])
        p.op("dve", lambda e: e.reciprocal(out=rstd, in_=tmp), reads=[tag + "b"], writes=[tag + "c"])

    for t in range(NT):
        xb_ = xt[t % 2]
        xs_ = xs[t % 2]
        rx, rs_ = "xt%d" % (t % 2), "xs%d" % (t % 2)
        p.dma("sp", lambda e, t=t, xb_=xb_: e.dma_start(out=xb_[:], in_=x_p[t * 128:(t + 1) * 128, :]), writes=[rx])
        tag = "st%d" % t
        rms_stats(xb_[:], 128, st[:, t, 0:1], st[:, t, 1:2], st[:, t, 2:3], [rx], tag)
        p.op("dve", lambda e, t=t, xb_=xb_, xs_=xs_: e.scalar_tensor_tensor(
            out=xs_[:], in0=xb_[:], scalar=st[:, t, 2:3], in1=lng_b[:], op0=ALU.mult, op1=ALU.mult),
            reads=[rx, tag + "c", "lng_b"], writes=[rs_])
        b = gen.next()
        pst = PS[b].bitcast(BF16)
        for kt in range(8):
            p.op("pe", lambda e, kt=kt, pst=pst, xs_=xs_: e.transpose(pst[:, kt * 128:(kt + 1) * 128],
                                                                     xs_[:, kt * 128:(kt + 1) * 128], idb[:]),
                 reads=[rs_, "idb"], writes=[psr(b)])
        eng = "act" if t % 2 == 0 else "dve"
        dst = xnT[:, :, t * 128:(t + 1) * 128]
        src = pst[:, :].rearrange("p (k t) -> p k t", k=8)
        if eng == "act":
            p.op("act", lambda e, dst=dst, src=src: e.copy(out=dst, in_=src), reads=[psr(b)], writes=["xnT.%d" % t])
        else:
            p.op("dve", lambda e, dst=dst, src=src: e.tensor_copy(out=dst, in_=src), reads=[psr(b)], writes=["xnT.%d" % t])

    st_s = p.sbuf("st_s", [128, 4], F32)
    p.dma("sp", lambda e: e.dma_start(out=xt[0][0:NS, :], in_=x_s), writes=["xt0"])
    rms_stats(xt[0][0:NS, :], NS, st_s[0:NS, 0:1], st_s[0:NS, 1:2], st_s[0:NS, 2:3], ["xt0"], "sts")
    p.op("dve", lambda e: e.scalar_tensor_tensor(out=xs[0][0:NS, :], in0=xt[0][0:NS, :], scalar=st_s[0:NS, 2:3], in1=lng_b[0:NS, :],
                                                 op0=ALU.mult, op1=ALU.mult), reads=["xt0", "stsc", "lng_b"], writes=["xs0"])
    b = gen.next()
    pst = PS[b].bitcast(BF16)
    for kt in range(8):
        p.op("pe", lambda e, kt=kt, pst=pst: e.transpose(pst[:, kt * 128:kt * 128 + NS], xs[0][0:NS, kt * 128:(kt + 1) * 128], idb[0:NS, 0:NS]),
             reads=["xs0", "idb"], writes=[psr(b)])
    p.op("dve", lambda e, pst=pst: e.tensor_copy(out=xnT[:, :, T:TA], in_=pst[:, :].rearrange("p (k t) -> p k t", k=8)[:, :, 0:NS]),
         reads=[psr(b)], writes=["xnT.s"])
    XN_ALL = ["xnT.%d" % t for t in range(NT)]
    if stop_after == 0:
        dbg = dout("dbg", [128, 8 * T], BF16)
        p.dma("sp", lambda e: e.dma_start(out=dbg, in_=xnT[:].rearrange("p k t -> p (k t)")), reads=XN_ALL, is_output=True)
        return nc, p, locals()

    def xn_blk(b):
        if b == NB:
            return ["xnT.s"]
        return ["xnT.%d" % t for t in range(4 * b, 4 * b + 4)]

    def bn(b):
        return NS if b == NB else 512

    def bsl(b):
        return slice(T, TA) if b == NB else slice(b * 512, (b + 1) * 512)

    NWB = 2
    wbuf = [p.sbuf("wbuf%d" % i, [128, 8, 512], BF16) for i in range(NWB)]
    wctr = [0]
    w_in_v = w_in.rearrange("(kt p) c -> p kt c", p=128)

    def load_w(c0, ncols=512, qperm=False):
        i = wctr[0] % NWB
        wctr[0] += 1
        wb = wbuf[i]
        name = "wbuf%d" % i
        if qperm:
            for h in range(4):
                for g in range(2):
                    srcv = w_in_v[:, :, g * 256 + h * 64: g * 256 + h * 64 + 64]
                    dstv = wb[:, :, h * 128 + g * 64: h * 128 + g * 64 + 64]
                    p.dma("pool", lambda e, srcv=srcv, dstv=dstv: e.dma_start(out=dstv, in_=srcv), writes=[name])
        else:
            for half in range(2):
                ks = slice(half * 4, half * 4 + 4)
                p.dma("pool", lambda e, ks=ks: e.dma_start(out=wb[:, ks, 0:ncols], in_=w_in_v[:, ks, c0:c0 + ncols]),
                      writes=[name])
        return wb, name

    def proj_fm(wb, wname, cc, blk, bank):
        for kt in range(8):
            p.op("pe", lambda e, kt=kt: e.matmul(PS[bank][:, 0:bn(blk)], wb[:, kt, cc * 128:(cc + 1) * 128],
                                                  xnT[:, kt, bsl(blk)],
                                                  start=(kt == 0), stop=(kt == 7)),
                 reads=[wname] + xn_blk(blk), writes=[psr(bank)])

    evac_rr = [0]

    def evac(out_ap, bank, writes, func=None, scale=1.0, eng=None, in_ap=None, extra_reads=(), n=512):
        src = PS[bank][:, 0:n] if in_ap is None else in_ap
        if func is not None:
            eng = "act"
        if eng is None:
            eng = "act" if evac_rr[0] % 2 == 0 else "dve"
            evac_rr[0] += 1
        rd = [psr(bank)] + list(extra_reads)
        if eng == "act":
            f = AF.Copy if func is None else func
            p.op("act", lambda e: e.activation(out=out_ap, in_=src, func=f, scale=scale), reads=rd, writes=writes)
        else:
            if scale == 1.0:
                p.op("dve", lambda e: e.tensor_copy(out=out_ap, in_=src), reads=rd, writes=writes)
            else:
                p.op("dve", lambda e: e.tensor_scalar_mul(out=out_ap, in0=src, scalar1=scale), reads=rd, writes=writes)

    W1bd = [p.ar("W1bd%d" % i, [128, 32, 128], BF16) for i in range(2)]
    kvss = p.ar("kvss", [NS, 768], F32)
    gates_s = p.ar("gates_s", [NS, 24], F32)
    VNs = p.ar("VNs", [NS, 128], BF16)
    VNw = p.ar("VNw", [NS, 128], BF16)
    QTs = p.ar("QTs", [128, 4, NS], BF16)
    KTs = p.ar("KTs", [128, 4, NS], BF16)
    keep_s = p.ar_off
    QT = p.ar("QT", [128, 4, TA], BF16)
    KT = p.ar("KT", [128, 4, TA], BF16)
    SGA = p.ar("SGA", [128, 4, TA], BF16)
    VS = p.ar("VS", [128, NT, 2, 65], BF16)
    VW = p.ar("VW", [128, NT, 2, 65], BF16)
    gates = p.ar("gates", [128, NT, 24], F32)
    kvst = [p.ar("kvst%d" % i, [128, 768], F32) for i in range(1)]
    p.op("pool", lambda e: e.memset(VS[:], 1.0), writes=["VS.ones"])
    p.op("pool", lambda e: e.memset(VW[:], 1.0), writes=["VW.ones"])

    wb, wn = load_w(0, qperm=True)
    for cc in range(4):
        for blk in range(NBA):
            bk = gen.next()
            proj_fm(wb, wn, cc, blk, bk)
            evac(QT[:, cc, bsl(blk)], bk, ["QT.%d.%d" % (cc, blk)], scale=0.125, n=bn(blk))
    if stop_after <= 0.5:
        return nc, p, locals()
    wkv1, wkv1n = load_w(C_KV)
    wkv2, wkv2n = load_w(C_KV + 512, 280)
    Cm = p.ar("Cm", [128, 512], BF16)
    Lm = p.ar("Lm", [128, 512], BF16)
    Em = p.ar("Em", [128, 16, 128], BF16)
    p.op("pool", lambda e: e.memset(Cm[:], 0.0), writes=["Cm"])
    p.op("pool", lambda e: e.affine_select(out=Cm[:], in_=Cm[:], compare_op=ALU.is_ge, fill=-BIG, base=0,
                                           pattern=[[1, 512]], channel_multiplier=-1), reads=["Cm"], writes=["Cm"])
    p.op("pool", lambda e: e.memset(Lm[:], 0.0), writes=["Lm"])
    p.op("pool", lambda e: e.affine_select(out=Lm[:], in_=Lm[:], compare_op=ALU.is_ge, fill=-BIG, base=384,
                                           pattern=[[-1, 512]], channel_multiplier=1), reads=["Lm"], writes=["Lm"])
    p.op("pool", lambda e: e.memset(Em[:], 0.0), writes=["Em"])
    p.op("pool", lambda e: e.affine_select(out=Em[0:32].rearrange("p t (a b) -> p t a b", a=2), in_=Em[0:32].rearrange("p t (a b) -> p t a b", a=2),
                                           compare_op=ALU.not_equal, fill=BIG, base=0,
                                           pattern=[[-2, 16], [-1, 2], [0, 64]], channel_multiplier=1), reads=["Em"], writes=["Em"])
    Asc = p.ar("Asc", [128, 8, 32], F32)
    Bsc = p.ar("Bsc", [128, 8, 32], F32)
    p.op("pool", lambda e: e.memset(Asc[:], 0.0), writes=["Asc"])
    p.op("pool", lambda e: e.memset(Bsc[:], 0.0), writes=["Bsc"])
    for t in range(8, 16):
        for half in range(2):
            cur = 2 * t + half
            ps_ = slice(half * 64, half * 64 + 64)
            p.op("pool", lambda e, t=t, ps_=ps_, cur=cur: e.memset(Asc[ps_, t - 8, 1:cur - 1], 1.0), reads=["Asc"], writes=["Asc"])
            p.op("pool", lambda e, t=t, ps_=ps_: e.memset(Bsc[ps_, t - 8, 0:1], 1e4), reads=["Bsc"], writes=["Bsc"])
            p.op("pool", lambda e, t=t, ps_=ps_, cur=cur: e.memset(Bsc[ps_, t - 8, cur - 1:cur + 1], 1e4), reads=["Bsc"], writes=["Bsc"])
            if cur + 1 < 32:
                p.op("pool", lambda e, t=t, ps_=ps_, cur=cur: e.memset(Bsc[ps_, t - 8, cur + 1:32], -1.0), reads=["Bsc"], writes=["Bsc"])

    for (wbx, wnx, cc, slot) in ((wkv1, wkv1n, 0, 0), (wkv1, wkv1n, 1, 1), (wkv1, wkv1n, 2, 2), (wkv2, wkv2n, 0, 3)):
        for blk in range(NBA):
            bk = gen.next()
            proj_fm(wbx, wnx, cc, blk, bk)
            evac(KT[:, slot, bsl(blk)], bk, ["KT.%d.%d" % (slot, blk)], n=bn(blk))
    if stop_after <= 0.6:
        return nc, p, locals()
    for t in range(NT):
        b1 = gen.next()
        b2 = gen.next()
        for kt in range(8):
            p.op("pe", lambda e, b1=b1, kt=kt, t=t: e.matmul(PS[b1][:, :], xnT[:, kt, t * 128:(t + 1) * 128], wkv1[:, kt, 0:512],
                                                      start=(kt == 0), stop=(kt == 7)),
                 reads=[wkv1n, "xnT.%d" % t], writes=[psr(b1)])
        for kt in range(8):
            p.op("pe", lambda e, b2=b2, kt=kt, t=t: e.matmul(PS[b2][:, 0:280], xnT[:, kt, t * 128:(t + 1) * 128], wkv2[:, kt, 0:280],
                                                      start=(kt == 0), stop=(kt == 7)),
                 reads=[wkv2n, "xnT.%d" % t], writes=[psr(b2)])
        ks_ = kvst[0]
        kn = "kvst0"
        p.op("dve", lambda e, b1=b1, ks_=ks_: e.tensor_copy(out=ks_[:, 0:512], in_=PS[b1][:, :]), reads=[psr(b1)], writes=[kn + "a"])
        p.op("act", lambda e, b2=b2, ks_=ks_: e.copy(out=ks_[:, 512:768], in_=PS[b2][:, 0:256]), reads=[psr(b2)], writes=[kn + "b"])
        p.op("act", lambda e, b2=b2, t=t: e.activation(out=gates[:, t, :], in_=PS[b2][:, 256:280], func=AF.Sigmoid),
             reads=[psr(b2)], writes=["gates.%d" % t])
        p.op("dve", lambda e, b1=b1, t=t: e.tensor_copy(out=VS[:, t, :, 0:64], in_=PS[b1][:, 384:512].rearrange("p (g d) -> p g d", g=2)),
             reads=[psr(b1), "VS.ones"], writes=["VS.%d" % t])
        p.op("dve", lambda e, b2=b2, t=t: e.tensor_copy(out=VW[:, t, :, 0:64], in_=PS[b2][:, 128:256].rearrange("p (g d) -> p g d", g=2)),
             reads=[psr(b2), "VW.ones"], writes=["VW.%d" % t])
        for i in range(4):
            p.dma("sp", lambda e, i=i, t=t, ks_=ks_: e.dma_start(out=o_kv[i][t * 128:(t + 1) * 128, :], in_=ks_[:, i * 128:(i + 1) * 128]),
                  reads=[kn + "a"], is_output=True)
        if t >= NT - 4:
            r0 = (t - (NT - 4)) * 128
            p.dma("sp", lambda e, r0=r0, ks_=ks_: e.dma_start(out=o_kw[r0:r0 + 128, :], in_=ks_[:, 512:640]), reads=[kn + "b"], is_output=True)
            p.dma("sp", lambda e, r0=r0, ks_=ks_: e.dma_start(out=o_vw[r0:r0 + 128, :], in_=ks_[:, 640:768]), reads=[kn + "b"], is_output=True)
    b1 = gen.next()
    b2 = gen.next()
    for kt in range(8):
        p.op("pe", lambda e, b1=b1, kt=kt: e.matmul(PS[b1][0:NS, :], xnT[:, kt, T:TA], wkv1[:, kt, 0:512], start=(kt == 0), stop=(kt == 7)),
             reads=[wkv1n, "xnT.s"], writes=[psr(b1)])
    for kt in range(8):
        p.op("pe", lambda e, b2=b2, kt=kt: e.matmul(PS[b2][0:NS, 0:280], xnT[:, kt, T:TA], wkv2[:, kt, 0:280], start=(kt == 0), stop=(kt == 7)),
             reads=[wkv2n, "xnT.s"], writes=[psr(b2)])
    p.op("dve", lambda e, b1=b1: e.tensor_copy(out=kvss[:, 0:512], in_=PS[b1][0:NS, :]), reads=[psr(b1)], writes=["kvss.a"])
    p.op("dve", lambda e, b1=b1: e.tensor_copy(out=VNs[:, :], in_=PS[b1][0:NS, 384:512]), reads=[psr(b1)], writes=["VNs"])
    p.op("act", lambda e, b2=b2: e.copy(out=kvss[:, 512:768], in_=PS[b2][0:NS, 0:256]), reads=[psr(b2)], writes=["kvss.b"])
    p.op("act", lambda e, b2=b2: e.copy(out=VNw[:, :], in_=PS[b2][0:NS, 128:256]), reads=[psr(b2)], writes=["VNw"])
    p.op("act", lambda e, b2=b2: e.activation(out=gates_s[:, :], in_=PS[b2][0:NS, 256:280], func=AF.Sigmoid), reads=[psr(b2)], writes=["gates_s"])
    for i in range(4):
        p.dma("sp", lambda e, i=i: e.dma_start(out=s_kv[i][:, :], in_=kvss[:, i * 128:(i + 1) * 128]), reads=["kvss.a"], is_output=True)
    for sq in range(4):
        p.dma("sp", lambda e, sq=sq: e.dma_start(out=s_kw[sq, 0:508, :], in_=ckw[sq, 4:512, :]), is_output=True)
        p.dma("sp", lambda e, sq=sq: e.dma_start(out=s_vw[sq, 0:508, :], in_=cvw[sq, 4:512, :]), is_output=True)
        p.dma("sp", lambda e, sq=sq: e.dma_start(out=s_kw[sq, 508:512, :], in_=kvss[4 * sq:4 * sq + 4, 512:640]), reads=["kvss.b"], is_output=True)
        p.dma("sp", lambda e, sq=sq: e.dma_start(out=s_vw[sq, 508:512, :], in_=kvss[4 * sq:4 * sq + 4, 640:768]), reads=["kvss.b"], is_output=True)
    if stop_after <= 0.7:
        return nc, p, locals()
    wb, wn = load_w(C_GA)
    for cc in range(4):
        for blk in range(NBA):
            bk = gen.next()
            proj_fm(wb, wn, cc, blk, bk)
            evac(SGA[:, cc, bsl(blk)], bk, ["SGA.%d.%d" % (cc, blk)], func=AF.Silu, n=bn(blk))


    if stop_after <= 1:
        return nc, p, locals()

    W2bd = [p.sbuf("W2bd%d" % i, [128, 128], BF16) for i in range(2)]
    PET = [p.sbuf("PET%d" % i, [128, 32], BF16) for i in range(2)]
    H0 = [p.sbuf("H0_%d" % i, [128, 1], F32) for i in range(2)]
    pen = p.sbuf("pen", [32, 128], F32)
    for i, (w1, w2, pe) in enumerate(((w1_k, w2_k, pe_k), (w1_v, w2_v, pe_v))):
        p.op("pool", lambda e, i=i: e.memset(W1bd[i][:], 0.0), writes=["W1bd%d" % i])
        p.op("pool", lambda e, i=i: e.memset(W2bd[i][:], 0.0), writes=["W2bd%d" % i])
        w1v = w1.rearrange("(j d) h -> d j h", d=64)
        for g in range(2):
            p.dma("pool", lambda e, i=i, g=g, w1v=w1v: e.dma_start(out=W1bd[i][g * 64:(g + 1) * 64, :, g * 64:(g + 1) * 64], in_=w1v),
                  writes=["W1bd%d" % i])
            p.dma("pool", lambda e, i=i, g=g, w2=w2: e.dma_start(out=W2bd[i][g * 64:(g + 1) * 64, g * 64:(g + 1) * 64], in_=w2),
                  writes=["W2bd%d" % i])
            p.dma("sp", lambda e, g=g, pe=pe: e.dma_start(out=pen[:, g * 64:(g + 1) * 64], in_=pe), writes=["pen"])
        bk = gen.next()
        p.op("pe", lambda e, bk=bk: e.transpose(PS[bk][:, 0:32], pen[:, :], idf[0:32, 0:32]), reads=["pen", "idf"], writes=[psr(bk)])
        p.op("dve", lambda e, bk=bk, i=i: e.tensor_copy(out=PET[i][:], in_=PS[bk][:, 0:32]), reads=[psr(bk)], writes=["PET%d" % i])
        bk = gen.next()
        for j in range(32):
            p.op("pe", lambda e, bk=bk, i=i, j=j: e.matmul(PS[bk][:, 0:1], W1bd[i][:, j, :], PET[i][:, j:j + 1], start=(j == 0), stop=(j == 31)),
                 reads=["W1bd%d" % i, "PET%d" % i], writes=[psr(bk)])
        p.op("dve", lambda e, bk=bk, i=i: e.tensor_copy(out=H0[i][:], in_=PS[bk][:, 0:1]), reads=[psr(bk)], writes=["H0_%d" % i])

    kcmpT = p.sbuf("kcmpT", [128, 128], BF16)
    vaug = p.sbuf("vaug", [128, 2, 97], BF16)
    hs = p.sbuf("hs", [128, 128], BF16)
    aggf = p.sbuf("aggf", [128, 32], F32)
    p.op("pool", lambda e: e.memset(aggf[:], 1.0), writes=["aggf"])
    p.op("pool", lambda e: e.affine_select(out=aggf[:], in_=aggf[:], compare_op=ALU.is_ge, fill=0.0, base=1,
                                           pattern=[[-4, 32]], channel_multiplier=1), reads=["aggf"], writes=["aggf"])
    p.op("pool", lambda e: e.affine_select(out=aggf[:], in_=aggf[:], compare_op=ALU.is_ge, fill=0.0, base=3,
                                           pattern=[[4, 32]], channel_multiplier=-1), reads=["aggf"], writes=["aggf"])
    p.op("pool", lambda e: e.memset(vaug[:], 1.0), writes=["vaug"])
    for g in range(2):
        p.op("pool", lambda e, g=g: e.tensor_copy(out=vaug[:, g, 65:97], in_=aggf[:]), reads=["aggf", "vaug"], writes=["vaug"])

    KT_ALL = lambda slot: ["KT.%d.%d" % (slot, b) for b in range(NB)]
    NCMP = 127
    for i in range(2):
        bk = gen.next()
        srcv = KT[:, i, :].rearrange("p (c j) -> p j c", j=16)
        for j in range(32):
            r, jj = j // 16, j % 16
            p.op("pe", lambda e, bk=bk, i=i, j=j, r=r, jj=jj, srcv=srcv: e.matmul(
                PS[bk][:, 0:NCMP], W1bd[i][:, j, :], srcv[:, jj, r:r + NCMP], start=(j == 0), stop=(j == 31)),
                reads=["W1bd%d" % i] + KT_ALL(i), writes=[psr(bk)])
        p.op("act", lambda e, bk=bk, i=i: e.activation(out=hs[:, 0:NCMP], in_=PS[bk][:, 0:NCMP], func=AF.Silu, bias=H0[i][:, 0:1]),
             reads=[psr(bk), "H0_%d" % i], writes=["hs"])
        bk2 = gen.next()
        if i == 0:
            p.op("pe", lambda e, bk2=bk2: e.matmul(PS[bk2][:, 0:NCMP], W2bd[0][:, :], hs[:, 0:NCMP], start=True, stop=True),
                 reads=["W2bd0", "hs"], writes=[psr(bk2)])
            p.op("dve", lambda e, bk2=bk2: e.tensor_copy(out=kcmpT[:, 0:NCMP], in_=PS[bk2][:, 0:NCMP]), reads=[psr(bk2)], writes=["kcmpT"])
        else:
            p.op("pe", lambda e, bk2=bk2: e.matmul(PS[bk2][0:NCMP, 0:128], hs[:, 0:NCMP], W2bd[1][:, :], start=True, stop=True),
                 reads=["W2bd1", "hs"], writes=[psr(bk2)])
            p.op("dve", lambda e, bk2=bk2: e.tensor_copy(out=vaug[0:NCMP, :, 0:64], in_=PS[bk2][0:NCMP, 0:128].rearrange("p (g d) -> p g d", g=2)),
                 reads=[psr(bk2), "vaug"], writes=["vaug"])

    if stop_after <= 2:
        dbg = dout("dbg", [128, 128], BF16)
        dbg2 = dout("dbg2", [128, 194], BF16)
        p.dma("sp", lambda e: e.dma_start(out=dbg, in_=kcmpT[:]), reads=["kcmpT"], is_output=True)
        p.dma("sp", lambda e: e.dma_start(out=dbg2, in_=vaug[:].rearrange("p g c -> p (g c)")), reads=["vaug"], is_output=True)
        return nc, p, locals()

    mcmp = p.ar("mcmp", [128, 512], BF16)
    PTb = [p.ar("PT%d" % i, [128, 512], BF16) for i in range(4)]
    Oacc = p.ar("Oacc", [128, 4, 512], F32)
    Obf = p.ar("Obf", [128, 4, 512], BF16)
    imp = p.ar("imp", [128, 4, 2, 32], F32)
    MT = p.ar("MT", [128, 2, 512], BF16)
    QZ = p.ar("QZ", [128, 8, 512], BF16)
    p.op("pool", lambda e: e.memset(MT[:], 0.0), writes=["MT.0", "MT.1"])
    p.op("pool", lambda e: e.memset(QZ[:], 0.0), writes=["QZ.%d" % h_ for h_ in range(8)])
    OAT = p.sbuf("OAT", [128, 4, TA], BF16)
    ep = [p.sbuf("ep%d" % i, [128, 16], F32) for i in range(2)]
    eptmp = [p.sbuf("eptmp%d" % i, [128, 4, 64], F32) for i in range(2)]
    sc = p.sbuf("sc", [128, 32], F32)
    scr = p.sbuf("scr", [128, 32], F32)
    m16 = p.sbuf("m16", [128, 16], F32)
    msel = p.sbuf("msel", [128, 32], BF16)
    sb_S = Banks([0, 1, 4])
    sb_O = Banks([2, 3])
    gen2 = Banks([5, 6, 7])
    ptc = [0]
    epc = [0]

    for b in range(NB):
        qs = slice(b * 512, (b + 1) * 512)
        gate_r = ["gates.%d" % t for t in range(4 * b, 4 * b + 4)]
        p.op("pool", lambda e: e.memset(mcmp[:], 0.0), writes=["mcmp"])
        p.op("pool", lambda e, b=b: e.affine_select(out=mcmp[:], in_=mcmp[:], compare_op=ALU.is_ge, fill=-BIG, base=512 * b - 31,
                                                    pattern=[[1, 512]], channel_multiplier=-16), reads=["mcmp"], writes=["mcmp"])
        for h_ in range(8):
            cp_, hh_ = h_ % 4, h_ // 4
            psl_ = slice(hh_ * 64, hh_ * 64 + 64)
            eng_ = "pool" if h_ % 2 == 0 else "dve"
            p.op(eng_, lambda e, h_=h_, cp_=cp_, psl_=psl_, b=b: e.tensor_copy(out=QZ[psl_, h_, :], in_=QT[psl_, cp_, b * 512:(b + 1) * 512]),
                 reads=["QT.%d.%d" % (cp_, b), "QZ.%d" % h_], writes=["QZ.%d" % h_])
        def run_tiles(tiles):
            sbk = [None] * len(tiles)
            def issue_S(n):
                sbk[n] = sb_S.next()
                tiles[n]["S"](sbk[n])
            for n0 in range(min(2, len(tiles))):
                issue_S(n0)
            for n, tl in enumerate(tiles):
                if n + 2 < len(tiles):
                    issue_S(n + 2)
                pi = ptc[0] % 4
                ptc[0] += 1
                lo, hi = tl["cols"]
                nr = tl["rows"]
                bs = sbk[n]
                p.op("act", lambda e, pi=pi, lo=lo, hi=hi, nr=nr, bs=bs: e.activation(out=PTb[pi][0:nr, lo:hi], in_=PS[bs][0:nr, lo:hi], func=AF.Exp),
                     reads=[psr(bs)], writes=["PT%d" % pi])
                tl["PV"](pi)
                if tl.get("epi"):
                    tl["epi"]()

        def head_info(cp, hh):
            h = cp + 4 * hh
            return h, hh, slice(hh * 64, hh * 64 + 64)

        tiles = []
        for cp in range(4):
            for hh in range(2):
                h, g, psl = head_info(cp, hh)
                bo = sb_O.next()

                def S(bs, cp=cp, psl=psl, b=b, h=h):
                    p.op("pe", lambda e: e.matmul(PS[bs][0:NCMP, :], kcmpT[:, 0:NCMP], QZ[:, h, :], start=True, stop=False),
                         reads=["kcmpT", "QZ.%d" % h], writes=[psr(bs)])
                    p.op("pe", lambda e: e.matmul(PS[bs][0:NCMP, :], idb[0:NCMP, 0:NCMP], mcmp[0:NCMP, :], start=False, stop=True),
                         reads=["idb", "mcmp"], writes=[psr(bs)])

                def PV(pi, g=g, bo=bo):
                    for qt in range(4):
                        p.op("pe", lambda e, qt=qt: e.matmul(PS[bo][:, qt * 97:(qt + 1) * 97], PTb[pi][0:NCMP, qt * 128:(qt + 1) * 128],
                                                              vaug[0:NCMP, g, :], start=(qt == 0), stop=True, skip_group_check=True),
                             reads=["PT%d" % pi, "vaug"], writes=[psr(bo)])

                def epi(h=h, g=g, bo=bo, b=b, first_in_group=(cp == 0)):
                    k = epc[0] % 2
                    epc[0] += 1
                    Ov = PS[bo][:, 0:388].rearrange("p (q c) -> p q c", q=4)
                    e_, et = ep[k], eptmp[k]
                    en, etn = "ep%d" % k, "eptmp%d" % k
                    p.op("dve", lambda e: e.tensor_scalar_max(out=e_[:, 0:4].unsqueeze(2), in0=Ov[:, :, 64:65], scalar1=1e-30), reads=[psr(bo)], writes=[en])
                    p.op("dve", lambda e: e.reciprocal(out=e_[:, 4:8], in_=e_[:, 0:4]), reads=[en], writes=[en])
                    p.op("dve", lambda e: e.tensor_tensor(out=e_[:, 8:12], in0=e_[:, 4:8], in1=gates[:, 4 * b:4 * b + 4, 3 * h + 0], op=ALU.mult),
                         reads=[en] + gate_r, writes=[en])
                    p.op("dve", lambda e: e.tensor_tensor(out=Oacc[:, :, h * 64:(h + 1) * 64], in0=Ov[:, :, 0:64],
                                                          in1=e_[:, 8:12].unsqueeze(2).to_broadcast([128, 4, 64]), op=ALU.mult),
                         reads=[psr(bo), en], writes=["Oacc.%d" % h])
                    if first_in_group:
                        p.op("dve", lambda e: e.tensor_tensor(out=imp[:, :, g, :], in0=Ov[:, :, 65:97],
                                                              in1=e_[:, 4:8].unsqueeze(2).to_broadcast([128, 4, 32]), op=ALU.mult),
                             reads=[psr(bo), en], writes=["imp.%d" % g])
                    else:
                        p.op("dve", lambda e: e.tensor_tensor(out=et[:, :, 0:32], in0=Ov[:, :, 65:97],
                                                              in1=e_[:, 4:8].unsqueeze(2).to_broadcast([128, 4, 32]), op=ALU.mult),
                             reads=[psr(bo), en], writes=[etn])
                        p.op("dve", lambda e: e.tensor_tensor(out=imp[:, :, g, :], in0=imp[:, :, g, :], in1=et[:, :, 0:32], op=ALU.add),
                             reads=[etn, "imp.%d" % g], writes=["imp.%d" % g])

                tiles.append(dict(S=S, rows=NCMP, cols=(0, 512), PV=PV, epi=epi))
        run_tiles(tiles)

        use_sel = b >= 2
        if use_sel:
            for qt in range(4):
                t = 4 * b + qt
                for g in range(2):
                    p.op("dve", lambda e, qt=qt, g=g, t=t: e.tensor_tensor(out=sc[:], in0=imp[:, qt, g, :], in1=Asc[:, t - 8, :], op=ALU.mult),
                         reads=["imp.%d" % g, "Asc"], writes=["sc"])
                    p.op("dve", lambda e, t=t: e.tensor_tensor(out=sc[:], in0=sc[:], in1=Bsc[:, t - 8, :], op=ALU.add), reads=["sc", "Bsc"], writes=["sc"])
                    p.op("dve", lambda e: e.max(out=m16[:, 0:8], in_=sc[:]), reads=["sc"], writes=["m16a"])
                    p.op("dve", lambda e: e.match_replace(out=scr[:], in_to_replace=m16[:, 0:8], in_values=sc[:], imm_value=-1e9),
                         reads=["sc", "m16a"], writes=["scr"])
                    p.op("dve", lambda e: e.max(out=m16[:, 8:16], in_=scr[:]), reads=["scr"], writes=["m16b"])
                    p.op("dve", lambda e: e.tensor_scalar(out=msel[:], in0=sc[:], scalar1=m16[:, 15:16], scalar2=1.0, op0=ALU.is_ge, op1=ALU.subtract),
                         reads=["sc", "m16b"], writes=["msel"])
                    bk = gen2.next()
                    pst = PS[bk].bitcast(BF16)
                    p.op("pe", lambda e, pst=pst, bk=bk: e.transpose(pst[0:32, 0:128], msel[:, :], idb[:]), reads=["msel", "idb"], writes=[psr(bk)])
                    p.op("dve", lambda e, pst=pst, g=g, qt=qt: e.tensor_copy(out=MT[0:32, g, qt * 128:(qt + 1) * 128], in_=pst[0:32, 0:128]),
                         reads=[psr(bk)], writes=["MT.%d" % g])

        tiles = []
        for cp in range(4):
            for hh in range(2):
                h, g, psl = head_info(cp, hh)
                for br, slot, Vt, vname in ((1, 2, VS, "VS"), (2, 3, VW, "VW")):
                    bo = sb_O.next()
                    if br == 1:
                        kts = list(range(0, 4 * b + 4))
                    else:
                        kts = [kt for kt in range(4 * b - 4, 4 * b + 4) if kt >= 0]
                    state = {"first": True}
                    for kt in kts:
                        i = kt - 4 * b
                        if i >= 0:
                            lo, hi = 128 * i, 512
                            qts = list(range(i, 4))
                            mask = ("C", 0, 512 - 128 * i)
                        elif br == 2:
                            lo, hi = 0, 128 * (5 + i)
                            qts = list(range(0, 5 + i))
                            c0 = -128 * i - 128
                            mask = ("L", c0, c0 + hi)
                        else:
                            lo, hi = 0, 512
                            qts = list(range(4))
                            mask = None
                        selm = (br == 1 and use_sel)

                        def S(bs, cp=cp, psl=psl, b=b, slot=slot, kt=kt, lo=lo, hi=hi, mask=mask, selm=selm, g=g, h=h):
                            last = (mask is None and not selm)
                            p.op("pe", lambda e: e.matmul(PS[bs][:, lo:hi], KT[:, slot, kt * 128:(kt + 1) * 128],
                                                          QZ[:, h, lo:hi], start=True, stop=last),
                                 reads=["KT.%d.%d" % (slot, kt // 4), "QZ.%d" % h], writes=[psr(bs)])
                            if selm:
                                p.op("pe", lambda e: e.matmul(PS[bs][:, lo:hi], Em[:, kt, :], MT[:, g, lo:hi], start=False, stop=(mask is None)),
                                     reads=["Em", "MT.%d" % g], writes=[psr(bs)])
                            if mask is not None:
                                mt = Cm if mask[0] == "C" else Lm
                                p.op("pe", lambda e: e.matmul(PS[bs][:, lo:hi], idb[:, :], mt[:, mask[1]:mask[2]], start=False, stop=True),
                                     reads=["idb", "Cm", "Lm"], writes=[psr(bs)])

                        def PV(pi, kt=kt, qts=qts, g=g, bo=bo, Vt=Vt, vname=vname, state=state, b=b):
                            for qt in qts:
                                first = state["first"]
                                state["first"] = False
                                p.op("pe", lambda e, qt=qt, first=first: e.matmul(PS[bo][:, qt * 65:(qt + 1) * 65], PTb[pi][:, qt * 128:(qt + 1) * 128],
                                                                                   Vt[:, kt, g, :], start=first, stop=(kt == 4 * b + qt), skip_group_check=True),
                                     reads=["PT%d" % pi, "%s.%d" % (vname, kt)], writes=[psr(bo)])

                        epi = None
                        if kt == kts[-1]:
                            def epi(h=h, bo=bo, b=b, br=br):
                                k = epc[0] % 2
                                epc[0] += 1
                                Ov = PS[bo][:, 0:260].rearrange("p (q c) -> p q c", q=4)
                                e_, et = ep[k], eptmp[k]
                                en, etn = "ep%d" % k, "eptmp%d" % k
                                p.op("dve", lambda e: e.reciprocal(out=e_[:, 4:8].unsqueeze(2), in_=Ov[:, :, 64:65]), reads=[psr(bo)], writes=[en])
                                p.op("dve", lambda e: e.tensor_tensor(out=e_[:, 8:12], in0=e_[:, 4:8], in1=gates[:, 4 * b:4 * b + 4, 3 * h + br], op=ALU.mult),
                                     reads=[en] + gate_r, writes=[en])
                                p.op("dve", lambda e: e.tensor_tensor(out=et[:, :, :], in0=Ov[:, :, 0:64],
                                                                      in1=e_[:, 8:12].unsqueeze(2).to_broadcast([128, 4, 64]), op=ALU.mult),
                                     reads=[psr(bo), en], writes=[etn])
                                p.op("pool", lambda e: e.tensor_tensor(out=Oacc[:, :, h * 64:(h + 1) * 64], in0=Oacc[:, :, h * 64:(h + 1) * 64],
                                                                       in1=et[:, :, :], op=ALU.add),
                                     reads=[etn, "Oacc.%d" % h], writes=["Oacc.%d" % h])
                        tiles.append(dict(S=S, rows=128, cols=(lo, hi), PV=PV, epi=epi))
        run_tiles(tiles)

        p.op("act", lambda e: e.copy(out=Obf[:], in_=Oacc[:]), reads=["Oacc.%d" % h for h in range(8)], writes=["Obf"])
        for fc in range(4):
            bk = gen2.next()
            pst = PS[bk].bitcast(BF16)
            for qt in range(4):
                p.op("pe", lambda e, pst=pst, qt=qt, fc=fc: e.transpose(pst[:, qt * 128:(qt + 1) * 128], Obf[:, qt, fc * 128:(fc + 1) * 128], idb[:]),
                     reads=["Obf", "idb"], writes=[psr(bk)])
            p.op("dve", lambda e, pst=pst, fc=fc, b=b: e.tensor_tensor(out=OAT[:, fc, b * 512:(b + 1) * 512], in0=pst[:, 0:512],
                                                                       in1=SGA[:, fc, b * 512:(b + 1) * 512], op=ALU.mult),
                 reads=[psr(bk), "SGA.%d.%d" % (fc, b)], writes=["OAT.%d.%d" % (fc, b)])

    if not do_sample:
        p.op("pool", lambda e: e.memset(OAT[:, :, T:TA], 0.0), writes=["OAT.%d.%d" % (fc, NB) for fc in range(4)])
    else:
        U32 = mybir.dt.uint32
        pt = din("pt", [512, 1], I32)
        pk_c = din("pk_c", [5120, 16384])
        pv_c = din("pv_c", [5120, 16384])
        pk_s = din("pk_s", [5120, 16384]).rearrange("n (h x) -> (n h) x", h=2)
        pv_s = din("pv_s", [5120, 16384]).rearrange("n (h x) -> (n h) x", h=2)
        scr_idx = nc.dram_tensor("scr_idx", [4, 128, 2], I32).ap()
        p.op("dve", lambda e: e.tensor_copy(out=QTs[:], in_=QT[:, :, T:TA]), reads=["QT.%d.%d" % (c_, NB) for c_ in range(4)], writes=["QTs"])
        p.op("dve", lambda e: e.tensor_copy(out=KTs[:], in_=KT[:, :, T:TA]), reads=["KT.%d.%d" % (c_, NB) for c_ in range(4)], writes=["KTs"])
        SGAs = p.sbuf("SGAs", [128, 4, NS], BF16)
        p.op("dve", lambda e: e.tensor_copy(out=SGAs[:], in_=SGA[:, :, T:TA]), reads=["SGA.%d.%d" % (c_, NB) for c_ in range(4)], writes=["SGAs"])
        p.arena_reset(keep=keep_s)
        NG = 3
        Gt = [p.ar("Gt%d" % i, [128, 2048], BF16) for i in range(NG)]
        gtc = [0]
        BGB = p.ar("BGB", [128, 16384], BF16)
        KTr = BGB.rearrange("p (c j q) -> p c j q", c=8, j=16)
        KsT = p.ar("KsT", [128, 64, 128], BF16)
        VGb = p.ar("VGb", [128, 8192], BF16)
        hs_s = p.ar("hs_s", [128, 1024], BF16)
        kcT_s = p.ar("kcT_s", [128, 1024], BF16)
        vc_s = p.ar("vc_s", [128, 8, 128], BF16)
        aggS = p.ar("aggS", [128, 8, 257], BF16)
        PTc = p.ar("PTc", [128, 8, 64], BF16)
        PTs = [p.ar("PTs%d" % i, [128, 8, 64], BF16) for i in range(2)]
        PTw = p.ar("PTw", [128, 4, 64], BF16)
        PTn = [p.ar("PTn%d" % i, [NS, 64], BF16) for i in range(2)]
        Kwn = p.ar("Kwn", [128, 4, 128], BF16)
        KwT = p.ar("KwT", [128, 4, 128], BF16)
        Vw_s = p.ar("Vw_s", [128, 4, 128], BF16)
        imp_n = p.ar("imp_n", [64, 257], F32)
        scs2 = imp_n[0:8, :]
        scs = p.ar("scs", [8, 257], F32)
        m16s = p.ar("m16s", [8, 16], F32)
        i16 = p.ar("i16", [8, 16], U32)
        idxw = p.ar("idxw", [8, 16, 2], I32)
        idxp = p.ar("idxp", [128, 2], I32)
        pgid = p.ar("pgid", [128, 1], I32)
        hpx = p.ar("hpx", [128, 1], I32)
        pti = p.ar("pti", [128, 1], I32)
        maskS = p.ar("maskS", [128, 64], BF16)
        maskW = p.ar("maskW", [128, 64], BF16)
        maskN = p.ar("maskN", [NS, 4, 64], BF16)
        SelE = p.ar("SelE", [64, 16], F32)
        SelO = p.ar("SelO", [64, 16], F32)
        Hsum = p.ar("Hsum", [64, 8], F32)
        pmask = p.ar("pmask", [128, 1], F32)
        Qz = p.ar("Qz", [128, 64], BF16)
        Osamp = p.ar("Osamp", [64, 64], F32)
        OW = p.ar("OW", [64, 128], F32)
        gcol = p.ar("gcol", [64, 3], F32)
        eps_ = p.ar("eps_", [64, 8], F32)
        sO = Banks([0, 1])
        gs6 = Banks([2, 3, 4, 5, 6, 7])

        def build_sample_consts():
            for cc in range(8):
                p.op("pool", lambda e, cc=cc: e.memset(aggS[:, cc, :], 1.0), reads=["aggS"], writes=["aggS"])
                p.op("pool", lambda e, cc=cc: e.affine_select(out=aggS[:, cc, :], in_=aggS[:, cc, :], compare_op=ALU.is_ge, fill=0.0, base=cc + 1,
                                                             pattern=[[-4, 257]], channel_multiplier=8), reads=["aggS"], writes=["aggS"])
                p.op("pool", lambda e, cc=cc: e.affine_select(out=aggS[:, cc, :], in_=aggS[:, cc, :], compare_op=ALU.is_ge, fill=0.0, base=3 - cc,
                                                             pattern=[[4, 257]], channel_multiplier=-8), reads=["aggS"], writes=["aggS"])
            p.op("pool", lambda e: e.memset(pmask[:], 1.0), writes=["pmask"])
            p.op("pool", lambda e: e.affine_select(out=pmask[:], in_=pmask[:], compare_op=ALU.not_equal, fill=0.0, base=-127,
                                                   pattern=[[0, 1]], channel_multiplier=1), reads=["pmask"], writes=["pmask"])
            for tl in (Hsum, SelE, SelO):
                p.op("pool", lambda e, tl=tl: e.memset(tl[:], 0.0), writes=["cst"])
            for g in range(2):
                for hq in range(4):
                    r0 = g * 32 + hq * 4
                    p.op("pool", lambda e, g=g, r0=r0: e.affine_select(out=Hsum[:, g * 4:g * 4 + 4], in_=Hsum[:, g * 4:g * 4 + 4], compare_op=ALU.not_equal, fill=1.0,
                                                                     base=-r0, pattern=[[-1, 4]], channel_multiplier=1), reads=["cst"], writes=["cst"])
                    fc = g * 2 + hq // 2
                    tl = SelE if hq % 2 == 0 else SelO
                    p.op("pool", lambda e, tl=tl, fc=fc, r0=r0: e.affine_select(out=tl[:, fc * 4:fc * 4 + 4], in_=tl[:, fc * 4:fc * 4 + 4], compare_op=ALU.not_equal, fill=1.0,
                                                                                base=-r0, pattern=[[-1, 4]], channel_multiplier=1), reads=["cst"], writes=["cst"])
            p.op("pool", lambda e: e.memset(maskS[:], 0.0), writes=["maskS"])
            mS4 = maskS.rearrange("p (g h t) -> p g h t", g=2, h=8)
            for g in range(2):
                for t_ in range(4):
                    cb = g * 4 + t_
                    v = mS4[:, g, 0:4, t_]
                    p.op("pool", lambda e, v=v: e.memset(v, 1.0), reads=["maskS"], writes=["maskS"])
                    p.op("pool", lambda e, v=v, cb=cb: e.affine_select(out=v, in_=v, compare_op=ALU.is_ge, fill=0.0, base=-cb * 16, pattern=[[0, 4]], channel_multiplier=1),
                         reads=["maskS"], writes=["maskS"])
                    p.op("pool", lambda e, v=v, cb=cb: e.affine_select(out=v, in_=v, compare_op=ALU.is_ge, fill=0.0, base=cb * 16 + 14, pattern=[[0, 4]], channel_multiplier=-1),
                         reads=["maskS"], writes=["maskS"])
            p.op("pool", lambda e: e.memset(maskW[:], 1.0), writes=["maskW"])
            p.op("pool", lambda e: e.affine_select(out=maskW.rearrange("p (a t) -> p a t", t=4), in_=maskW.rearrange("p (a t) -> p a t", t=4), compare_op=ALU.is_ge, fill=0.0,
                                                   base=0, pattern=[[0, 16], [-1, 4]], channel_multiplier=1), reads=["maskW"], writes=["maskW"])
            p.op("pool", lambda e: e.memset(maskN[:], 1.0), writes=["maskN"])
            for sq in range(4):
                v = maskN[:, sq, :].rearrange("p (a t) -> p a t", t=4)
                p.op("pool", lambda e, v=v, sq=sq: e.affine_select(out=v, in_=v, compare_op=ALU.is_ge, fill=0.0, base=-4 * sq, pattern=[[0, 16], [0, 4]], channel_multiplier=1),
                     reads=["maskN"], writes=["maskN"])
                p.op("pool", lambda e, v=v, sq=sq: e.affine_select(out=v, in_=v, compare_op=ALU.is_ge, fill=0.0, base=4 * sq + 3, pattern=[[0, 16], [0, 4]], channel_multiplier=-1),
                     reads=["maskN"], writes=["maskN"])
                p.op("pool", lambda e, v=v, sq=sq: e.affine_select(out=v, in_=v, compare_op=ALU.is_ge, fill=0.0, base=4 * sq, pattern=[[0, 16], [1, 4]], channel_multiplier=-1),
                     reads=["maskN"], writes=["maskN"])


        OATS = ["OAT.%d.%d" % (fc, NB) for fc in range(4)]
        evs = [0]

        def evac2(out_ap, in_ap, bank, writes, reads=()):
            eng = "act" if evs[0] % 2 == 0 else "dve"
            evs[0] += 1
            if eng == "act":
                p.op("act", lambda e: e.copy(out=out_ap, in_=in_ap), reads=[psr(bank)] + list(reads), writes=writes)
            else:
                p.op("dve", lambda e: e.tensor_copy(out=out_ap, in_=in_ap), reads=[psr(bank)] + list(reads), writes=writes)

        def branch_epilogue(bo, br, first):
            p.op("dve", lambda e: e.tensor_scalar_max(out=eps_[:, 0:1], in0=PS[bo][0:64, 128:129], scalar1=1e-30), reads=[psr(bo)], writes=["eps_"])
            p.op("dve", lambda e: e.reciprocal(out=eps_[:, 1:2], in_=eps_[:, 0:1]), reads=["eps_"], writes=["eps_"])
            p.op("dve", lambda e: e.tensor_tensor(out=eps_[:, 2:3], in0=eps_[:, 1:2], in1=gcol[:, br:br + 1], op=ALU.mult), reads=["eps_", "gcol"], writes=["eps_"])
            for g in range(2):
                rs_ = slice(g * 32, g * 32 + 32)
                if first:
                    p.op("dve", lambda e, rs_=rs_, g=g: e.tensor_scalar_mul(out=Osamp[rs_, :], in0=PS[bo][rs_, g * 64:(g + 1) * 64], scalar1=eps_[rs_, 2:3]),
                         reads=[psr(bo), "eps_"], writes=["Osamp"])
                else:
                    p.op("dve", lambda e, rs_=rs_, g=g: e.scalar_tensor_tensor(out=Osamp[rs_, :], in0=PS[bo][rs_, g * 64:(g + 1) * 64], scalar=eps_[rs_, 2:3],
                                                                               in1=Osamp[rs_, :], op0=ALU.mult, op1=ALU.add),
                         reads=[psr(bo), "eps_", "Osamp"], writes=["Osamp"])

        def new_keys(bo, slot, Vn, vnn, pidx, sq):
            bk = gs6.next()
            p.op("pe", lambda e, bk=bk: e.matmul(PS[bk][0:NS, 0:64], KTs[:, slot, :], Qz[:, :], start=True, stop=True), reads=["KTs", "Qz"], writes=[psr(bk)])
            Pn = PTn[pidx]
            p.op("act", lambda e, bk=bk, Pn=Pn: e.activation(out=Pn[:, :], in_=PS[bk][0:NS, 0:64], func=AF.Exp), reads=[psr(bk)], writes=["PTn%d" % pidx])
            p.op("dve", lambda e, Pn=Pn: e.tensor_tensor(out=Pn[:, :], in0=Pn[:, :], in1=maskN[:, sq, :], op=ALU.mult), reads=["PTn%d" % pidx, "maskN"], writes=["PTn%d" % pidx])
            p.op("pe", lambda e, Pn=Pn: e.matmul(PS[bo][0:64, 0:128], Pn[:, :], Vn[:, :], start=False, stop=True, skip_group_check=True),
                 reads=["PTn%d" % pidx, vnn], writes=[psr(bo)])
            p.op("pe", lambda e, Pn=Pn: e.matmul(PS[bo][0:64, 128:129], Pn[:, :], ones_b[0:NS, 0:1], start=False, stop=True, skip_group_check=True),
                 reads=["PTn%d" % pidx, "ones_b"], writes=[psr(bo)])

        def sec_pti(sq):
            p.dma("sp", lambda e, sq=sq: e.dma_start(out=pti[:, :], in_=pt[sq * 128:(sq + 1) * 128, :]), writes=["pti"])
        def sec_comp(sq):
            for pi_, pool_ in enumerate((pk_c, pv_c)):
                for rc in range(8):
                    G = Gt[gtc[0] % NG]
                    gn = "Gt%d" % (gtc[0] % NG)
                    gtc[0] += 1
                    p.dma("pool", lambda e, G=G, pool_=pool_, rc=rc: e.indirect_dma_start(
                        out=G[:, :], out_offset=None, in_=pool_, in_offset=bass.IndirectOffsetOnAxis(ap=pti[:, :], axis=0), element_offset=rc * 2048),
                        reads=["pti"], writes=[gn])
                    for hf in range(2):
                        bk = gs6.next()
                        pst = PS[bk].bitcast(BF16)
                        for r in range(8):
                            rr = hf * 8 + r
                            p.op("pe", lambda e, pst=pst, r=r, rr=rr, G=G: e.transpose(pst[:, r * 128:(r + 1) * 128], G[:, rr * 128:(rr + 1) * 128], idb[:]),
                                 reads=[gn, "idb"], writes=[psr(bk)])
                        evac2(KTr[:, rc, hf * 8:(hf + 1) * 8, :], pst[:, :].rearrange("p (j q) -> p j q", j=8), bk, ["BGB"])
                bA = gs6.next()
                bB = gs6.next()
                for j in range(32):
                    if j < 16:
                        p.op("pe", lambda e, j=j, pi_=pi_, bA=bA: e.matmul(PS[bA][:, :].rearrange("p (c q) -> p c q", c=4), W1bd[pi_][:, j, :], KTr[:, 0:4, j, :],
                                                                             start=(j == 0), stop=False, skip_group_check=True), reads=["BGB", "W1bd%d" % pi_], writes=[psr(bA)])
                        p.op("pe", lambda e, j=j, pi_=pi_, bB=bB: e.matmul(PS[bB][:, :].rearrange("p (c q) -> p c q", c=4), W1bd[pi_][:, j, :], KTr[:, 4:8, j, :],
                                                                             start=(j == 0), stop=False, skip_group_check=True), reads=["BGB", "W1bd%d" % pi_], writes=[psr(bB)])
                    else:
                        jj = j - 16
                        p.op("pe", lambda e, j=j, jj=jj, pi_=pi_, bA=bA: e.matmul(PS[bA][:, :].rearrange("p (c q) -> p c q", c=4), W1bd[pi_][:, j, :], KTr[:, 1:5, jj, :],
                                                                                    start=False, stop=(j == 31), skip_group_check=True), reads=["BGB", "W1bd%d" % pi_], writes=[psr(bA)])
                        p.op("pe", lambda e, j=j, jj=jj, pi_=pi_, bB=bB: e.matmul(PS[bB][:, 0:384].rearrange("p (c q) -> p c q", c=3), W1bd[pi_][:, j, :], KTr[:, 5:8, jj, :],
                                                                                    start=False, stop=False, skip_group_check=True), reads=["BGB", "W1bd%d" % pi_], writes=[psr(bB)])
                        p.op("pe", lambda e, j=j, jj=jj, pi_=pi_, bB=bB: e.matmul(PS[bB][:, 384:511], W1bd[pi_][:, j, :], KTr[:, 0, jj, 1:128],
                                                                                    start=False, stop=(j == 31), skip_group_check=True), reads=["BGB", "W1bd%d" % pi_], writes=[psr(bB)])
                p.op("act", lambda e, pi_=pi_, bA=bA: e.activation(out=hs_s[:, 0:512], in_=PS[bA][:, :], func=AF.Silu, bias=H0[pi_][:, 0:1]),
                     reads=[psr(bA), "H0_%d" % pi_], writes=["hs_s.a"])
                p.op("act", lambda e, pi_=pi_, bB=bB: e.activation(out=hs_s[:, 512:1024], in_=PS[bB][:, :], func=AF.Silu, bias=H0[pi_][:, 0:1]),
                     reads=[psr(bB), "H0_%d" % pi_], writes=["hs_s.b"])
                if pi_ == 0:
                    for hf in range(2):
                        bk = gs6.next()
                        p.op("pe", lambda e, hf=hf, bk=bk: e.matmul(PS[bk][:, :], W2bd[0][:, :], hs_s[:, hf * 512:(hf + 1) * 512], start=True, stop=True),
                             reads=["W2bd0", "hs_s.a", "hs_s.b"], writes=[psr(bk)])
                        evac2(kcT_s[:, hf * 512:(hf + 1) * 512], PS[bk][:, :], bk, ["kcT_s.%d" % hf])
                else:
                    for hf in range(2):
                        bk = gs6.next()
                        for c4 in range(4):
                            cc = hf * 4 + c4
                            p.op("pe", lambda e, cc=cc, c4=c4, bk=bk: e.matmul(PS[bk][:, c4 * 128:(c4 + 1) * 128], hs_s[:, cc * 128:(cc + 1) * 128], W2bd[1][:, :],
                                                                                 start=(c4 == 0), stop=True, skip_group_check=True),
                                 reads=["W2bd1", "hs_s.a", "hs_s.b"], writes=[psr(bk)])
                        evac2(vc_s[:, hf * 4:(hf + 1) * 4, :], PS[bk][:, :].rearrange("p (c q) -> p c q", c=4), bk, ["vc_s.%d" % hf])

        def sec_cmp(sq):
            p.op("pool", lambda e: e.memset(Qz[:], 0.0), writes=["Qz"])
            p.op("dve", lambda e, sq=sq: e.tensor_copy(out=Qz[0:64, 0:16].rearrange("p (h t) -> p h t", h=4), in_=QTs[0:64, :, 4 * sq:4 * sq + 4]),
                 reads=["QTs", "Qz"], writes=["Qz"])
            p.op("dve", lambda e, sq=sq: e.tensor_copy(out=Qz[64:128, 32:48].rearrange("p (h t) -> p h t", h=4), in_=QTs[64:128, :, 4 * sq:4 * sq + 4]),
                 reads=["QTs", "Qz"], writes=["Qz"])
            p.op("pool", lambda e: e.memset(gcol[:], 0.0), writes=["gcol"])
            for g in range(2):
                for hq in range(4):
                    r0 = g * 32 + hq * 4
                    h = g * 4 + hq
                    p.dma("sp", lambda e, r0=r0, h=h, sq=sq: e.dma_start(out=gcol[r0:r0 + 4, 0:3], in_=gates_s[4 * sq:4 * sq + 4, h * 3:h * 3 + 3]),
                          reads=["gates_s", "gcol"], writes=["gcol"])
            bk = gs6.next()
            for cc in range(8):
                p.op("pe", lambda e, cc=cc, bk=bk: e.matmul(PS[bk][:, cc * 64:(cc + 1) * 64], kcT_s[:, cc * 128:(cc + 1) * 128], Qz[:, :],
                                                             start=(cc == 0), stop=True, skip_group_check=True),
                     reads=["kcT_s.%d" % (cc // 4), "Qz"], writes=[psr(bk)])
            p.op("act", lambda e, bk=bk: e.activation(out=PTc[:].rearrange("p c q -> p (c q)"), in_=PS[bk][:, :], func=AF.Exp), reads=[psr(bk)], writes=["PTc"])
            p.op("dve", lambda e: e.tensor_scalar_mul(out=PTc[:, 7, :], in0=PTc[:, 7, :], scalar1=pmask[:, 0:1]), reads=["PTc", "pmask"], writes=["PTc"])
            bo = sO.next()
            for cc in range(8):
                p.op("pe", lambda e, cc=cc, bo=bo: e.matmul(PS[bo][0:64, 0:128], PTc[:, cc, :], vc_s[:, cc, :], start=(cc == 0), stop=(cc == 7), skip_group_check=True),
                     reads=["PTc", "vc_s.%d" % (cc // 4)], writes=[psr(bo)])
                p.op("pe", lambda e, cc=cc, bo=bo: e.matmul(PS[bo][0:64, 128:129], PTc[:, cc, :], ones_b[:, 0:1], start=False, stop=(cc == 7), skip_group_check=True),
                     reads=["PTc", "ones_b"], writes=[psr(bo)])
                p.op("pe", lambda e, cc=cc, bo=bo: e.matmul(PS[bo][0:64, 129:386], PTc[:, cc, :], aggS[:, cc, :], start=False, stop=(cc == 7), skip_group_check=True),
                     reads=["PTc", "aggS"], writes=[psr(bo)])
            branch_epilogue(bo, 0, True)
            p.op("dve", lambda e, bo=bo: e.tensor_scalar_mul(out=imp_n[:, :], in0=PS[bo][0:64, 129:386], scalar1=eps_[:, 1:2]), reads=[psr(bo), "eps_"], writes=["imp_n"])
            bk = gs6.next()
            p.op("pe", lambda e, bk=bk: e.matmul(PS[bk][0:8, 0:257], Hsum[:, :], imp_n[:, :], start=True, stop=True), reads=["cst", "imp_n"], writes=[psr(bk)])
            p.op("dve", lambda e, bk=bk: e.tensor_copy(out=scs[:, :], in_=PS[bk][0:8, 0:257]), reads=[psr(bk)], writes=["scs"])
            p.op("dve", lambda e: e.memset(scs[:, 0:1], -1.0), reads=["scs"], writes=["scs"])
            p.op("dve", lambda e: e.memset(scs[:, 255:257], -1.0), reads=["scs"], writes=["scs"])
            p.op("dve", lambda e: e.max(out=m16s[:, 0:8], in_=scs[:, :]), reads=["scs"], writes=["m16s.a"])
            p.op("dve", lambda e: e.max_index(out=i16[:, 0:8], in_max=m16s[:, 0:8], in_values=scs[:, :]), reads=["scs", "m16s.a"], writes=["i16.a"])
            p.op("dve", lambda e: e.match_replace(out=scs2[:, :], in_to_replace=m16s[:, 0:8], in_values=scs[:, :], imm_value=-1e9), reads=["scs", "m16s.a", "imp_n"], writes=["imp_n"])
            p.op("dve", lambda e: e.max(out=m16s[:, 8:16], in_=scs2[:, :]), reads=["imp_n"], writes=["m16s.b"])
            p.op("dve", lambda e: e.max_index(out=i16[:, 8:16], in_max=m16s[:, 8:16], in_values=scs2[:, :]), reads=["imp_n", "m16s.b"], writes=["i16.b"])
            i16i = i16.bitcast(I32)
            p.op("dve", lambda e, i16i=i16i: e.memset(i16i[:, 13:14], 0), reads=["i16.a", "i16.b"], writes=["i16.c"])
            p.op("dve", lambda e, i16i=i16i: e.memset(i16i[:, 14:15], 255), reads=["i16.c"], writes=["i16.c"])
            p.op("dve", lambda e, i16i=i16i: e.memset(i16i[:, 15:16], 0), reads=["i16.c"], writes=["i16.c"])
            p.op("dve", lambda e, i16i=i16i: e.tensor_single_scalar(out=idxw[:, :, 0], in_=i16i[:, :], scalar=1, op=ALU.arith_shift_right),
                 reads=["i16.a", "i16.b", "i16.c"], writes=["idxw.a"])
            p.op("dve", lambda e, sq=sq: e.tensor_single_scalar(out=idxw[:, :, 0], in_=idxw[:, :, 0], scalar=sq * 128, op=ALU.add),
                 reads=["idxw.a"], writes=["idxw.a"])
            p.op("dve", lambda e, i16i=i16i: e.tensor_single_scalar(out=idxw[:, :, 1], in_=i16i[:, :], scalar=1, op=ALU.bitwise_and),
                 reads=["i16.a", "i16.b", "i16.c"], writes=["idxw.b"])
            p.dma("sp", lambda e, sq=sq: e.dma_start(out=scr_idx[sq].rearrange("(r k) c -> r (k c)", k=16), in_=idxw[:].rearrange("p k c -> p (k c)")),
                  reads=["idxw.a", "idxw.b"], writes=["scr_idx"])
            p.dma("sp", lambda e, sq=sq: e.dma_start(out=idxp[:, :], in_=scr_idx[sq]), reads=["scr_idx"], writes=["idxp"])
            p.dma("pool", lambda e: e.indirect_dma_start(out=pgid[:, :], out_offset=None, in_=pt, in_offset=bass.IndirectOffsetOnAxis(ap=idxp[:, 0:1], axis=0)),
                  reads=["idxp"], writes=["pgid"])
            p.op("dve", lambda e: e.scalar_tensor_tensor(out=hpx[:, :], in0=pgid[:, :], scalar=2, in1=idxp[:, 1:2], op0=ALU.mult, op1=ALU.add),
                 reads=["pgid", "idxp"], writes=["hpx"])
        def sec_gath(sq):
            for hf in range(2):
                p.dma("pool", lambda e, hf=hf: e.indirect_dma_start(out=wbuf[hf][:].rearrange("p a b -> p (a b)"), out_offset=None, in_=pk_s,
                                                                     in_offset=bass.IndirectOffsetOnAxis(ap=hpx[:, :], axis=0), element_offset=hf * 4096),
                      reads=["hpx"], writes=["wbuf%d" % hf])
            p.dma("pool", lambda e: e.indirect_dma_start(out=VGb[:, :], out_offset=None, in_=pv_s, in_offset=bass.IndirectOffsetOnAxis(ap=hpx[:, :], axis=0)),
                  reads=["hpx"], writes=["VGb"])

        def sec_sel(sq):
            for k8 in range(8):
                bk = gs6.next()
                pst = PS[bk].bitcast(BF16)
                for r in range(8):
                    k = k8 * 8 + r
                    kw_ = wbuf[k // 32][:].rearrange("p a b -> p (a b)")
                    p.op("pe", lambda e, pst=pst, r=r, k=k, kw_=kw_: e.transpose(pst[:, r * 128:(r + 1) * 128], kw_[:, (k % 32) * 128:(k % 32 + 1) * 128], idb[:]),
                         reads=["wbuf%d" % (k // 32), "idb"], writes=[psr(bk)])
                evac2(KsT[:, k8 * 8:(k8 + 1) * 8, :], pst[:, :].rearrange("p (j q) -> p j q", j=8), bk, ["KsT.%d" % k8])
            bo = sO.next()
            for k8 in range(8):
                bk = gs6.next()
                for r in range(8):
                    k = k8 * 8 + r
                    p.op("pe", lambda e, r=r, k=k, bk=bk: e.matmul(PS[bk][:, r * 64:(r + 1) * 64], KsT[:, k, :], Qz[:, :], start=(r == 0), stop=True, skip_group_check=True),
                         reads=["KsT.%d" % k8, "Qz"], writes=[psr(bk)])
                P_ = PTs[k8 % 2]
                pn = "PTs%d" % (k8 % 2)
                p.op("act", lambda e, bk=bk, P_=P_: e.activation(out=P_[:].rearrange("p c q -> p (c q)"), in_=PS[bk][:, :], func=AF.Exp), reads=[psr(bk)], writes=[pn])
                p.op("dve", lambda e, P_=P_: e.tensor_tensor(out=P_[:], in0=P_[:], in1=maskS[:, :].unsqueeze(1).to_broadcast([128, 8, 64]), op=ALU.mult),
                     reads=[pn, "maskS"], writes=[pn])
                for r in range(8):
                    k = k8 * 8 + r
                    first = (k == 0)
                    p.op("pe", lambda e, r=r, k=k, bo=bo, P_=P_, first=first: e.matmul(PS[bo][0:64, 0:128], P_[:, r, :], VGb[:, k * 128:(k + 1) * 128],
                                                                                         start=first, stop=False, skip_group_check=True), reads=[pn, "VGb"], writes=[psr(bo)])
                    p.op("pe", lambda e, r=r, bo=bo, P_=P_: e.matmul(PS[bo][0:64, 128:129], P_[:, r, :], ones_b[:, 0:1], start=False, stop=False, skip_group_check=True),
                         reads=[pn, "ones_b"], writes=[psr(bo)])

            new_keys(bo, 2, VNs, "VNs", 0, sq)
            branch_epilogue(bo, 1, False)

        def sec_win(sq):
            p.dma("pool", lambda e, sq=sq: e.dma_start(out=Kwn[:], in_=ckw[sq].rearrange("(i r) c -> r i c", r=128)), writes=["Kwn"])
            p.dma("pool", lambda e, sq=sq: e.dma_start(out=Vw_s[:], in_=cvw[sq].rearrange("(i r) c -> r i c", r=128)), writes=["Vw_s"])
            bk = gs6.next()
            pst = PS[bk].bitcast(BF16)
            for i in range(4):
                p.op("pe", lambda e, i=i, pst=pst: e.transpose(pst[:, i * 128:(i + 1) * 128], Kwn[:, i, :], idb[:]), reads=["Kwn", "idb"], writes=[psr(bk)])
            evac2(KwT[:], pst[:, 0:512].rearrange("p (j q) -> p j q", j=4), bk, ["KwT"])
            bk = gs6.next()
            for i in range(4):
                p.op("pe", lambda e, i=i, bk=bk: e.matmul(PS[bk][:, i * 64:(i + 1) * 64], KwT[:, i, :], Qz[:, :], start=(i == 0), stop=True, skip_group_check=True),
                     reads=["KwT", "Qz"], writes=[psr(bk)])
            p.op("act", lambda e, bk=bk: e.activation(out=PTw[:].rearrange("p c q -> p (c q)"), in_=PS[bk][:, 0:256], func=AF.Exp), reads=[psr(bk)], writes=["PTw"])
            p.op("dve", lambda e: e.tensor_tensor(out=PTw[:, 0, :], in0=PTw[:, 0, :], in1=maskW[:, :], op=ALU.mult), reads=["PTw", "maskW"], writes=["PTw"])
            bo = sO.next()
            for i in range(4):
                p.op("pe", lambda e, i=i, bo=bo: e.matmul(PS[bo][0:64, 0:128], PTw[:, i, :], Vw_s[:, i, :], start=(i == 0), stop=False, skip_group_check=True),
                     reads=["PTw", "Vw_s"], writes=[psr(bo)])
                p.op("pe", lambda e, i=i, bo=bo: e.matmul(PS[bo][0:64, 128:129], PTw[:, i, :], ones_b[:, 0:1], start=False, stop=False, skip_group_check=True),
                     reads=["PTw", "ones_b"], writes=[psr(bo)])
            new_keys(bo, 3, VNw, "VNw", 1, sq)
            branch_epilogue(bo, 2, False)

        def sec_place(sq):
            p.op("dve", lambda e: e.tensor_copy(out=OW[:, 0:64], in_=Osamp[:, :]), reads=["Osamp"], writes=["OW"])
            p.op("dve", lambda e: e.tensor_copy(out=OW[:, 64:128], in_=Osamp[:, :]), reads=["Osamp", "OW"], writes=["OW"])
            b1 = gs6.next()
            b2 = gs6.next()
            p.op("pe", lambda e, b1=b1: e.matmul(PS[b1][0:64, 0:16], OW[:, 0:64], SelE[:, :], start=True, stop=True), reads=["OW", "cst"], writes=[psr(b1)])
            p.op("pe", lambda e, b2=b2: e.matmul(PS[b2][:, 0:16], OW[:, :], SelO[:, :], start=True, stop=True), reads=["OW", "cst"], writes=[psr(b2)])
            p.op("dve", lambda e, b1=b1, sq=sq: e.tensor_tensor(out=OAT[0:64, :, T + 4 * sq:T + 4 * sq + 4], in0=PS[b1][0:64, 0:16].rearrange("p (f t) -> p f t", f=4),
                                                                 in1=SGAs[0:64, :, 4 * sq:4 * sq + 4], op=ALU.mult), reads=[psr(b1), "SGAs"] + OATS, writes=OATS)
            p.op("dve", lambda e, b2=b2, sq=sq: e.tensor_tensor(out=OAT[64:128, :, T + 4 * sq:T + 4 * sq + 4], in0=PS[b2][64:128, 0:16].rearrange("p (f t) -> p f t", f=4),
                                                                 in1=SGAs[64:128, :, 4 * sq:4 * sq + 4], op=ALU.mult), reads=[psr(b2), "SGAs"] + OATS, writes=OATS)
        sec_pti(0)
        sec_comp(0)
        build_sample_consts()
        sec_cmp(0)
        sec_win(0)
        for sq in range(4):
            if sq + 1 < 4:
                sec_pti(sq + 1)
                sec_comp(sq + 1)
            sec_gath(sq)
            sec_sel(sq)
            sec_place(sq)
            if sq + 1 < 4:
                sec_cmp(sq + 1)
                sec_win(sq + 1)
    if stop_after <= 3:
        dbg = dout("dbg", [128, 4 * T], BF16)
        p.dma("sp", lambda e: e.dma_start(out=dbg.rearrange("p (c t) -> p c t", c=4), in_=OAT[:, :, 0:T]),
              reads=["OAT.%d.%d" % (fc, b) for fc in range(4) for b in range(NB)], is_output=True)
        return nc, p, locals()


    p.arena_reset()
    CBT = p.ar("CBT", [128, 4, TA], BF16)
    keep45 = p.ar_off
    UT = p.ar("UT", [128, 4, 30 + T], BF16)
    DG = p.ar("DG", [128, 4, 31, 128], BF16)
    accA2 = [[p.ar("accA%d_%d" % (j, i), [128, 512], F32) for i in range(4)] for j in range(2)]
    accA = accA2[0]
    xb16 = p.ar("xb16", [128, 4, 512], BF16)
    xsq16 = p.ar("xsq16", [128, 4, 512], BF16)
    mean = p.ar("mean", [128, 512], F32)
    rstd = p.ar("rstd", [128, 512], F32)
    msq = p.ar("msq", [128, 512], F32)
    sgt = [p.ar("sgt%d" % i, [128, 512], F32) for i in range(2)]
    YS = p.ar("YS", [128, 4, 512], BF16)
    PW = p.ar("PW", [128, 4, 512], BF16)
    vecn = p.ar("vecn", [35, 512], F32)
    vecT = p.ar("vecT", [128, 4, 35], F32)
    utok = p.ar("utok", [128, 512], F32)

    wv, wvn = load_w(C_GLU)
    wg, wgn = load_w(C_GLU + 512)
    p.dma("sp", lambda e: e.dma_start(out=vecn[:, :], in_=vecs), writes=["vecn"])
    for ct in range(4):
        bk = gen.next()
        p.op("pe", lambda e, bk=bk, ct=ct: e.transpose(PS[bk][:, 0:35], vecn[:, ct * 128:(ct + 1) * 128], idf[0:35, 0:35]),
             reads=["vecn", "idf"], writes=[psr(bk)])
        p.op("dve", lambda e, bk=bk, ct=ct: e.tensor_copy(out=vecT[:, ct, :], in_=PS[bk][:, 0:35]), reads=[psr(bk)], writes=["vecT"])
    p.dma("pool", lambda e: e.dma_start(out=PW[:], in_=pw_w.rearrange("(kt p) c -> p kt c", p=128)), writes=["PW"])
    for ct in range(4):
        eng = "dve"
        p.op(eng, lambda e, ct=ct: e.tensor_tensor(out=DG[:, ct, :, :], in0=idb[:, :].unsqueeze(1).to_broadcast([128, 31, 128]),
                                                   in1=vecT[:, ct, 0:31].unsqueeze(2).to_broadcast([128, 31, 128]), op=ALU.mult),
             reads=["idb", "vecT"], writes=["DG.%d" % ct])
    p.op("pool", lambda e: e.memset(UT[:, :, 0:30], 0.0), writes=["UT.h"])

    UTs = p.ar("UTs", [128, 4, 4, 34], F32)
    sct = p.ar("sct", [120, 512], F32)
    utok_s = p.ar("utok_s", [NS, 512], F32)
    p.dma("sp", lambda e: e.dma_start(out=sct[:, :], in_=sconv.rearrange("s r c -> (s r) c")), writes=["sct"])
    for ct in range(4):
        bk = gen.next()
        p.op("pe", lambda e, bk=bk, ct=ct: e.transpose(PS[bk][:, 0:120], sct[:, ct * 128:(ct + 1) * 128], idf[0:120, 0:120]),
             reads=["sct", "idf"], writes=[psr(bk)])
        p.op("dve", lambda e, bk=bk, ct=ct: e.tensor_copy(out=UTs[:, ct, :, 0:30], in_=PS[bk][:, 0:120].rearrange("p (s r) -> p s r", s=4)),
             reads=[psr(bk)], writes=["UTs.h%d" % ct])
    for sq in range(4):
        p.dma("sp", lambda e, sq=sq: e.dma_start(out=s_conv[sq, 0:26, :], in_=sconv[sq, 4:30, :]), is_output=True)

    sgc = [0]
    for ct in range(4):
        for blk in range(NBA):
            n = bn(blk)
            bk = gen.next()
            proj_fm(wg, wgn, ct, blk, bk)
            k = sgc[0] % 2
            sgc[0] += 1
            p.op("act", lambda e, bk=bk, k=k, n=n: e.activation(out=sgt[k][:, 0:n], in_=PS[bk][:, 0:n], func=AF.Sigmoid), reads=[psr(bk)], writes=["sgt%d" % k])
            bk2 = gen.next()
            proj_fm(wv, wvn, ct, blk, bk2)
            if blk < NB:
                p.op("dve", lambda e, bk2=bk2, k=k, ct=ct, blk=blk: e.tensor_tensor(out=UT[:, ct, 30 + blk * 512:30 + (blk + 1) * 512], in0=PS[bk2][:, :],
                                                                                     in1=sgt[k][:], op=ALU.mult),
                     reads=[psr(bk2), "sgt%d" % k], writes=["UT.%d.%d" % (ct, blk)])
            else:
                p.op("dve", lambda e, bk2=bk2, k=k, ct=ct: e.tensor_tensor(out=UTs[:, ct, :, 30:34], in0=PS[bk2][:, 0:NS].rearrange("p (s t) -> p s t", s=4),
                                                                            in1=sgt[k][:, 0:NS].rearrange("p (s t) -> p s t", s=4), op=ALU.mult),
                     reads=[psr(bk2), "sgt%d" % k, "UTs.h%d" % ct], writes=["UTs.n%d" % ct])
    for (lo, hi, m, ut_, un, rd) in ((T - 128, T, 128, utok, "utok", "xnT.%d" % (NT - 1)), (T, TA, NS, utok_s, "utok_s", "xnT.s")):
        b1 = gen.next()
        b2 = gen.next()
        for kt in range(8):
            p.op("pe", lambda e, b1=b1, kt=kt, lo=lo, hi=hi, m=m: e.matmul(PS[b1][0:m, :], xnT[:, kt, lo:hi], wv[:, kt, :], start=(kt == 0), stop=(kt == 7)),
                 reads=[wvn, rd], writes=[psr(b1)])
        for kt in range(8):
            p.op("pe", lambda e, b2=b2, kt=kt, lo=lo, hi=hi, m=m: e.matmul(PS[b2][0:m, :], xnT[:, kt, lo:hi], wg[:, kt, :], start=(kt == 0), stop=(kt == 7)),
                 reads=[wgn, rd], writes=[psr(b2)])
        p.op("act", lambda e, b2=b2, m=m, ut_=ut_: e.activation(out=ut_[0:m, :], in_=PS[b2][0:m, :], func=AF.Sigmoid), reads=[psr(b2)], writes=[un])
        p.op("dve", lambda e, b1=b1, m=m, ut_=ut_: e.tensor_tensor(out=ut_[0:m, :], in0=PS[b1][0:m, :], in1=ut_[0:m, :], op=ALU.mult), reads=[psr(b1), un], writes=[un])
    p.dma("sp", lambda e: e.dma_start(out=o_conv, in_=utok[98:128, :]), reads=["utok"], is_output=True)
    for sq in range(4):
        p.dma("sp", lambda e, sq=sq: e.dma_start(out=s_conv[sq, 26:30, :], in_=utok_s[4 * sq:4 * sq + 4, :]), reads=["utok_s"], is_output=True)

    wgb, wgbn = load_w(C_GB)
    NTA = 20

    def ln_pw(blk, acc, an):
        n = bn(blk)
        for ct in range(4):
            p.op("act", lambda e, ct=ct: e.copy(out=xb16[:, ct, 0:n], in_=acc[ct][:, 0:n]), reads=[an % ct], writes=["xb16.%d" % ct])
            p.op("act", lambda e, ct=ct: e.activation(out=xsq16[:, ct, 0:n], in_=acc[ct][:, 0:n], func=AF.Square), reads=[an % ct], writes=["xsq16.%d" % ct])
        b1 = gen.next()
        b2 = gen.next()
        for ct in range(4):
            p.op("pe", lambda e, ct=ct: e.matmul(PS[b1][:, 0:n], ones_b[:, :], xb16[:, ct, 0:n], start=(ct == 0), stop=(ct == 3)),
                 reads=["ones_b", "xb16.%d" % ct], writes=[psr(b1)])
        for ct in range(4):
            p.op("pe", lambda e, ct=ct: e.matmul(PS[b2][:, 0:n], ones_b[:, :], xsq16[:, ct, 0:n], start=(ct == 0), stop=(ct == 3)),
                 reads=["ones_b", "xsq16.%d" % ct], writes=[psr(b2)])
        p.op("dve", lambda e: e.tensor_scalar_mul(out=mean[:, 0:n], in0=PS[b1][:, 0:n], scalar1=1.0 / 512), reads=[psr(b1)], writes=["mean"])
        p.op("dve", lambda e: e.tensor_tensor(out=msq[:, 0:n], in0=mean[:, 0:n], in1=mean[:, 0:n], op=ALU.mult), reads=["mean"], writes=["msq"])
        p.op("dve", lambda e: e.scalar_tensor_tensor(out=rstd[:, 0:n], in0=PS[b2][:, 0:n], scalar=1.0 / 512, in1=msq[:, 0:n], op0=ALU.mult, op1=ALU.subtract),
             reads=[psr(b2), "msq"], writes=["rstd"])
        p.op("dve", lambda e: e.tensor_scalar_add(out=rstd[:, 0:n], in0=rstd[:, 0:n], scalar1=EPS), reads=["rstd"], writes=["rstd"])
        p.op("act", lambda e: e.activation(out=rstd[:, 0:n], in_=rstd[:, 0:n], func=AF.Sqrt), reads=["rstd"], writes=["rstd"])
        p.op("dve", lambda e: e.reciprocal(out=rstd[:, 0:n], in_=rstd[:, 0:n]), reads=["rstd"], writes=["rstd"])
        for ct in range(4):
            p.op("dve", lambda e, ct=ct: e.tensor_tensor(out=acc[ct][:, 0:n], in0=acc[ct][:, 0:n], in1=mean[:, 0:n], op=ALU.subtract),
                 reads=[an % ct, "mean"], writes=[an % ct])
            p.op("pool", lambda e, ct=ct: e.tensor_tensor(out=acc[ct][:, 0:n], in0=acc[ct][:, 0:n], in1=rstd[:, 0:n], op=ALU.mult),
                 reads=[an % ct, "rstd"], writes=[an % ct])
            p.op("act", lambda e, ct=ct: e.activation(out=YS[:, ct, 0:n], in_=acc[ct][:, 0:n], func=AF.Silu, scale=vecT[:, ct, 32:33], bias=vecT[:, ct, 33:34]),
                 reads=[an % ct, "vecT"], writes=["YS.%d" % ct])
        for co in range(4):
            bk = gen.next()
            proj_fm(wgb, wgbn, co, blk, bk)
            k = sgc[0] % 2
            sgc[0] += 1
            p.op("act", lambda e, bk=bk, k=k: e.activation(out=sgt[k][:, 0:n], in_=PS[bk][:, 0:n], func=AF.Silu), reads=[psr(bk)], writes=["sgt%d" % k])
            bk2 = gen.next()
            for ci in range(4):
                p.op("pe", lambda e, bk2=bk2, ci=ci, co=co: e.matmul(PS[bk2][:, 0:n], PW[:, ci, co * 128:(co + 1) * 128], YS[:, ci, 0:n], start=(ci == 0), stop=(ci == 3)),
                     reads=["PW", "YS.%d" % ci], writes=[psr(bk2)])
            p.op("dve", lambda e, bk2=bk2, k=k, co=co: e.scalar_tensor_tensor(out=CBT[:, co, bsl(blk)], in0=PS[bk2][:, 0:n],
                                                                               scalar=vecT[:, co, 34:35], in1=sgt[k][:, 0:n], op0=ALU.add, op1=ALU.mult),
                 reads=[psr(bk2), "sgt%d" % k, "vecT"], writes=["CBT.%d.%d" % (co, blk)])

    def conv_blk(blk, acc, an):
        def u_sl(ct, w):
            return UT[:, ct, blk * 512 + w: blk * 512 + w + 512]

        def u_reads(ct):
            r = ["UT.%d.%d" % (ct, blk)]
            r.append("UT.%d.%d" % (ct, blk - 1) if blk > 0 else "UT.h")
            return r
        for ct in range(4):
            bk = gen.next()
            for w in range(31):
                uw_ = u_sl(ct, w)
                p.op("pe", lambda e, ct=ct, w=w, uw_=uw_, bk=bk: e.matmul(PS[bk][:, :], DG[:, ct, w, :], uw_, start=(w == 0), stop=(w == 30)),
                     reads=u_reads(ct) + ["DG.%d" % ct], writes=[psr(bk)])
            if ct % 2 == 0:
                p.op("dve", lambda e, ct=ct, bk=bk: e.tensor_scalar(out=acc[ct][:], in0=PS[bk][:, :], scalar1=vecT[:, ct, 31:32], scalar2=None, op0=ALU.add),
                     reads=[psr(bk), "vecT"], writes=[an % ct])
            else:
                p.op("act", lambda e, ct=ct, bk=bk: e.activation(out=acc[ct][:], in_=PS[bk][:, :], func=AF.Identity, bias=vecT[:, ct, 31:32]),
                     reads=[psr(bk), "vecT"], writes=[an % ct])

    accS = [p.ar("accS%d" % i, [128, NS], F32) for i in range(4)]
    for w in range(31):
        for ct in range(4):
            uw_ = UTs[:, ct, :, w:w + 4]
            av = accS[ct][:, 0:NS].rearrange("p (s t) -> p s t", s=4)
            rd = ["UTs.h%d" % ct, "UTs.n%d" % ct, "vecT"]
            if w == 0:
                p.op("dve", lambda e, ct=ct, uw_=uw_, av=av: e.tensor_scalar(out=av, in0=uw_, scalar1=vecT[:, ct, 0:1], scalar2=vecT[:, ct, 31:32],
                                                                             op0=ALU.mult, op1=ALU.add), reads=rd, writes=["accS%d" % ct])
            else:
                p.op("dve", lambda e, ct=ct, w=w, uw_=uw_, av=av: e.scalar_tensor_tensor(out=av, in0=uw_, scalar=vecT[:, ct, w:w + 1], in1=av,
                                                                                          op0=ALU.mult, op1=ALU.add), reads=rd + ["accS%d" % ct], writes=["accS%d" % ct])

    ANS = ["accA0_%d", "accA1_%d"]
    conv_blk(0, accA2[0], ANS[0])
    for blk in range(NB):
        if blk + 1 < NB:
            conv_blk(blk + 1, accA2[(blk + 1) % 2], ANS[(blk + 1) % 2])
        ln_pw(blk, accA2[blk % 2], ANS[blk % 2])

    ln_pw(NB, accS, "accS%d")

    if stop_after <= 4:
        dbg4 = dout("dbg4", [128, 4 * 512], BF16)
        p.dma("sp", lambda e: e.dma_start(out=dbg4, in_=YS[:].rearrange("p c t -> p (c t)")), reads=["YS.%d" % c_ for c_ in range(4)], is_output=True)
        dbg5 = dout("dbg5", [128, 3 * 512], F32)
        p.dma("sp", lambda e: e.dma_start(out=dbg5[:, 0:512], in_=mean[:]), reads=["mean"], is_output=True)
        p.dma("sp", lambda e: e.dma_start(out=dbg5[:, 512:1024], in_=rstd[:]), reads=["rstd"], is_output=True)
        p.dma("sp", lambda e: e.dma_start(out=dbg5[:, 1024:1536], in_=accA[0][:]), reads=["accA0_0"], is_output=True)
        dbg = dout("dbg", [128, 4 * T], BF16)
        p.dma("sp", lambda e: e.dma_start(out=dbg.rearrange("p (c t) -> p c t", c=4), in_=CBT[:, :, 0:T]),
              reads=["CBT.%d.%d" % (fc, b) for fc in range(4) for b in range(NB)], is_output=True)
        return nc, p, locals()

    p.arena_reset(keep=keep45)
    WPA = p.ar("WPA", [128, 4, D], BF16)
    WPB = p.ar("WPB", [128, 4, D], BF16)
    HT = p.ar("HT", [128, 8, TA], BF16)
    WO = p.ar("WO", [128, 8, D], BF16)
    fing_b = p.ar("fing_b", [128, D], F32)
    smt = [p.ar("smt%d" % i, [128, 512], F32) for i in range(4)]
    xr = [p.ar("xr%d" % i, [128, D], F32) for i in range(2)]
    yo = [p.ar("yo%d" % i, [128, D], F32) for i in range(2)]
    st2 = p.sbuf("st2", [128, NT + 1, 4], F32)
    p.dma("pool", lambda e: e.dma_start(out=WPA[:], in_=w_pa.rearrange("(kt p) c -> p kt c", p=128)), writes=["WPA"])
    p.dma("pool", lambda e: e.dma_start(out=WPB[:], in_=w_pb.rearrange("(kt p) c -> p kt c", p=128)), writes=["WPB"])
    for half in range(2):
        p.dma("pool", lambda e, half=half: e.dma_start(out=WO[:, half * 4:(half + 1) * 4, :],
                                                        in_=w_o.rearrange("(kt p) c -> p kt c", p=128)[:, half * 4:(half + 1) * 4, :]), writes=["WO"])
    p.dma("sp", lambda e: e.dma_start(out=fing_b[:], in_=final_g.broadcast_to([128, D])), writes=["fing_b"])
    for half in range(2):
        wma, wman = load_w(C_MA + half * 512)
        wmb, wmbn = load_w(C_MB + half * 512)
        for blk in range(NBA):
            n = bn(blk)
            for ii in range(4):
                i = half * 4 + ii
                bka = gen.next()
                proj_fm(wma, wman, ii, blk, bka)
                p.op("act", lambda e, bka=bka, n=n: e.activation(out=smt[0][:, 0:n], in_=PS[bka][:, 0:n], func=AF.Sigmoid), reads=[psr(bka)], writes=["smt0"])
                bkb = gen.next()
                proj_fm(wmb, wmbn, ii, blk, bkb)
                p.op("act", lambda e, bkb=bkb, n=n: e.activation(out=smt[1][:, 0:n], in_=PS[bkb][:, 0:n], func=AF.Sigmoid), reads=[psr(bkb)], writes=["smt1"])
                bk1 = gen.next()
                for f_ in range(4):
                    p.op("pe", lambda e, f_=f_, i=i, bk1=bk1, blk=blk, n=n: e.matmul(PS[bk1][:, 0:n], WPA[:, f_, i * 128:(i + 1) * 128], OAT[:, f_, bsl(blk)],
                                                                                      start=(f_ == 0), stop=(f_ == 3)),
                         reads=["WPA", "OAT.%d.%d" % (f_, blk)], writes=[psr(bk1)])
                p.op("dve", lambda e, bk1=bk1, n=n: e.tensor_tensor(out=smt[2][:, 0:n], in0=PS[bk1][:, 0:n], in1=smt[0][:, 0:n], op=ALU.mult),
                     reads=[psr(bk1), "smt0"], writes=["smt2"])
                bk2 = gen.next()
                for f_ in range(4):
                    p.op("pe", lambda e, f_=f_, i=i, bk2=bk2, blk=blk, n=n: e.matmul(PS[bk2][:, 0:n], WPB[:, f_, i * 128:(i + 1) * 128], CBT[:, f_, bsl(blk)],
                                                                                      start=(f_ == 0), stop=(f_ == 3)),
                         reads=["WPB", "CBT.%d.%d" % (f_, blk)], writes=[psr(bk2)])
                p.op("dve", lambda e, bk2=bk2, n=n: e.tensor_tensor(out=smt[3][:, 0:n], in0=PS[bk2][:, 0:n], in1=smt[1][:, 0:n], op=ALU.mult),
                     reads=[psr(bk2), "smt1"], writes=["smt3"])
                p.op("pool", lambda e, i=i, blk=blk, n=n: e.tensor_tensor(out=HT[:, i, bsl(blk)], in0=smt[2][:, 0:n], in1=smt[3][:, 0:n], op=ALU.add),
                     reads=["smt2", "smt3"], writes=["HT.%d.%d" % (i, blk)])

    for t in range(NT + 1):
        xr_, yo_ = xr[t % 2], yo[t % 2]
        xrn, yon = "xr%d" % (t % 2), "yo%d" % (t % 2)
        if t < NT:
            m, lo, hi, src, dst, hblk = 128, t * 128, (t + 1) * 128, x_p[t * 128:(t + 1) * 128, :], y_p[t * 128:(t + 1) * 128, :], t // 4
        else:
            m, lo, hi, src, dst, hblk = NS, T, TA, x_s, y_s, NB
        p.dma("sp", lambda e, xr_=xr_, m=m, src=src: e.dma_start(out=xr_[0:m, :], in_=src), writes=[xrn])
        for half in range(2):
            bk = gen.next()
            for kt in range(8):
                p.op("pe", lambda e, kt=kt, half=half, bk=bk, m=m, lo=lo, hi=hi: e.matmul(PS[bk][0:m, :], HT[:, kt, lo:hi], WO[:, kt, half * 512:(half + 1) * 512],
                                                                                           start=(kt == 0), stop=(kt == 7)),
                     reads=["WO", "HT.%d.%d" % (kt, hblk)], writes=[psr(bk)])
            p.op("dve", lambda e, half=half, bk=bk, xr_=xr_, m=m: e.tensor_tensor(out=xr_[0:m, half * 512:(half + 1) * 512], in0=PS[bk][0:m, :],
                                                                                  in1=xr_[0:m, half * 512:(half + 1) * 512], op=ALU.add),
                 reads=[psr(bk), xrn], writes=[xrn])
        tag = "st2_%d" % t
        rms_stats(xr_[0:m, :], m, st2[0:m, t, 0:1], st2[0:m, t, 1:2], st2[0:m, t, 2:3], [xrn], tag)
        p.op("dve", lambda e, t=t, xr_=xr_, yo_=yo_, m=m: e.scalar_tensor_tensor(out=yo_[0:m, :], in0=xr_[0:m, :], scalar=st2[0:m, t, 2:3], in1=fing_b[0:m, :],
                                                                                 op0=ALU.mult, op1=ALU.mult),
             reads=[xrn, tag + "c", "fing_b"], writes=[yon])
        p.dma("sp", lambda e, yo_=yo_, m=m, dst=dst: e.dma_start(out=dst, in_=yo_[0:m, :]), reads=[yon], is_output=True)

    return nc, p, locals()


_CACHE = {}


def kernel(x_prompt, x_sample, cache_k_cmp, cache_v_cmp, cache_k_slc, cache_v_slc,
           cache_k_win, cache_v_win, state_conv, page_table,
           ln_g, w_in, pe_k, w1_k, w2_k, pe_v, w1_v, w2_v,
           dw_k, dw_b, cln_g, cln_b, pw_w, pw_b, w_pa, w_pb, w_o, final_g, _stop=99, _sample=True):
    f = lambda a: np.ascontiguousarray(np.asarray(a, dtype=np.float32))
    import os
    if os.environ.get("KDEV_NOSAMPLE"):
        _sample = False
    nc, p, env = build_program(stop_after=_stop, do_sample=_sample)
    if "fin" not in env:
        p.finish()
    vecs = np.concatenate([f(dw_k)[0], f(dw_b), f(cln_g), f(cln_b), f(pw_b)], axis=0)
    shared = {
        "w_in": f(w_in)[0], "ln_g": f(ln_g), "final_g": f(final_g).reshape(1, D),
        "w1_k": f(w1_k)[0], "w1_v": f(w1_v)[0], "w2_k": f(w2_k)[0], "w2_v": f(w2_v)[0],
        "pe_k": f(pe_k)[0], "pe_v": f(pe_v)[0], "vecs": vecs, "pw_w": f(pw_w)[0],
        "w_pa": f(w_pa)[0], "w_pb": f(w_pb)[0], "w_o": f(w_o)[0],
    }
    in_maps = []
    xp = f(x_prompt)
    pools = [f(a)[0].reshape(5120, 16384) for a in (cache_k_cmp, cache_v_cmp, cache_k_slc, cache_v_slc)] if _sample else [None] * 4
    for c in range(8):
        m = dict(shared)
        m["x_p"] = xp[c]
        m["x_s"] = f(x_sample)[4 * c:4 * c + 4].reshape(NS, D)
        m["ckw"] = f(cache_k_win)[0, 4 * c:4 * c + 4].reshape(4, 512, 128)
        m["cvw"] = f(cache_v_win)[0, 4 * c:4 * c + 4].reshape(4, 512, 128)
        m["sconv"] = f(state_conv)[0, 4 * c:4 * c + 4]
        m["pt"] = np.ascontiguousarray(np.asarray(page_table, dtype=np.int32)[4 * c:4 * c + 4].reshape(512, 1))
        m["pk_c"] = pools[0]
        m["pv_c"] = pools[1]
        m["pk_s"] = pools[2]
        m["pv_s"] = pools[3]
        in_maps.append(m)
    names = set(env["in_names"])
    in_maps = [{k: v for k, v in m.items() if k in names} for m in in_maps]
    res = run_bass_kernel_spmd(nc, in_maps, core_ids=list(range(8)))
    R = res.results
    global _LAST
    _LAST = R

    def get(name, shape):
        if name in R[0]:
            return np.stack([np.asarray(R[c][name], dtype=np.float32).reshape(shape) for c in range(8)], 0)
        return np.zeros((8,) + tuple(shape), np.float32)

    y_prompt = get("y_p", (T, D))
    pk = [get("o_kv%d" % i, (T, 2, 64))[None] for i in range(4)]
    p_kw = get("o_kw", (512, 2, 64))[None]
    p_vw = get("o_vw", (512, 2, 64))[None]
    p_conv = get("o_conv", (30, 512))[None]
    def gets(name, shape):
        if name in R[0]:
            a = np.stack([np.asarray(R[c][name], dtype=np.float32).reshape((4,) + tuple(shape)) for c in range(8)], 0)
            return a.reshape((32,) + tuple(shape))
        return np.zeros((32,) + tuple(shape), np.float32)
    y_sample = gets("y_s", (4, D))
    sk = [gets("s_kv%d" % i, (4, 2, 64))[None] for i in range(4)]
    s_kw = gets("s_kw", (512, 2, 64))[None]
    s_vw = gets("s_vw", (512, 2, 64))[None]
    s_conv = gets("s_conv", (30, 512))[None]
    return (y_prompt, y_sample, pk[0], pk[1], pk[2], pk[3], p_kw, p_vw, p_conv,
            sk[0], sk[1], sk[2], sk[3], s_kw, s_vw, s_conv)
```
